# Optimizing a Trainium2 kernel written in Bass

```python
import math
import jax, jax.numpy as jnp
from jax import lax
import numpy as np

D_MODEL = 2048
BATCH = 1
SEQ = 16384
DEPTH = 4

N_MIXERS = 4
EPS = 1e-6
NEG_INF = -1e30
MEM_LEN = 256

A_HEADS = 32
A_KV_HEADS = 4
A_HEAD_DIM = 64
A_GROUP = A_HEADS // A_KV_HEADS
A_QKV = (A_HEADS + 2 * A_KV_HEADS) * A_HEAD_DIM
WINDOW = 128
A_BLOCK = 128
REL_BUCKETS = 32
REL_MAX_EXACT = 16
REL_MAX_DIST = 128

B_HEADS = 4
B_KEY_DIM = D_MODEL // 2 // B_HEADS
B_VAL_DIM = D_MODEL // B_HEADS
B_QKVR = 2 * B_HEADS * B_KEY_DIM + 2 * B_HEADS * B_VAL_DIM
B_GATE_RANK = 16
B_GATE_TAU = 16.0
B_CHUNK = 64

C_KERNEL = 31

D_CHUNK = 128
D_GROUPS = 8
D_HALF = 2 * D_MODEL

X_HEADS = 4
X_HEAD_DIM = 128

FFN_DIM = 4 * D_MODEL
FFN_KERNEL = 3

N_LAYERS_A = (DEPTH + 3) // N_MIXERS
N_LAYERS_B = (DEPTH + 2) // N_MIXERS
N_LAYERS_C = (DEPTH + 1) // N_MIXERS
N_LAYERS_D = DEPTH // N_MIXERS

kernel_name = 'hybrid_interleaved_swa_gla_conformer_gmlp_trunk'


def rmsnorm(x, g):
    xf = x.astype(jnp.float32)
    y = xf * lax.rsqrt(jnp.mean(xf * xf, axis=-1, keepdims=True) + EPS)
    return (y * g.astype(jnp.float32)).astype(x.dtype)


def layernorm(x, g, b):
    xf = x.astype(jnp.float32)
    mu = jnp.mean(xf, axis=-1, keepdims=True)
    xc = xf - mu
    var = jnp.mean(xc * xc, axis=-1, keepdims=True)
    y = xc * lax.rsqrt(var + EPS) * g.astype(jnp.float32) + b.astype(jnp.float32)
    return y.astype(x.dtype)


def causal_depthwise_conv(x, w, b):
    k = w.shape[0]
    y = lax.conv_general_dilated(
        x, w[:, None, :].astype(x.dtype), window_strides=(1,), padding=[(k - 1, 0)],
        dimension_numbers=('NWC', 'WIO', 'NWC'), feature_group_count=x.shape[-1])
    return y + b


def t5_bucket(dist):
    n = np.maximum(dist, 0)
    large = REL_MAX_EXACT + (np.log(np.maximum(n, 1) / REL_MAX_EXACT)
                             / math.log(REL_MAX_DIST / REL_MAX_EXACT)
                             * (REL_BUCKETS - REL_MAX_EXACT)).astype(np.int32)
    large = np.minimum(large, REL_BUCKETS - 1)
    return np.where(n < REL_MAX_EXACT, n, large).astype(np.int32)


def sliding_window_attention(h, w_qkv, sinks, w_o, rel_table):
    b, s, _ = h.shape
    nb = s // A_BLOCK
    q, k, v = jnp.split(h @ w_qkv, [A_HEADS * A_HEAD_DIM, (A_HEADS + A_KV_HEADS) * A_HEAD_DIM], axis=-1)
    q = q.reshape(b, nb, A_BLOCK, A_KV_HEADS, A_GROUP, A_HEAD_DIM) * (A_HEAD_DIM ** -0.5)
    pad = ((0, 0), (A_BLOCK, 0), (0, 0))
    kp = jnp.pad(k, pad).reshape(b, nb + 1, A_BLOCK, A_KV_HEADS, A_HEAD_DIM)
    vp = jnp.pad(v, pad).reshape(b, nb + 1, A_BLOCK, A_KV_HEADS, A_HEAD_DIM)
    kb = jnp.concatenate([kp[:, :-1], kp[:, 1:]], axis=2)
    vb = jnp.concatenate([vp[:, :-1], vp[:, 1:]], axis=2)
    logits = jnp.einsum('bnikgd,bnjkd->bnkgij', q, kb).astype(jnp.float32)
    qi = np.arange(A_BLOCK)[:, None]
    kj = np.arange(2 * A_BLOCK)[None, :]
    dist = qi + A_BLOCK - kj
    in_window = (dist >= 0) & (dist < WINDOW)
    key_valid = (np.arange(nb)[:, None] > 0) | (kj >= A_BLOCK)
    mask = in_window[None, :, :] & key_valid[:, None, :]
    bias = rel_table.astype(jnp.float32)[t5_bucket(dist)]
    bias = jnp.transpose(bias, (2, 0, 1)).reshape(A_KV_HEADS, A_GROUP, A_BLOCK, 2 * A_BLOCK)
    logits = jnp.where(mask[None, :, None, None], logits + bias, NEG_INF)
    sink = sinks.astype(jnp.float32).reshape(A_KV_HEADS, A_GROUP)[None, None, :, :, None, None]
    m = jnp.maximum(jnp.max(logits, axis=-1, keepdims=True), sink)
    p = jnp.exp(logits - m)
    denom = jnp.sum(p, axis=-1, keepdims=True) + jnp.exp(sink - m)
    probs = (p / denom).astype(h.dtype)
    o = jnp.einsum('bnkgij,bnjkd->bnikgd', probs, vb)
    return o.reshape(b, s, A_HEADS * A_HEAD_DIM) @ w_o


def gated_linear_attention(h, w_qkvr, w_g1, w_g2, g_bias, o_norm, w_o):
    b, s, _ = h.shape
    nc = s // B_CHUNK
    dk_all = B_HEADS * B_KEY_DIM
    dv_all = B_HEADS * B_VAL_DIM
    q, k, v, r = jnp.split(h @ w_qkvr, [dk_all, 2 * dk_all, 2 * dk_all + dv_all], axis=-1)
    gk = (h @ w_g1) @ w_g2 + g_bias
    log_a = jax.nn.log_sigmoid(gk.astype(jnp.float32)) / B_GATE_TAU

    def chunks(t, d):
        return t.astype(jnp.float32).reshape(b, nc, B_CHUNK, B_HEADS, d)

    q = chunks(q, B_KEY_DIM) * (B_KEY_DIM ** -0.5)
    k = chunks(k, B_KEY_DIM)
    v = chunks(v, B_VAL_DIM)
    cum = jnp.cumsum(chunks(log_a, B_KEY_DIM), axis=2)
    last = cum[:, :, -1]
    q_dec = q * jnp.exp(cum)
    k_inv = k * jnp.exp(-cum)
    k_end = k * jnp.exp(last[:, :, None] - cum)
    causal = np.tril(np.ones((B_CHUNK, B_CHUNK), dtype=bool))
    att = jnp.where(causal, jnp.einsum('bnihd,bnjhd->bnhij', q_dec, k_inv), 0.0)
    o_intra = jnp.einsum('bnhij,bnjhe->bnihe', att, v)

    def step(state, xs):
        q_c, k_c, v_c, last_c = xs
        o_c = jnp.einsum('bihd,bhde->bihe', q_c, state)
        state = jnp.exp(last_c)[..., None] * state + jnp.einsum('bjhd,bjhe->bhde', k_c, v_c)
        return state, o_c

    xs = (jnp.moveaxis(q_dec, 1, 0), jnp.moveaxis(k_end, 1, 0),
          jnp.moveaxis(v, 1, 0), jnp.moveaxis(last, 1, 0))
    state0 = jnp.zeros((b, B_HEADS, B_KEY_DIM, B_VAL_DIM), jnp.float32)
    _, o_inter = lax.scan(step, state0, xs)
    o = o_intra + jnp.moveaxis(o_inter, 0, 1)
    o = rmsnorm(o, o_norm).reshape(b, s, dv_all).astype(h.dtype)
    return (o * jax.nn.silu(r)) @ w_o


def conformer_conv_module(h, w_pw1, b_pw1, w_dw, b_dw, ln_g, ln_b, w_pw2, b_pw2):
    a, g = jnp.split(h @ w_pw1 + b_pw1, 2, axis=-1)
    z = causal_depthwise_conv(a * jax.nn.sigmoid(g), w_dw, b_dw)
    z = jax.nn.silu(layernorm(z, ln_g, ln_b))
    return z @ w_pw2 + b_pw2


def chunked_spatial_gating(h, w_in, b_in, ln_g, ln_b, w_s, b_s, w_out):
    b, s, _ = h.shape
    nch = s // D_CHUNK
    u, v = jnp.split(jax.nn.gelu(h @ w_in + b_in, approximate=False), 2, axis=-1)
    v = layernorm(v, ln_g, ln_b).reshape(b, nch, D_CHUNK, D_GROUPS, D_HALF // D_GROUPS)
    w = jnp.where(np.tril(np.ones((D_CHUNK, D_CHUNK), dtype=bool)), w_s, 0.0)
    sv = jnp.einsum('gts,bcsgd->bctgd', w, v) + b_s.T[:, :, None]
    return (u * sv.reshape(b, s, D_HALF)) @ w_out


def memory_cross_attention(h, mem_n, w_q, w_kv, w_o):
    b, s, _ = h.shape
    m = mem_n.shape[1]
    q = (h @ w_q).reshape(b, s, X_HEADS, X_HEAD_DIM) * (X_HEAD_DIM ** -0.5)
    k, v = jnp.split(mem_n @ w_kv, 2, axis=-1)
    k = k.reshape(b, m, X_HEADS, X_HEAD_DIM)
    v = v.reshape(b, m, X_HEADS, X_HEAD_DIM)
    logits = jnp.einsum('bshd,bmhd->bhsm', q, k).astype(jnp.float32)
    p = jax.nn.softmax(logits, axis=-1).astype(h.dtype)
    o = jnp.einsum('bhsm,bmhd->bshd', p, v).reshape(b, s, X_HEADS * X_HEAD_DIM)
    return o @ w_o


def conv_gated_ffn(h, w_gate_up, w_conv, b_conv, w_down):
    gate, up = jnp.split(h @ w_gate_up, 2, axis=-1)
    gate = causal_depthwise_conv(gate, w_conv, b_conv)
    return (jax.nn.gelu(gate, approximate=True) * up) @ w_down


def setup_inputs(seed: int = 0) -> dict:
    key = jax.random.key(seed)
    keys = iter(jax.random.split(key, 64))
    f32 = jnp.float32

    def wt(shape, fan_in):
        return jax.random.normal(next(keys), shape, f32) * (fan_in ** -0.5)

    def gain(shape):
        return 1.0 + 0.05 * jax.random.normal(next(keys), shape, f32)

    def small(shape, scale=0.02):
        return scale * jax.random.normal(next(keys), shape, f32)

    D = D_MODEL
    return {
        'x': jax.random.normal(next(keys), (BATCH, SEQ, D), f32),
        'mem': jax.random.normal(next(keys), (BATCH, MEM_LEN, D), f32),
        'norm_mix_pre': gain((DEPTH, D)),
        'norm_mix_post': gain((DEPTH, D)),
        'norm_mem': gain((DEPTH, D)),
        'norm_xattn_pre': gain((DEPTH, D)),
        'norm_xattn_post': gain((DEPTH, D)),
        'norm_ffn_pre': gain((DEPTH, D)),
        'norm_ffn_post': gain((DEPTH, D)),
        'rel_bias_table': small((REL_BUCKETS, A_HEADS), 0.5),
        'a_w_qkv': wt((N_LAYERS_A, D, A_QKV), D),
        'a_sinks': small((N_LAYERS_A, A_HEADS), 0.5),
        'a_w_o': wt((N_LAYERS_A, A_HEADS * A_HEAD_DIM, D), A_HEADS * A_HEAD_DIM),
        'b_w_qkvr': wt((N_LAYERS_B, D, B_QKVR), D),
        'b_w_gate1': wt((N_LAYERS_B, D, B_GATE_RANK), D),
        'b_w_gate2': wt((N_LAYERS_B, B_GATE_RANK, B_HEADS * B_KEY_DIM), B_GATE_RANK),
        'b_gate_bias': small((N_LAYERS_B, B_HEADS * B_KEY_DIM), 0.1),
        'b_o_norm': gain((N_LAYERS_B, B_VAL_DIM)),
        'b_w_o': wt((N_LAYERS_B, B_HEADS * B_VAL_DIM, D), B_HEADS * B_VAL_DIM),
        'c_w_pw1': wt((N_LAYERS_C, D, 2 * D), D),
        'c_b_pw1': small((N_LAYERS_C, 2 * D)),
        'c_w_dw': wt((N_LAYERS_C, C_KERNEL, D), C_KERNEL),
        'c_b_dw': small((N_LAYERS_C, D)),
        'c_ln_g': gain((N_LAYERS_C, D)),
        'c_ln_b': small((N_LAYERS_C, D)),
        'c_w_pw2': wt((N_LAYERS_C, D, D), D),
        'c_b_pw2': small((N_LAYERS_C, D)),
        'd_w_in': wt((N_LAYERS_D, D, 2 * D_HALF), D),
        'd_b_in': small((N_LAYERS_D, 2 * D_HALF)),
        'd_ln_g': gain((N_LAYERS_D, D_HALF)),
        'd_ln_b': small((N_LAYERS_D, D_HALF)),
        'd_w_s': wt((N_LAYERS_D, D_GROUPS, D_CHUNK, D_CHUNK), D_CHUNK),
        'd_b_s': gain((N_LAYERS_D, D_GROUPS, D_CHUNK)),
        'd_w_out': wt((N_LAYERS_D, D_HALF, D), D_HALF),
        'x_w_q': wt((DEPTH, D, X_HEADS * X_HEAD_DIM), D),
        'x_w_kv': wt((DEPTH, D, 2 * X_HEADS * X_HEAD_DIM), D),
        'x_w_o': wt((DEPTH, X_HEADS * X_HEAD_DIM, D), X_HEADS * X_HEAD_DIM),
        'f_w_gate_up': wt((DEPTH, D, 2 * FFN_DIM), D),
        'f_w_conv': wt((DEPTH, FFN_KERNEL, FFN_DIM), FFN_KERNEL),
        'f_b_conv': small((DEPTH, FFN_DIM)),
        'f_w_down': wt((DEPTH, FFN_DIM, D), FFN_DIM),
    }


def reference(x, mem, norm_mix_pre, norm_mix_post, norm_mem, norm_xattn_pre, norm_xattn_post,
              norm_ffn_pre, norm_ffn_post, rel_bias_table,
              a_w_qkv, a_sinks, a_w_o,
              b_w_qkvr, b_w_gate1, b_w_gate2, b_gate_bias, b_o_norm, b_w_o,
              c_w_pw1, c_b_pw1, c_w_dw, c_b_dw, c_ln_g, c_ln_b, c_w_pw2, c_b_pw2,
              d_w_in, d_b_in, d_ln_g, d_ln_b, d_w_s, d_b_s, d_w_out,
              x_w_q, x_w_kv, x_w_o,
              f_w_gate_up, f_w_conv, f_b_conv, f_w_down):
    for i in range(DEPTH):
        kind, j = i % N_MIXERS, i // N_MIXERS
        h = rmsnorm(x, norm_mix_pre[i])
        if kind == 0:
            y = sliding_window_attention(h, a_w_qkv[j], a_sinks[j], a_w_o[j], rel_bias_table)
        elif kind == 1:
            y = gated_linear_attention(h, b_w_qkvr[j], b_w_gate1[j], b_w_gate2[j],
                                       b_gate_bias[j], b_o_norm[j], b_w_o[j])
        elif kind == 2:
            y = conformer_conv_module(h, c_w_pw1[j], c_b_pw1[j], c_w_dw[j], c_b_dw[j],
                                      c_ln_g[j], c_ln_b[j], c_w_pw2[j], c_b_pw2[j])
        else:
            y = chunked_spatial_gating(h, d_w_in[j], d_b_in[j], d_ln_g[j], d_ln_b[j],
                                       d_w_s[j], d_b_s[j], d_w_out[j])
        x = x + rmsnorm(y, norm_mix_post[i])
        mem_n = rmsnorm(mem, norm_mem[i])
        y = memory_cross_attention(rmsnorm(x, norm_xattn_pre[i]), mem_n, x_w_q[i], x_w_kv[i], x_w_o[i])
        x = x + rmsnorm(y, norm_xattn_post[i])
        y = conv_gated_ffn(rmsnorm(x, norm_ffn_pre[i]), f_w_gate_up[i], f_w_conv[i], f_b_conv[i], f_w_down[i])
        x = x + rmsnorm(y, norm_ffn_post[i])
    return x
```

```python
from contextlib import ExitStack
import numpy as np
import concourse.bass as bass
import concourse.mybir as mybir

F32 = mybir.dt.float32
BF16 = mybir.dt.bfloat16
AF = mybir.ActivationFunctionType
ALU = mybir.AluOpType
AX = mybir.AxisListType
NDS = 8
DBG = {"skip_xattn": False, "skip_mixer": False, "dump": False}


class Buf:
    __slots__ = ("w", "r")

    def __init__(self):
        self.w = None
        self.r = {}


class TT:
    def __init__(self, t):
        self.t = t
        self.bufs = {}

    def b(self, key=None):
        if key not in self.bufs:
            self.bufs[key] = Buf()
        return self.bufs[key]

    def all(self):
        return list(self.bufs.values())

    def __getitem__(self, idx):
        return self.t[idx]


class Eng:
    def __init__(self, name, h, sem):
        self.name, self.h, self.sem = name, h, sem
        self.count = 0
        self.seen = {}
        self.dsems = []
        self.duses = [0] * NDS
        self.dn = 0


class Ctx:
    def __init__(self, nc):
        self.nc = nc
        self.es = ExitStack()
        self.sems = {}
        self.eng = {}
        for name, h in (("pe", nc.tensor), ("act", nc.scalar), ("dve", nc.vector),
                        ("pool", nc.gpsimd), ("sp", nc.sync)):
            s = self.es.enter_context(nc.semaphore("s_" + name))
            self.eng[name] = Eng(name, h, s)
            self.sems[name] = s
        for q in ("sp", "pool", "act"):
            E = self.eng[q]
            for i in range(NDS):
                s = self.es.enter_context(nc.semaphore(f"d_{q}{i}"))
                E.dsems.append(s)
                self.sems[("d", q, i)] = s
        self.ninstr = 0

    def sbuf(self, es, name, shape, dt):
        self.uid = getattr(self, "uid", 0) + 1
        return TT(es.enter_context(self.nc.sbuf_tensor(f"{name}_u{self.uid}", list(shape), dt)))

    def psum(self, es, name, shape, dt):
        return TT(es.enter_context(self.nc.psum_tensor(name, list(shape), dt)))

    def dram(self, name, shape, dt, kind="Internal"):
        return TT(self.nc.dram_tensor(name, list(shape), dt, kind=kind).ap())

    def _wait(self, E, key, val):
        if val <= 0:
            return
        if E.seen.get(key, 0) >= val:
            return
        E.h.wait_ge(self.sems[key], val)
        E.seen[key] = val
        self.ninstr += 1

    def _sync(self, E, reads, writes):
        deps = {}
        for b in reads:
            if b.w is not None:
                k, v = b.w
                if deps.get(k, 0) < v:
                    deps[k] = v
        for b in writes:
            if b.w is not None:
                k, v = b.w
                if deps.get(k, 0) < v:
                    deps[k] = v
            for k, v in b.r.items():
                if deps.get(k, 0) < v:
                    deps[k] = v
        for k, v in deps.items():
            if k == "pe" and E.name == "pe":
                continue
            self._wait(E, k, v)

    def _mark(self, tok, reads, writes):
        k, v = tok
        for b in reads:
            if b.r.get(k, 0) < v:
                b.r[k] = v
        for b in writes:
            b.w = tok
            b.r = {}

    def op(self, eng, fn, reads=(), writes=()):
        E = self.eng[eng]
        self._sync(E, reads, writes)
        ins = fn(E.h)
        E.count += 1
        ins.then_inc(E.sem, 1)
        self.ninstr += 1
        self._mark((eng, E.count), reads, writes)

    def mm(self, out_ap, pairs, out_buf, read_bufs, start=True, stop=True):
        E = self.eng["pe"]
        self._sync(E, read_bufs, [out_buf])
        n = len(pairs)
        ins = None
        for i, (l, r) in enumerate(pairs):
            ins = self.nc.tensor.matmul(out_ap, l, r, start=(start and i == 0), stop=(stop and i == n - 1))
        E.count += 1
        ins.then_inc(E.sem, 1)
        self.ninstr += n
        self._mark(("pe", E.count), read_bufs, [out_buf])

    def transpose(self, out_ap, in_ap, ident_ap, out_buf, read_bufs):
        E = self.eng["pe"]
        self._sync(E, read_bufs, [out_buf])
        ins = self.nc.tensor.transpose(out_ap, in_ap, ident_ap)
        E.count += 1
        ins.then_inc(E.sem, 1)
        self.ninstr += 1
        self._mark(("pe", E.count), read_bufs, [out_buf])

    def dma(self, q, out, in_, reads=(), writes=()):
        E = self.eng[q]
        idx = E.dn % NDS
        E.dn += 1
        key = ("d", q, idx)
        self._wait(E, key, E.duses[idx] * 16)
        self._sync(E, reads, writes)
        E.h.dma_start(out=out, in_=in_).then_inc(E.dsems[idx], 16)
        E.duses[idx] += 1
        self.ninstr += 1
        self._mark((key, E.duses[idx] * 16), reads, writes)

    def barrier(self):
        for E in self.eng.values():
            for A in self.eng.values():
                if A is not E:
                    self._wait(E, A.name, A.count)
            for q in ("sp", "pool", "act"):
                Q = self.eng[q]
                for i in range(NDS):
                    self._wait(E, ("d", q, i), Q.duses[i] * 16)

    def finish(self):
        self.barrier()


def colfmt(v):
    v = np.asarray(v, np.float32)
    return np.ascontiguousarray(v.reshape(-1, 128).T)

T = 2048
TB = 512
NTB = 4
D = 2048
KC = 16
EPS = 1e-6


class Prog:
    def __init__(self):
        self.nc = bass.Bass("TRN2", target_bir_lowering=False)
        self.c = Ctx(self.nc)
        c = self.c
        self.in_names = []
        self.out_names = []
        self.ps = [c.psum(c.es, f"ps{i}", [128, 512], F32) for i in range(6)]
        self.psb = [c.psum(c.es, f"psb{i}", [128, 1024], BF16) for i in range(2)]
        self.ones = c.sbuf(c.es, "ones", [128, 128], BF16)
        c.op("dve", lambda e: e.memset(self.ones[:], 1.0), writes=[self.ones.b()])
        self.identf = c.sbuf(c.es, "identf", [128, 128], F32)
        self.ident = c.sbuf(c.es, "ident", [128, 128], BF16)
        idd = self.inp("ident_in", [128, 128])
        c.dma("sp", self.identf[:], idd[:], reads=[idd.b()], writes=[self.identf.b()])
        c.op("dve", lambda e: e.tensor_copy(self.ident[:], self.identf[:]), reads=[self.identf.b()], writes=[self.ident.b()])

    def inp(self, name, shape, dt=F32):
        self.in_names.append(name)
        return self.c.dram(name, shape, dt, kind="ExternalInput")

    def out(self, name, shape, dt=F32):
        self.out_names.append(name)
        return self.c.dram(name, shape, dt, kind="ExternalOutput")

    def dump(self, name, tt, shape, dt):
        if not DBG.get("dump"):
            return
        o = self.out("dbg_" + name, shape, dt)
        self.c.dma("sp", o[:], tt[:], reads=tt.all() or [tt.b()], writes=[o.b()])

    def scratch(self, name, shape, dt):
        return self.c.dram(name, shape, dt)

    def load_small(self, es, name, dram_ap, shape, src_tt):
        c = self.c
        t = c.sbuf(es, name, shape, F32)
        c.dma("sp", t[:], dram_ap, reads=[src_tt.b()], writes=[t.b()])
        return t


def rstd_from_stat(c, rstd, ps_ap, n, dim, rbuf, psbuf):
    c.op("dve", lambda e: e.tensor_scalar(rstd[:, 0:n], ps_ap, 1.0 / dim, EPS, ALU.mult, ALU.add),
         reads=[psbuf], writes=[rbuf])
    c.op("act", lambda e: e.activation(out=rstd[:, 0:n], in_=rstd[:, 0:n], func=AF.Sqrt),
         reads=[rbuf], writes=[rbuf])
    c.op("dve", lambda e: e.reciprocal(rstd[:, 0:n], rstd[:, 0:n]),
         reads=[rbuf], writes=[rbuf])


def prenorm(P, xT, halo, H, gcol, hT):
    c = P.c
    with ExitStack() as es:
        xs = c.sbuf(es, "pn_xs", [128, KC, TB], F32)
        sq = c.sbuf(es, "pn_sq", [128, KC, TB], BF16)
        rstd = c.sbuf(es, "pn_rstd", [128, TB], F32)
        blocks = []
        if H:
            blocks.append((halo, None, 0, H, 0))
        for tb in range(NTB):
            blocks.append((xT, tb, tb * TB, TB, H + tb * TB))
        for bi, (src, key, s0, n, c0) in enumerate(blocks):
            c.dma("sp", xs[:, :, 0:n], src[:, s0:s0 + n].rearrange("(m p) t -> p m t", p=128),
                  reads=[src.b(key)], writes=[xs.b()])
            c.op("act", lambda e: e.activation(out=sq[:, :, 0:n], in_=xs[:, :, 0:n], func=AF.Square),
                 reads=[xs.b()], writes=[sq.b()])
            pst = P.ps[4 + bi % 2]
            c.mm(pst[:, 0:n], [(P.ones[:], sq[:, kc, 0:n]) for kc in range(KC)], pst.b(), [sq.b(), P.ones.b()])
            rstd_from_stat(c, rstd, pst[:, 0:n], n, D, rstd.b(), pst.b())
            for kc in range(KC):
                eng = "dve"
                c.op(eng, lambda e, kc=kc: e.scalar_tensor_tensor(
                    out=hT[:, kc, c0:c0 + n], in0=xs[:, kc, 0:n], scalar=gcol[:, kc:kc + 1], in1=rstd[:, 0:n],
                    op0=ALU.mult, op1=ALU.mult),
                    reads=[xs.b(), rstd.b(), gcol.b()], writes=[hT.b(("tb", bi))])
    c.barrier()


def hT_bufs(hT):
    return hT.all()


def final_proj(P, zt_get, KCz, W, Wtt, bias_col, gcol, xT_src, xT_dst):
    c = P.c
    ncols = min(512, 8192 // KCz)
    with ExitStack() as es:
        NW = 2
        wbufs = [c.sbuf(es, f"fp_w{i}", [128, KCz, ncols], BF16) for i in range(NW)]
        ysb = c.sbuf(es, "fp_y", [128, KC, TB], F32)
        xsb = c.sbuf(es, "fp_x", [128, KC, TB], F32)
        ysq = [c.sbuf(es, f"fp_ysq{i}", [128, TB], BF16) for i in range(2)]
        rstd = c.sbuf(es, "fp_rstd", [128, TB], F32)
        wi = 0
        for tb in range(NTB):
            zfn, zbufs = zt_get(tb)
            pst = P.ps[4 + tb % 2]
            for cg in range(D // ncols):
                wb = wbufs[wi % NW]
                wi += 1
                c.dma("pool", wb[:], W[:, cg * ncols:(cg + 1) * ncols].rearrange("(kc p) n -> p kc n", p=128),
                      reads=[Wtt.b()], writes=[wb.b()])
                for mm_ in range(ncols // 128):
                    m = cg * (ncols // 128) + mm_
                    pb = P.ps[m % 4]
                    c.mm(pb[:], [(wb[:, kc, mm_ * 128:(mm_ + 1) * 128], zfn(kc)) for kc in range(KCz)],
                         pb.b(), [wb.b()] + zbufs)
                    if bias_col is not None:
                        c.op("act", lambda e, m=m, pb=pb: e.activation(out=ysb[:, m, :], in_=pb[:], func=AF.Identity,
                                                                       bias=bias_col[:, m:m + 1], scale=1.0),
                             reads=[pb.b(), bias_col.b()], writes=[ysb.b(m)])
                        sqin, sqb = ysb[:, m, :], [ysb.b(m)]
                    else:
                        c.op("act", lambda e, m=m, pb=pb: e.activation(out=ysb[:, m, :], in_=pb[:], func=AF.Copy),
                             reads=[pb.b()], writes=[ysb.b(m)])
                        sqin, sqb = pb[:], [pb.b()]
                    yq = ysq[m % 2]
                    c.op("act", lambda e, yq=yq, sqin=sqin: e.activation(out=yq[:], in_=sqin, func=AF.Square),
                         reads=sqb, writes=[yq.b()])
                    c.mm(pst[:], [(P.ones[:], yq[:])], pst.b(), [yq.b(), P.ones.b()], start=(m == 0), stop=(m == KC - 1))
            rstd_from_stat(c, rstd, pst[:], TB, D, rstd.b(), pst.b())
            c.dma("sp", xsb[:], xT_src[:, tb * TB:(tb + 1) * TB].rearrange("(m p) t -> p m t", p=128),
                  reads=[xT_src.b(tb)], writes=[xsb.b()])
            for m in range(KC):
                c.op("dve", lambda e, m=m: e.scalar_tensor_tensor(
                    out=ysb[:, m, :], in0=ysb[:, m, :], scalar=gcol[:, m:m + 1], in1=rstd[:], op0=ALU.mult, op1=ALU.mult),
                    reads=[ysb.b(m), rstd.b(), gcol.b()], writes=[ysb.b(m)])
                c.op("pool", lambda e, m=m: e.tensor_tensor(out=ysb[:, m, :], in0=ysb[:, m, :], in1=xsb[:, m, :], op=ALU.add),
                     reads=[ysb.b(m), xsb.b()], writes=[ysb.b(m)])
            c.dma("sp", xT_dst[:, tb * TB:(tb + 1) * TB].rearrange("(m p) t -> p m t", p=128), ysb[:],
                  reads=[ysb.b(m) for m in range(KC)], writes=[xT_dst.b(tb)])
    c.barrier()


def ffn(P, L, xT_src, xT_dst, halo, prm):
    c = P.c
    H = 2
    hid = P.scratch(f"hid{L}", [8192, T], BF16)
    with ExitStack() as es0:
        hT = c.sbuf(es0, "ffn_hT", [128, KC, H + T], BF16)
        gpre = P.load_small(es0, "ffn_gpre", prm["g_ffn_pre"][L], [128, KC], prm["g_ffn_pre"])
        gpost = P.load_small(es0, "ffn_gpost", prm["g_ffn_post"][L], [128, KC], prm["g_ffn_post"])
        fcw = P.load_small(es0, "ffn_cw", prm["fcw"][L], [128, 64 * 3], prm["fcw"])
        fcb = P.load_small(es0, "ffn_cb", prm["fcb"][L], [128, 64], prm["fcb"])
        prenorm(P, xT_src, halo, H, gpre, hT)
        Wgu = prm["f_w_gate_up"]
        with ExitStack() as es:
            wg = [c.sbuf(es, f"ffn_wg{i}", [128, KC, 512], BF16) for i in range(2)]
            wu = [c.sbuf(es, f"ffn_wu{i}", [128, KC, 512], BF16) for i in range(2)]
            gsb = [c.sbuf(es, f"ffn_gsb{i}", [128, H + T], F32) for i in range(2)]
            cv = [c.sbuf(es, f"ffn_cv{i}", [128, TB], F32) for i in range(2)]
            ge = [c.sbuf(es, f"ffn_ge{i}", [128, TB], F32) for i in range(2)]
            hsb = [c.sbuf(es, f"ffn_h{i}", [128, T], BF16) for i in range(2)]
            hb = hT.all()
            it = 0
            for jg in range(16):
                g_, u_ = wg[jg % 2], wu[jg % 2]
                c.dma("pool", g_[:], Wgu[L, :, jg * 512:(jg + 1) * 512].rearrange("(kc p) n -> p kc n", p=128),
                      reads=[Wgu.b()], writes=[g_.b()])
                c.dma("pool", u_[:], Wgu[L, :, 8192 + jg * 512:8192 + (jg + 1) * 512].rearrange("(kc p) n -> p kc n", p=128),
                      reads=[Wgu.b()], writes=[u_.b()])
                for jj in range(4):
                    j = jg * 4 + jj
                    gs, hs = gsb[j % 2], hsb[j % 2]
                    ph = P.ps[4]
                    c.mm(ph[:, 0:H], [(g_[:, kc, jj * 128:(jj + 1) * 128], hT[:, kc, 0:H]) for kc in range(KC)],
                         ph.b(), [g_.b()] + hb)
                    c.op("act", lambda e, gs=gs, ph=ph: e.activation(out=gs[:, 0:H], in_=ph[:, 0:H], func=AF.Copy),
                         reads=[ph.b()], writes=[gs.b("h")])
                    for tb in range(NTB):
                        pg, pu = P.ps[it % 2], P.ps[2 + it % 2]
                        cvt, get = cv[it % 2], ge[it % 2]
                        it += 1
                        c0 = H + tb * TB
                        c.mm(pg[:], [(g_[:, kc, jj * 128:(jj + 1) * 128], hT[:, kc, c0:c0 + TB]) for kc in range(KC)],
                             pg.b(), [g_.b()] + hb)
                        c.mm(pu[:], [(u_[:, kc, jj * 128:(jj + 1) * 128], hT[:, kc, c0:c0 + TB]) for kc in range(KC)],
                             pu.b(), [u_.b()] + hb)
                        c.op("act", lambda e, gs=gs, pg=pg, c0=c0: e.activation(out=gs[:, c0:c0 + TB], in_=pg[:], func=AF.Copy),
                             reads=[pg.b()], writes=[gs.b(tb)])
                        prev = gs.b("h") if tb == 0 else gs.b(tb - 1)
                        s = tb * TB
                        c.op("act", lambda e, gs=gs, cvt=cvt, s=s, j=j: e.activation(
                            out=cvt[:], in_=gs[:, s:s + TB], func=AF.Identity, bias=fcb[:, j:j + 1], scale=fcw[:, j * 3:j * 3 + 1]),
                            reads=[gs.b(tb), prev, fcw.b(), fcb.b()], writes=[cvt.b()])
                        for k in (1, 2):
                            c.op("dve", lambda e, gs=gs, cvt=cvt, s=s, j=j, k=k: e.scalar_tensor_tensor(
                                out=cvt[:], in0=gs[:, s + k:s + k + TB], scalar=fcw[:, j * 3 + k:j * 3 + k + 1], in1=cvt[:],
                                op0=ALU.mult, op1=ALU.add),
                                reads=[gs.b(tb), prev, cvt.b(), fcw.b()], writes=[cvt.b()])
                        c.op("act", lambda e, cvt=cvt, get=get: e.activation(out=get[:], in_=cvt[:], func=AF.Gelu_apprx_tanh),
                             reads=[cvt.b()], writes=[get.b()])
                        c.op("dve", lambda e, get=get, pu=pu, hs=hs, s=s: e.tensor_tensor(
                            out=hs[:, s:s + TB], in0=get[:], in1=pu[:], op=ALU.mult),
                            reads=[get.b(), pu.b()], writes=[hs.b()])
                    c.dma("sp", hid[j * 128:(j + 1) * 128, :], hs[:], reads=[hs.b()], writes=[hid.b(j)])
        c.barrier()
    with ExitStack() as es:
        zt = c.sbuf(es, "ffn_zt", [128, 64, TB], BF16)
        gpost = P.load_small(es, "ffn_gpost2", prm["g_ffn_post"][L], [128, KC], prm["g_ffn_post"])

        def zt_get(tb):
            c.dma("sp", zt[:], hid[:, tb * TB:(tb + 1) * TB].rearrange("(j p) t -> p j t", p=128),
                  reads=hid.all(), writes=[zt.b()])
            return (lambda kc: zt[:, kc, :]), [zt.b()]

        final_proj(P, zt_get, 64, prm["f_w_down"][L], prm["f_w_down"], None, gpost, xT_src, xT_dst)


def std_blocks(H):
    bl = []
    if H:
        bl.append((0, H))
    for tb in range(NTB):
        bl.append((H + tb * TB, TB))
    return bl


def proj_fm(P, hT, hbufs, W_ap_fn, Wtt, ncols, blocks, epi, wname="pfw", kc_n=KC, psl=(0, 1, 2, 3)):
    c = P.c
    with ExitStack() as es:
        wb = [c.sbuf(es, f"{wname}{i}", [128, kc_n, 512], BF16) for i in range(2)]
        it = 0
        for cg in range((ncols + 511) // 512):
            w = wb[cg % 2]
            nc_ = min(512, ncols - cg * 512)
            c.dma("pool", w[:, :, 0:nc_], W_ap_fn(cg * 512, nc_).rearrange("(kc p) n -> p kc n", p=128),
                  reads=[Wtt.b()], writes=[w.b()])
            for mm_ in range((nc_ + 127) // 128):
                m = cg * 4 + mm_
                mw = min(128, nc_ - mm_ * 128)
                for bi, (c0, n) in enumerate(blocks):
                    pb = P.ps[psl[it % len(psl)]]
                    it += 1
                    c.mm(pb[0:mw, 0:n], [(w[:, kc, mm_ * 128:mm_ * 128 + mw], hT[:, kc, c0:c0 + n]) for kc in range(kc_n)],
                         pb.b(), [w.b()] + hbufs)
                    epi(m, bi, pb, n)
        c.barrier()


def proj_tm(P, hT, hbufs, W_ap_fn, Wtt, ncols, tiles, epi, wname="ptw", kc_n=KC, psl=(0, 1, 2, 3)):
    c = P.c
    with ExitStack() as es:
        wb = [c.sbuf(es, f"{wname}{i}", [128, kc_n, 512], BF16) for i in range(2)]
        it = 0
        for nb in range((ncols + 511) // 512):
            w = wb[nb % 2]
            nc_ = min(512, ncols - nb * 512)
            c.dma("pool", w[:, :, 0:nc_], W_ap_fn(nb * 512, nc_).rearrange("(kc p) n -> p kc n", p=128),
                  reads=[Wtt.b()], writes=[w.b()])
            for ti, c0 in enumerate(tiles):
                pb = P.ps[psl[it % len(psl)]]
                it += 1
                c.mm(pb[:, 0:nc_], [(hT[:, kc, c0:c0 + 128], w[:, kc, 0:nc_]) for kc in range(kc_n)],
                     pb.b(), [w.b()] + hbufs)
                epi(nb, ti, pb, nc_)
        c.barrier()


def prenorm_blocks(P, blocks, gcol, hT, nkc=KC):
    c = P.c
    with ExitStack() as es:
        xs = c.sbuf(es, "pn_xs", [128, KC, TB], F32)
        sq = c.sbuf(es, "pn_sq", [128, KC, TB], BF16)
        rstd = c.sbuf(es, "pn_rstd", [128, TB], F32)
        for bi, (src, key, s0, n, c0) in enumerate(blocks):
            c.dma("sp", xs[:, :, 0:n], src[:, s0:s0 + n].rearrange("(m p) t -> p m t", p=128),
                  reads=[src.b(key)], writes=[xs.b()])
            c.op("act", lambda e: e.activation(out=sq[:, :, 0:n], in_=xs[:, :, 0:n], func=AF.Square),
                 reads=[xs.b()], writes=[sq.b()])
            pst = P.ps[4 + bi % 2]
            c.mm(pst[:, 0:n], [(P.ones[:], sq[:, kc, 0:n]) for kc in range(KC)], pst.b(), [sq.b(), P.ones.b()])
            rstd_from_stat(c, rstd, pst[:, 0:n], n, D, rstd.b(), pst.b())
            for kc in range(KC):
                c.op("dve", lambda e, kc=kc: e.scalar_tensor_tensor(
                    out=hT[:, kc, c0:c0 + n], in0=xs[:, kc, 0:n], scalar=gcol[:, kc:kc + 1], in1=rstd[:, 0:n],
                    op0=ALU.mult, op1=ALU.mult),
                    reads=[xs.b(), rstd.b(), gcol.b()], writes=[hT.b(("tb", bi))])
    c.barrier()


class AttnScratch:
    def __init__(self, P, es):
        c = P.c
        self.s = [c.sbuf(es, f"at_s{i}", [128, 256], F32) for i in range(2)]
        self.p = [c.sbuf(es, f"at_p{i}", [128, 256], F32) for i in range(2)]
        self.pn = [c.sbuf(es, f"at_pn{i}", [128, 256], BF16) for i in range(2)]
        self.pT = [c.sbuf(es, f"at_pT{i}", [128, 2, 128], BF16) for i in range(2)]
        self.st = [c.sbuf(es, f"at_st{i}", [128, 8], F32) for i in range(2)]
        self.it = 0


def attn_core(P, S, q_ap, k_ap, qk_bufs, v_aps, v_bufs, scale, bias_ap, extra_ap, sink_ap, cbufs, out_fn):
    c = P.c
    k = S.it % 2
    S.it += 1
    s, p, pn, pT, st = S.s[k], S.p[k], S.pn[k], S.pT[k], S.st[k]
    pl = P.ps[k]
    c.mm(pl[:, 0:256], [(q_ap, k_ap)], pl.b(), qk_bufs)
    if bias_ap is not None:
        c.op("dve", lambda e: e.scalar_tensor_tensor(out=s[:], in0=pl[:, 0:256], scalar=scale, in1=bias_ap,
                                                     op0=ALU.mult, op1=ALU.add),
             reads=[pl.b()] + cbufs, writes=[s.b()])
        if extra_ap is not None:
            c.op("pool", lambda e: e.tensor_tensor(out=s[:], in0=s[:], in1=extra_ap, op=ALU.add),
                 reads=[s.b()] + cbufs, writes=[s.b()])
        src, srcb, esc = s[:], s.b(), 1.0
    else:
        src, srcb, esc = pl[:, 0:256], pl.b(), scale
    c.op("dve", lambda e: e.reduce_max(out=st[:, 0:1], in_=src, axis=AX.X), reads=[srcb], writes=[st.b()])
    if sink_ap is not None:
        c.op("dve", lambda e: e.tensor_tensor(out=st[:, 0:1], in0=st[:, 0:1], in1=sink_ap, op=ALU.max),
             reads=[st.b()] + cbufs, writes=[st.b()])
    c.op("dve", lambda e: e.tensor_scalar(st[:, 1:2], st[:, 0:1], -esc, None, ALU.mult), reads=[st.b()], writes=[st.b()])
    c.op("act", lambda e: e.activation(out=p[:], in_=src, func=AF.Exp, bias=st[:, 1:2], scale=esc, accum_out=st[:, 2:3]),
         reads=[srcb, st.b()], writes=[p.b(), st.b()])
    if sink_ap is not None:
        c.op("act", lambda e: e.activation(out=st[:, 3:4], in_=st[:, 1:2], func=AF.Exp, bias=sink_ap, scale=1.0),
             reads=[st.b()] + cbufs, writes=[st.b()])
        c.op("dve", lambda e: e.tensor_tensor(out=st[:, 2:3], in0=st[:, 2:3], in1=st[:, 3:4], op=ALU.add),
             reads=[st.b()], writes=[st.b()])
    c.op("dve", lambda e: e.reciprocal(st[:, 4:5], st[:, 2:3]), reads=[st.b()], writes=[st.b()])
    c.op("dve", lambda e: e.tensor_scalar(pn[:], p[:], st[:, 4:5], None, ALU.mult), reads=[p.b(), st.b()], writes=[pn.b()])
    pb = P.psb[k]
    for kt in range(2):
        c.transpose(pb[:, kt * 128:(kt + 1) * 128], pn[:, kt * 128:(kt + 1) * 128], P.ident[:], pb.b(), [pn.b(), P.ident.b()])
    c.op("act", lambda e: e.activation(out=pT[:, 0, :], in_=pb[:, 0:128], func=AF.Copy), reads=[pb.b()], writes=[pT.b()])
    c.op("act", lambda e: e.activation(out=pT[:, 1, :], in_=pb[:, 128:256], func=AF.Copy), reads=[pb.b()], writes=[pT.b()])
    po = P.ps[2 + k]
    c.mm(po[:, 0:128], [(v_aps[0], pT[:, 0, :]), (v_aps[1], pT[:, 1, :])], po.b(), [pT.b()] + v_bufs)
    out_fn(po)


def xattn(P, L, xT_src, xT_dst, prm):
    c = P.c
    with ExitStack() as es1:
        qT = c.sbuf(es1, "xa_qT", [128, 4, T], BF16)
        kT = c.sbuf(es1, "xa_kT", [128, 4, 256], BF16)
        vm = c.sbuf(es1, "xa_v", [128, 2, 512], BF16)
        oT = c.sbuf(es1, "xa_oT", [128, 4, T], BF16)
        gpost = P.load_small(es1, "xa_gpost", prm["g_xattn_post"][L], [128, KC], prm["g_xattn_post"])
        with ExitStack() as es0:
            hT = c.sbuf(es0, "xa_hT", [128, KC, T], BF16)
            mT = c.sbuf(es0, "xa_mT", [128, KC, 256], BF16)
            gpre = P.load_small(es0, "xa_gpre", prm["g_xattn_pre"][L], [128, KC], prm["g_xattn_pre"])
            gmem = P.load_small(es0, "xa_gmem", prm["g_mem"][L], [128, KC], prm["g_mem"])
            prenorm_blocks(P, [(xT_src, tb, tb * TB, TB, tb * TB) for tb in range(NTB)], gpre, hT)
            prenorm_blocks(P, [(prm["memT"], None, 0, 256, 0)], gmem, mT)
            Wq, Wkv = prm["x_w_q"], prm["x_w_kv"]

            def epi_q(m, bi, pb, n):
                c.op("act", lambda e: e.activation(out=qT[:, m, bi * TB:(bi + 1) * TB], in_=pb[:, 0:n], func=AF.Copy),
                     reads=[pb.b()], writes=[qT.b()])
            proj_fm(P, hT, hT.all(), lambda c0, n: Wq[L, :, c0:c0 + n], Wq, 512, std_blocks(0), epi_q)

            def epi_k(m, bi, pb, n):
                c.op("act", lambda e: e.activation(out=kT[:, m, :], in_=pb[:, 0:n], func=AF.Copy),
                     reads=[pb.b()], writes=[kT.b()])
            proj_fm(P, mT, mT.all(), lambda c0, n: Wkv[L, :, c0:c0 + n], Wkv, 512, [(0, 256)], epi_k)

            def epi_v(nb, ti, pb, n):
                c.op("act", lambda e: e.activation(out=vm[:, ti, :], in_=pb[:, 0:n], func=AF.Copy),
                     reads=[pb.b()], writes=[vm.b()])
            proj_tm(P, mT, mT.all(), lambda c0, n: Wkv[L, :, 512 + c0:512 + c0 + n], Wkv, 512, [0, 128], epi_v)
            c.barrier()
            P.dump("hT", hT, [128, KC, T], BF16)
            P.dump("mT", mT, [128, KC, 256], BF16)
            c.barrier()
        c.barrier()
        with ExitStack() as es2:
            S = AttnScratch(P, es2)
            for i in range(T // 128):
                for h in range(4):
                    def out_fn(po, i=i, h=h):
                        c.op("act", lambda e: e.activation(out=oT[:, h, i * 128:(i + 1) * 128], in_=po[:, 0:128], func=AF.Copy),
                             reads=[po.b()], writes=[oT.b((h, i))])
                    attn_core(P, S, qT[:, h, i * 128:(i + 1) * 128], kT[:, h, :], [qT.b(), kT.b()],
                              [vm[:, 0, h * 128:(h + 1) * 128], vm[:, 1, h * 128:(h + 1) * 128]], [vm.b()],
                              128 ** -0.5, None, None, None, [], out_fn)
        c.barrier()
        P.dump("qT", qT, [128, 4, T], BF16)
        P.dump("kT", kT, [128, 4, 256], BF16)
        P.dump("vm", vm, [128, 2, 512], BF16)
        P.dump("oT", oT, [128, 4, T], BF16)
        c.barrier()

        def zt_get(tb):
            return (lambda kc: oT[:, kc, tb * TB:(tb + 1) * TB]), oT.all()
        final_proj(P, zt_get, 4, prm["x_w_o"][L], prm["x_w_o"], None, gpost, xT_src, xT_dst)


def swa(P, L, j, xT_src, xT_dst, halo, prm):
    c = P.c
    H = 128
    Wqkv = prm["a_w_qkv"]
    qd = P.scratch(f"swa_qd{L}", [2048, T], BF16)
    with ExitStack() as es1:
        kT2 = c.sbuf(es1, "sw_kT2", [128, 4, H + T], BF16)
        vdup = c.sbuf(es1, "sw_vdup", [128, 17, 512], BF16)
        gpost = P.load_small(es1, "sw_gpost", prm["g_mix_post"][L], [128, KC], prm["g_mix_post"])
        with ExitStack() as es0:
            hT = c.sbuf(es0, "sw_hT", [128, KC, H + T], BF16)
            gpre = P.load_small(es0, "sw_gpre", prm["g_mix_pre"][L], [128, KC], prm["g_mix_pre"])
            prenorm_blocks(P, [(halo, None, 0, H, 0)] + [(xT_src, tb, tb * TB, TB, H + tb * TB) for tb in range(NTB)], gpre, hT)
            hb = hT.all()
            with ExitStack() as es:
                stg = [c.sbuf(es, f"sw_stg{i}", [128, TB], BF16) for i in range(4)]
                cnt = [0]

                def epi_q(m, bi, pb, n):
                    st = stg[cnt[0] % 4]
                    cnt[0] += 1
                    c.op("act", lambda e: e.activation(out=st[:], in_=pb[:, 0:n], func=AF.Copy), reads=[pb.b()], writes=[st.b()])
                    c.dma("sp", qd[m * 128:(m + 1) * 128, bi * TB:(bi + 1) * TB], st[:], reads=[st.b()], writes=[qd.b()])
                proj_fm(P, hT, hb, lambda c0, n: Wqkv[j, :, c0:c0 + n], Wqkv, 2048, std_blocks(0)[0:0] + [(H + tb * TB, TB) for tb in range(NTB)], epi_q)
                wkd = c.sbuf(es, "sw_wkd", [128, KC, 512], BF16)
                wvd = c.sbuf(es, "sw_wvd", [128, KC, 512], BF16)
                for g in range(4):
                    for dup in range(2):
                        o = g * 128 + dup * 64
                        c.dma("pool", wkd[:, :, o:o + 64], Wqkv[j, :, 2048 + g * 64:2048 + (g + 1) * 64].rearrange("(kc p) n -> p kc n", p=128),
                              reads=[Wqkv.b()], writes=[wkd.b()])
                        c.dma("pool", wvd[:, :, o:o + 64], Wqkv[j, :, 2304 + g * 64:2304 + (g + 1) * 64].rearrange("(kc p) n -> p kc n", p=128),
                              reads=[Wqkv.b()], writes=[wvd.b()])
                it = 0
                for g in range(4):
                    for (c0, n) in std_blocks(H):
                        pb = P.ps[it % 4]
                        it += 1
                        c.mm(pb[:, 0:n], [(wkd[:, kc, g * 128:(g + 1) * 128], hT[:, kc, c0:c0 + n]) for kc in range(KC)], pb.b(), [wkd.b()] + hb)
                        c.op("act", lambda e, pb=pb, g=g, c0=c0, n=n: e.activation(out=kT2[:, g, c0:c0 + n], in_=pb[:, 0:n], func=AF.Copy),
                             reads=[pb.b()], writes=[kT2.b()])
                for ti in range(17):
                    pb = P.ps[it % 4]
                    it += 1
                    c.mm(pb[:], [(hT[:, kc, ti * 128:(ti + 1) * 128], wvd[:, kc, :]) for kc in range(KC)], pb.b(), [wvd.b()] + hb)
                    c.op("act", lambda e, pb=pb, ti=ti: e.activation(out=vdup[:, ti, :], in_=pb[:], func=AF.Copy),
                         reads=[pb.b()], writes=[vdup.b()])
            c.barrier()
        qT = c.sbuf(es1, "sw_qT", [128, KC, T], BF16)
        c.dma("sp", qT[:], qd[:, :].rearrange("(m p) t -> p m t", p=128), reads=qd.all(), writes=[qT.b()])
        with ExitStack() as es2:
            biasT = c.sbuf(es2, "sw_bias", [128, 32, 256], F32)
            tab = P.load_small(es2, "sw_tab", prm["rel_bc"][:, :], [128, 1024], prm["rel_bc"])
            sink = P.load_small(es2, "sw_sink", prm["sink_bc"][j], [128, 32], prm["sink_bc"])
            madd = P.load_small(es2, "sw_madd", prm["maskadd"][:, :], [128, 256], prm["maskadd"])
            hm = P.load_small(es2, "sw_hm", prm["halo_mask"][:, :], [128, 256], prm["halo_mask"])
            eb = [c.sbuf(es2, f"sw_eb{i}", [128, 256], F32) for i in range(2)]
            for h in range(32):
                c.op("pool", lambda e, h=h: e.tensor_copy(biasT[:, h, :], madd[:]), reads=[madd.b()], writes=[biasT.b(h)])
            Eoh = prm["Eoh"]
            for b in range(32):
                e_ = eb[b % 2]
                c.dma("sp", e_[:], Eoh[b], reads=[Eoh.b()], writes=[e_.b()])
                for h in range(32):
                    c.op("dve", lambda e, h=h, b=b, e_=e_: e.scalar_tensor_tensor(
                        out=biasT[:, h, :], in0=e_[:], scalar=tab[:, b * 32 + h:b * 32 + h + 1], in1=biasT[:, h, :],
                        op0=ALU.mult, op1=ALU.add), reads=[e_.b(), tab.b(), biasT.b(h)], writes=[biasT.b(h)])
            S = AttnScratch(P, es2)
            for i in range(T // 128):
                for h in range(32):
                    g, hp, tl = h // 8, (h % 2) * 64, h // 2

                    def out_fn(po, i=i, hp=hp, tl=tl, h=h):
                        c.op("act", lambda e: e.activation(out=qT[hp:hp + 64, tl, i * 128:(i + 1) * 128], in_=po[hp:hp + 64, 0:128], func=AF.Copy),
                             reads=[po.b()], writes=[qT.b((h, i))])
                    attn_core(P, S, qT[hp:hp + 64, tl, i * 128:(i + 1) * 128], kT2[hp:hp + 64, g, i * 128:(i + 2) * 128],
                              [qT.b(), qT.b((h, i)), kT2.b()],
                              [vdup[:, i, g * 128:(g + 1) * 128], vdup[:, i + 1, g * 128:(g + 1) * 128]], [vdup.b()],
                              0.125, biasT[:, h, :], (hm[:] if i == 0 else None), sink[:, h:h + 1],
                              [biasT.b(h), hm.b(), sink.b()], out_fn)
        c.barrier()

        def zt_get(tb):
            return (lambda kc: qT[:, kc, tb * TB:(tb + 1) * TB]), qT.all()
        final_proj(P, zt_get, KC, prm["a_w_o"][j], prm["a_w_o"], None, gpost, xT_src, xT_dst)


def stat_accum(P, src_ap, src_buf, tb, s1, s2, first, tmp):
    c = P.c
    a, b = tmp
    c.op("act", lambda e: e.activation(out=a[:], in_=src_ap, func=AF.Copy), reads=[src_buf], writes=[a.b()])
    c.op("act", lambda e: e.activation(out=b[:], in_=src_ap, func=AF.Square), reads=[src_buf], writes=[b.b()])
    for (st, t_, pi) in ((s1, a, 4), (s2, b, 5)):
        pb = P.ps[pi]
        c.mm(pb[:], [(P.ones[:], t_[:])], pb.b(), [t_.b(), P.ones.b()])
        sl = st[:, tb * TB:(tb + 1) * TB]
        if first:
            c.op("dve", lambda e, sl=sl, pb=pb: e.tensor_copy(sl, pb[:]), reads=[pb.b()], writes=[st.b(tb)])
        else:
            c.op("dve", lambda e, sl=sl, pb=pb: e.tensor_tensor(out=sl, in0=sl, in1=pb[:], op=ALU.add),
                 reads=[pb.b(), st.b(tb)], writes=[st.b(tb)])


def ln_stats(P, s1, s2, tb, dim, mean, rstd, tmp):
    c = P.c
    sl1, sl2 = s1[:, tb * TB:(tb + 1) * TB], s2[:, tb * TB:(tb + 1) * TB]
    c.op("dve", lambda e: e.tensor_scalar(mean[:], sl1, 1.0 / dim, None, ALU.mult), reads=[s1.b(tb)], writes=[mean.b()])
    c.op("dve", lambda e: e.tensor_tensor(out=tmp[:], in0=mean[:], in1=mean[:], op=ALU.mult), reads=[mean.b()], writes=[tmp.b()])
    c.op("dve", lambda e: e.scalar_tensor_tensor(out=rstd[:], in0=sl2, scalar=1.0 / dim, in1=tmp[:], op0=ALU.mult, op1=ALU.subtract),
         reads=[s2.b(tb), tmp.b()], writes=[rstd.b()])
    c.op("dve", lambda e: e.tensor_scalar(rstd[:], rstd[:], 0.0, EPS, ALU.max, ALU.add), reads=[rstd.b()], writes=[rstd.b()])
    c.op("act", lambda e: e.activation(out=rstd[:], in_=rstd[:], func=AF.Sqrt), reads=[rstd.b()], writes=[rstd.b()])
    c.op("dve", lambda e: e.reciprocal(rstd[:], rstd[:]), reads=[rstd.b()], writes=[rstd.b()])


def conformer(P, L, j, xT_src, xT_dst, halo, prm):
    c = P.c
    H = 32
    W1 = prm["c_w_pw1"]
    cvd = P.scratch(f"cf_cvd{L}", [2048, T], F32)
    with ExitStack() as es1:
        s1 = c.sbuf(es1, "cf_s1", [128, T], F32)
        s2 = c.sbuf(es1, "cf_s2", [128, T], F32)
        with ExitStack() as es0:
            hT = c.sbuf(es0, "cf_hT", [128, KC, H + T], BF16)
            gpre = P.load_small(es0, "cf_gpre", prm["g_mix_pre"][L], [128, KC], prm["g_mix_pre"])
            b1 = P.load_small(es0, "cf_b1", prm["c_b_pw1c"][j], [128, 32], prm["c_b_pw1c"])
            wdw = P.load_small(es0, "cf_wdw", prm["c_w_dwc"][j], [128, 16 * 31], prm["c_w_dwc"])
            bdw = P.load_small(es0, "cf_bdw", prm["c_b_dwc"][j], [128, 16], prm["c_b_dwc"])
            hv = P.load_small(es0, "cf_hv", prm["hv"][:, :], [128, 1], prm["hv"])
            prenorm_blocks(P, [(halo, None, 0, H, 0)] + [(xT_src, tb, tb * TB, TB, H + tb * TB) for tb in range(NTB)], gpre, hT)
            hb = hT.all()
            wa = [c.sbuf(es0, f"cf_wa{i}", [128, KC, 128], BF16) for i in range(2)]
            wg = [c.sbuf(es0, f"cf_wg{i}", [128, KC, 128], BF16) for i in range(2)]
            glb = [c.sbuf(es0, f"cf_gl{i}", [128, H + T], F32) for i in range(2)]
            cvb = [c.sbuf(es0, f"cf_cv{i}", [128, T], F32) for i in range(2)]
            sgt = [c.sbuf(es0, f"cf_sg{i}", [128, TB], F32) for i in range(2)]
            tmpa = c.sbuf(es0, "cf_ta", [128, TB], BF16)
            tmpb = c.sbuf(es0, "cf_tb", [128, TB], BF16)
            it = 0
            for m in range(16):
                wa_, wg_, gl, cvo = wa[m % 2], wg[m % 2], glb[m % 2], cvb[m % 2]
                c.dma("pool", wa_[:], W1[j, :, m * 128:(m + 1) * 128].rearrange("(kc p) n -> p kc n", p=128), reads=[W1.b()], writes=[wa_.b()])
                c.dma("pool", wg_[:], W1[j, :, 2048 + m * 128:2048 + (m + 1) * 128].rearrange("(kc p) n -> p kc n", p=128), reads=[W1.b()], writes=[wg_.b()])
                for (c0, n) in std_blocks(H):
                    pa, pg, sg = P.ps[it % 2], P.ps[2 + it % 2], sgt[it % 2]
                    it += 1
                    c.mm(pa[:, 0:n], [(wa_[:, kc, :], hT[:, kc, c0:c0 + n]) for kc in range(KC)], pa.b(), [wa_.b()] + hb)
                    c.mm(pg[:, 0:n], [(wg_[:, kc, :], hT[:, kc, c0:c0 + n]) for kc in range(KC)], pg.b(), [wg_.b()] + hb)
                    c.op("act", lambda e, sg=sg, pg=pg, n=n, m=m: e.activation(out=sg[:, 0:n], in_=pg[:, 0:n], func=AF.Sigmoid, bias=b1[:, 16 + m:17 + m], scale=1.0),
                         reads=[pg.b(), b1.b()], writes=[sg.b()])
                    c.op("dve", lambda e, sg=sg, pa=pa, n=n, m=m, gl=gl, c0=c0: e.scalar_tensor_tensor(
                        out=gl[:, c0:c0 + n], in0=pa[:, 0:n], scalar=b1[:, m:m + 1], in1=sg[:, 0:n], op0=ALU.add, op1=ALU.mult),
                        reads=[pa.b(), sg.b(), b1.b()], writes=[gl.b()])
                c.op("dve", lambda e, gl=gl: e.tensor_scalar(gl[:, 0:H], gl[:, 0:H], hv[:, 0:1], None, ALU.mult), reads=[gl.b(), hv.b()], writes=[gl.b()])
                c.op("act", lambda e, gl=gl, cvo=cvo, m=m: e.activation(out=cvo[:], in_=gl[:, 2:2 + T], func=AF.Identity,
                                                                       bias=bdw[:, m:m + 1], scale=wdw[:, m * 31:m * 31 + 1]),
                     reads=[gl.b(), wdw.b(), bdw.b()], writes=[cvo.b()])
                for k in range(1, 31):
                    c.op("dve", lambda e, gl=gl, cvo=cvo, m=m, k=k: e.scalar_tensor_tensor(
                        out=cvo[:], in0=gl[:, 2 + k:2 + k + T], scalar=wdw[:, m * 31 + k:m * 31 + k + 1], in1=cvo[:], op0=ALU.mult, op1=ALU.add),
                        reads=[gl.b(), cvo.b(), wdw.b()], writes=[cvo.b()])
                c.dma("sp", cvd[m * 128:(m + 1) * 128, :], cvo[:], reads=[cvo.b()], writes=[cvd.b(m)])
                for tb in range(NTB):
                    stat_accum(P, cvo[:, tb * TB:(tb + 1) * TB], cvo.b(), tb, s1, s2, m == 0, (tmpa, tmpb))
        c.barrier()
        zT = c.sbuf(es1, "cf_zT", [128, KC, T], BF16)
        gpost = P.load_small(es1, "cf_gpost", prm["g_mix_post"][L], [128, KC], prm["g_mix_post"])
        b2 = P.load_small(es1, "cf_b2", prm["c_b_pw2c"][j], [128, 16], prm["c_b_pw2c"])
        with ExitStack() as es2:
            lg = P.load_small(es2, "cf_lg", prm["c_ln_gc"][j], [128, 16], prm["c_ln_gc"])
            lb = P.load_small(es2, "cf_lb", prm["c_ln_bc"][j], [128, 16], prm["c_ln_bc"])
            cx = c.sbuf(es2, "cf_cx", [128, KC, TB], F32)
            mean = c.sbuf(es2, "cf_mean", [128, TB], F32)
            rstd = c.sbuf(es2, "cf_rstd", [128, TB], F32)
            tmp = c.sbuf(es2, "cf_tmp", [128, TB], F32)
            for tb in range(NTB):
                ln_stats(P, s1, s2, tb, D, mean, rstd, tmp)
                c.dma("sp", cx[:], cvd[:, tb * TB:(tb + 1) * TB].rearrange("(m p) t -> p m t", p=128), reads=cvd.all(), writes=[cx.b()])
                for m in range(KC):
                    c.op("pool", lambda e, m=m: e.tensor_tensor(out=cx[:, m, :], in0=cx[:, m, :], in1=mean[:], op=ALU.subtract),
                         reads=[cx.b(), mean.b()], writes=[cx.b()])
                    c.op("dve", lambda e, m=m: e.tensor_tensor(out=cx[:, m, :], in0=cx[:, m, :], in1=rstd[:], op=ALU.mult),
                         reads=[cx.b(), rstd.b()], writes=[cx.b()])
                    c.op("act", lambda e, m=m, tb=tb: e.activation(out=zT[:, m, tb * TB:(tb + 1) * TB], in_=cx[:, m, :], func=AF.Silu,
                                                                   bias=lb[:, m:m + 1], scale=lg[:, m:m + 1]),
                         reads=[cx.b(), lg.b(), lb.b()], writes=[zT.b()])
        c.barrier()

        def zt_get(tb):
            return (lambda kc: zT[:, kc, tb * TB:(tb + 1) * TB]), zT.all()
        final_proj(P, zt_get, KC, prm["c_w_pw2"][j], prm["c_w_pw2"], b2, gpost, xT_src, xT_dst)


def gmlp(P, L, j, xT_src, xT_dst, prm):
    c = P.c
    Win = prm["d_w_in"]
    ud = P.scratch(f"gm_ud{L}", [4096, T], BF16)
    vd = P.scratch(f"gm_vd{L}", [4096, T], F32)
    zd = P.scratch(f"gm_zd{L}", [4096, T], BF16)
    with ExitStack() as es1:
        s1 = c.sbuf(es1, "gm_s1", [128, T], F32)
        s2 = c.sbuf(es1, "gm_s2", [128, T], F32)
        with ExitStack() as es0:
            hT = c.sbuf(es0, "gm_hT", [128, KC, T], BF16)
            gpre = P.load_small(es0, "gm_gpre", prm["g_mix_pre"][L], [128, KC], prm["g_mix_pre"])
            bin_ = P.load_small(es0, "gm_bin", prm["d_b_inc"][j], [128, 64], prm["d_b_inc"])
            prenorm_blocks(P, [(xT_src, tb, tb * TB, TB, tb * TB) for tb in range(NTB)], gpre, hT)
            stb = [c.sbuf(es0, f"gm_stb{i}", [128, TB], BF16) for i in range(3)]
            stf = [c.sbuf(es0, f"gm_stf{i}", [128, TB], F32) for i in range(3)]
            tmpa = c.sbuf(es0, "gm_ta", [128, TB], BF16)
            tmpb = c.sbuf(es0, "gm_tb", [128, TB], BF16)
            cnt = [0]

            def epi_u(m, bi, pb, n):
                st = stb[cnt[0] % 3]
                cnt[0] += 1
                c.op("act", lambda e: e.activation(out=st[:], in_=pb[:, 0:n], func=AF.Gelu, bias=bin_[:, m:m + 1], scale=1.0),
                     reads=[pb.b(), bin_.b()], writes=[st.b()])
                c.dma("sp", ud[m * 128:(m + 1) * 128, bi * TB:(bi + 1) * TB], st[:], reads=[st.b()], writes=[ud.b()])
            proj_fm(P, hT, hT.all(), lambda c0, n: Win[j, :, c0:c0 + n], Win, 4096, std_blocks(0), epi_u)

            def epi_v(m, bi, pb, n):
                st = stf[cnt[0] % 3]
                cnt[0] += 1
                c.op("act", lambda e: e.activation(out=st[:], in_=pb[:, 0:n], func=AF.Gelu, bias=bin_[:, 32 + m:33 + m], scale=1.0),
                     reads=[pb.b(), bin_.b()], writes=[st.b()])
                c.dma("sp", vd[m * 128:(m + 1) * 128, bi * TB:(bi + 1) * TB], st[:], reads=[st.b()], writes=[vd.b()])
                stat_accum(P, st[:], st.b(), bi, s1, s2, m == 0, (tmpa, tmpb))
            proj_fm(P, hT, hT.all(), lambda c0, n: Win[j, :, 4096 + c0:4096 + c0 + n], Win, 4096, std_blocks(0), epi_v)
        c.barrier()
        with ExitStack() as es2:
            lg = P.load_small(es2, "gm_lg", prm["d_ln_gc"][j], [128, 32], prm["d_ln_gc"])
            lb = P.load_small(es2, "gm_lb", prm["d_ln_bc"][j], [128, 32], prm["d_ln_bc"])
            wsf = P.load_small(es2, "gm_wsf", prm["d_w_sT"][j], [128, 8 * 128], prm["d_w_sT"])
            tri = P.load_small(es2, "gm_tri", prm["triT"][:, :], [128, 128], prm["triT"])
            bsb = P.load_small(es2, "gm_bsb", prm["d_b_sbc"][j], [128, 8 * 128], prm["d_b_sbc"])
            wm = c.sbuf(es2, "gm_wm", [128, 8, 128], BF16)
            for g in range(8):
                c.op("dve", lambda e, g=g: e.tensor_tensor(out=wm[:, g, :], in0=wsf[:, g * 128:(g + 1) * 128], in1=tri[:], op=ALU.mult),
                     reads=[wsf.b(), tri.b()], writes=[wm.b()])
            vx = c.sbuf(es2, "gm_vx", [128, 8, TB], F32)
            vln = c.sbuf(es2, "gm_vln", [128, 32, TB], BF16)
            vtokb = [c.sbuf(es2, f"gm_vtok{i}", [128, 4096], BF16) for i in range(2)]
            uxb = [c.sbuf(es2, f"gm_ux{i}", [128, 32, 128], BF16) for i in range(2)]
            ztb = [c.sbuf(es2, f"gm_zt{i}", [128, 32, 128], BF16) for i in range(2)]
            mean = c.sbuf(es2, "gm_mean", [128, TB], F32)
            rstd = c.sbuf(es2, "gm_rstd", [128, TB], F32)
            tmp = c.sbuf(es2, "gm_tmp", [128, TB], F32)
            svt = [c.sbuf(es2, f"gm_sv{i}", [128, 512], F32) for i in range(2)]
            it = 0
            for tb in range(NTB):
                ln_stats(P, s1, s2, tb, 4096, mean, rstd, tmp)
                for qtr in range(4):
                    c.dma("sp", vx[:], vd[qtr * 1024:(qtr + 1) * 1024, tb * TB:(tb + 1) * TB].rearrange("(m p) t -> p m t", p=128),
                          reads=vd.all(), writes=[vx.b()])
                    for mm_ in range(8):
                        m = qtr * 8 + mm_
                        c.op("pool", lambda e, mm_=mm_: e.tensor_tensor(out=vx[:, mm_, :], in0=vx[:, mm_, :], in1=mean[:], op=ALU.subtract),
                             reads=[vx.b(), mean.b()], writes=[vx.b()])
                        c.op("dve", lambda e, mm_=mm_: e.tensor_tensor(out=vx[:, mm_, :], in0=vx[:, mm_, :], in1=rstd[:], op=ALU.mult),
                             reads=[vx.b(), rstd.b()], writes=[vx.b()])
                        c.op("act", lambda e, mm_=mm_, m=m: e.activation(out=vln[:, m, :], in_=vx[:, mm_, :], func=AF.Identity,
                                                                         bias=lb[:, m:m + 1], scale=lg[:, m:m + 1]),
                             reads=[vx.b(), lg.b(), lb.b()], writes=[vln.b()])
                for tt in range(4):
                    k2 = (tb * 4 + tt) % 2
                    vtok, ux, zt = vtokb[k2], uxb[k2], ztb[k2]
                    t0 = tb * TB + tt * 128
                    c.dma("sp", ux[:], ud[:, t0:t0 + 128].rearrange("(m p) t -> p m t", p=128), reads=ud.all(), writes=[ux.b()])
                    for q in range(4):
                        pb = P.psb[q % 2]
                        for r in range(8):
                            m = q * 8 + r
                            c.transpose(pb[:, r * 128:(r + 1) * 128], vln[:, m, tt * 128:(tt + 1) * 128], P.ident[:], pb.b(), [vln.b(), P.ident.b()])
                        c.op("act", lambda e, pb=pb, vtok=vtok, q=q: e.activation(out=vtok[:, q * 1024:(q + 1) * 1024], in_=pb[:], func=AF.Copy),
                             reads=[pb.b()], writes=[vtok.b()])
                    for g in range(8):
                        pb = P.ps[it % 4]
                        sv = svt[it % 2]
                        it += 1
                        for dc in range(4):
                            ch = g * 4 + dc
                            c.mm(pb[:, dc * 128:(dc + 1) * 128], [(vtok[:, ch * 128:(ch + 1) * 128], wm[:, g, :])], pb.b(), [vtok.b(), wm.b()])
                        for dc in range(4):
                            ch = g * 4 + dc
                            c.op("dve", lambda e, sv=sv, pb=pb, g=g, dc=dc: e.tensor_tensor(
                                out=sv[:, dc * 128:(dc + 1) * 128], in0=pb[:, dc * 128:(dc + 1) * 128], in1=bsb[:, g * 128:(g + 1) * 128], op=ALU.add),
                                reads=[pb.b(), bsb.b()], writes=[sv.b()])
                            c.op("pool", lambda e, sv=sv, ch=ch, dc=dc, zt=zt, ux=ux: e.tensor_tensor(
                                out=zt[:, ch, :], in0=sv[:, dc * 128:(dc + 1) * 128], in1=ux[:, ch, :], op=ALU.mult),
                                reads=[sv.b(), ux.b()], writes=[zt.b()])
                    c.dma("sp", zd[:, t0:t0 + 128].rearrange("(m p) t -> p m t", p=128), zt[:], reads=[zt.b()], writes=[zd.b((tb, tt))])
        c.barrier()
    with ExitStack() as es3:
        zt2 = c.sbuf(es3, "gm_zt2", [128, 32, TB], BF16)
        gpost = P.load_small(es3, "gm_gpost", prm["g_mix_post"][L], [128, KC], prm["g_mix_post"])

        def zt_get(tb):
            c.dma("sp", zt2[:], zd[:, tb * TB:(tb + 1) * TB].rearrange("(m p) t -> p m t", p=128), reads=zd.all(), writes=[zt2.b()])
            return (lambda kc: zt2[:, kc, :]), [zt2.b()]
        final_proj(P, zt_get, 32, prm["d_w_out"][j], prm["d_w_out"], None, gpost, xT_src, xT_dst)


GLA_STOP = [99]


class _Stop(Exception):
    pass


def _chk(c, n):
    if GLA_STOP[0] == n:
        c.barrier()
        raise _Stop()


def gla_a(P, L, j, xT_src, prm, o_loc, qtil, sr, Send, Lam):
    c = P.c
    Wq = prm["b_w_qkvr"]
    EcpD = P.scratch("gl_ecp", [1024, T], F32)
    EcmD = P.scratch("gl_ecm", [1024, T], F32)
    EendD = P.scratch("gl_eend", [T, 1024], F32)
    qdT = P.scratch("gl_qdT", [1024, T], BF16)
    kinvT = P.scratch("gl_kinvT", [1024, T], BF16)
    kendD = P.scratch("gl_kend", [T, 1024], BF16)
    vD = P.scratch("gl_v", [T, 2048], BF16)
    with ExitStack() as es1:
        lastT = c.sbuf(es1, "gl_last", [128, 256], F32)
        EL = c.sbuf(es1, "gl_EL", [128, 256], F32)
        PF = c.sbuf(es1, "gl_PF", [128, 256], F32)
        EP = c.sbuf(es1, "gl_EP", [128, 256], F32)
        with ExitStack() as es0:
            hT = c.sbuf(es0, "gl_hT", [128, KC, T], BF16)
            gpre = P.load_small(es0, "gl_gpre", prm["g_mix_pre"][L], [128, KC], prm["g_mix_pre"])
            prenorm_blocks(P, [(xT_src, tb, tb * TB, TB, tb * TB) for tb in range(NTB)], gpre, hT)
            hb = hT.all()
            g1T = c.sbuf(es0, "gl_g1T", [16, T], BF16)
            Wg1 = prm["b_w_gate1"]

            def epi_g1(m, bi, pb, n):
                c.op("act", lambda e: e.activation(out=g1T[0:16, bi * TB:(bi + 1) * TB], in_=pb[0:16, 0:n], func=AF.Copy),
                     reads=[pb.b()], writes=[g1T.b()])
            proj_fm(P, hT, hb, lambda c0, n: Wg1[j, :, c0:c0 + n], Wg1, 16, std_blocks(0), epi_g1, wname="gl_w1")
            if GLA_STOP[0] == 1:
                c.barrier()
                return
            with ExitStack() as es:
                wg2 = c.sbuf(es, "gl_wg2", [16, 1024], BF16)
                c.dma("pool", wg2[:], prm["b_w_gate2"][j], reads=[prm["b_w_gate2"].b()], writes=[wg2.b()])
                gb = P.load_small(es, "gl_gb", prm["b_gb_bc"][j], [128, 1024], prm["b_gb_bc"])
                tri2 = P.load_small(es, "gl_tri2", prm["tri2"][:, :], [128, 128], prm["tri2"])
                u2 = P.load_small(es, "gl_u2", prm["u2"][:, :], [128, 128], prm["u2"])
                lab = [c.sbuf(es, f"gl_la{i}", [128, 1024], F32) for i in range(2)]
                ecpb = [c.sbuf(es, f"gl_ecp{i}", [128, 1024], F32) for i in range(2)]
                ecmb = [c.sbuf(es, f"gl_ecm{i}", [128, 1024], F32) for i in range(2)]
                eeb = [c.sbuf(es, f"gl_ee{i}", [128, 1024], F32) for i in range(2)]
                cumsb = [c.sbuf(es, f"gl_cums{i}", [128, 512], F32) for i in range(2)]
                for ti in range(16):
                    la, ecp, ecm, ee = lab[ti % 2], ecpb[ti % 2], ecmb[ti % 2], eeb[ti % 2]
                    for b in range(2):
                        pk = P.ps[b]
                        c.mm(pk[:], [(g1T[0:16, ti * 128:(ti + 1) * 128], wg2[0:16, b * 512:(b + 1) * 512])], pk.b(), [g1T.b(), wg2.b()])
                        c.op("dve", lambda e, la=la, pk=pk, b=b: e.tensor_tensor(out=la[:, b * 512:(b + 1) * 512], in0=pk[:], in1=gb[:, b * 512:(b + 1) * 512], op=ALU.add),
                             reads=[pk.b(), gb.b()], writes=[la.b()])
                    c.op("act", lambda e, la=la: e.activation(out=la[:], in_=la[:], func=AF.Exp, scale=-1.0), reads=[la.b()], writes=[la.b()])
                    c.op("act", lambda e, la=la: e.activation(out=la[:], in_=la[:], func=AF.Ln, bias=1.0, scale=1.0), reads=[la.b()], writes=[la.b()])
                    _chk(c, 20)
                    c.op("dve", lambda e, la=la: e.tensor_scalar(la[:], la[:], -1.0 / 16.0, None, ALU.mult), reads=[la.b()], writes=[la.b()])
                    _chk(c, 21)
                    for half in range(2):
                        pc = P.ps[2 + half]
                        for q in range(4):
                            dc = half * 4 + q
                            c.mm(pc[:, q * 128:(q + 1) * 128], [(la[:, dc * 128:(dc + 1) * 128], tri2[:])], pc.b(), [la.b(), tri2.b()])
                        c.op("act", lambda e, ecp=ecp, pc=pc, half=half: e.activation(out=ecp[:, half * 512:(half + 1) * 512], in_=pc[:], func=AF.Exp),
                             reads=[pc.b()], writes=[ecp.b()])
                        c.op("act", lambda e, ecm=ecm, pc=pc, half=half: e.activation(out=ecm[:, half * 512:(half + 1) * 512], in_=pc[:], func=AF.Exp, scale=-1.0),
                             reads=[pc.b()], writes=[ecm.b()])
                        _chk(c, 22)
                        cums = cumsb[half]
                        c.op("act", lambda e, cums=cums, pc=pc: e.activation(out=cums[:], in_=pc[:], func=AF.Copy), reads=[pc.b()], writes=[cums.b()])
                        for q in range(4):
                            dc = half * 4 + q
                            for cc in range(2):
                                n = ti * 2 + cc
                                col = q * 128 + cc * 64 + 63
                                c.op("dve", lambda e, col=col, n=n, dc=dc, cums=cums: e.tensor_copy(lastT[:, n * 8 + dc:n * 8 + dc + 1], cums[:, col:col + 1]),
                                     reads=[cums.b()], writes=[lastT.b()])
                    _chk(c, 23)
                    for dc in range(8):
                        c.dma("sp", EcpD[dc * 128:(dc + 1) * 128, ti * 128:(ti + 1) * 128], ecp[:, dc * 128:(dc + 1) * 128], reads=[ecp.b()], writes=[EcpD.b()])
                        c.dma("sp", EcmD[dc * 128:(dc + 1) * 128, ti * 128:(ti + 1) * 128], ecm[:, dc * 128:(dc + 1) * 128], reads=[ecm.b()], writes=[EcmD.b()])
                    _chk(c, 24)
                    for b in range(2):
                        pe = P.ps[4 + b]
                        c.mm(pe[:], [(u2[:], la[:, b * 512:(b + 1) * 512])], pe.b(), [la.b(), u2.b()])
                        c.op("act", lambda e, ee=ee, pe=pe, b=b: e.activation(out=ee[:, b * 512:(b + 1) * 512], in_=pe[:], func=AF.Exp),
                             reads=[pe.b()], writes=[ee.b()])
                    c.dma("sp", EendD[ti * 128:(ti + 1) * 128, :], ee[:], reads=[ee.b()], writes=[EendD.b()])
            c.barrier()
            if GLA_STOP[0] == 2:
                return
            c.op("act", lambda e: e.activation(out=EL[:], in_=lastT[:], func=AF.Exp), reads=[lastT.b()], writes=[EL.b()])
            c.op("dve", lambda e: e.memset(PF[:, 0:8], 0.0), writes=[PF.b()])
            for n in range(1, 32):
                c.op("dve", lambda e, n=n: e.tensor_tensor(out=PF[:, n * 8:(n + 1) * 8], in0=PF[:, (n - 1) * 8:n * 8], in1=lastT[:, (n - 1) * 8:n * 8], op=ALU.add),
                     reads=[PF.b(), lastT.b()], writes=[PF.b()])
            c.op("act", lambda e: e.activation(out=EP[:], in_=PF[:], func=AF.Exp), reads=[PF.b()], writes=[EP.b()])
            lam = c.sbuf(es0, "gl_lam", [128, 8], F32)
            c.op("dve", lambda e: e.tensor_tensor(out=lam[:], in0=PF[:, 248:256], in1=lastT[:, 248:256], op=ALU.add),
                 reads=[PF.b(), lastT.b()], writes=[lam.b()])
            c.dma("sp", Lam[:, :], lam[:], reads=[lam.b()], writes=[Lam.b()])
            with ExitStack() as es:
                ecs = [c.sbuf(es, f"gl_ecs{i}", [128, TB], F32) for i in range(3)]
                qfb = [c.sbuf(es, f"gl_qf{i}", [128, TB], F32) for i in range(2)]
                stg = [c.sbuf(es, f"gl_stg{i}", [128, TB], BF16) for i in range(4)]
                cnt = [0]

                def epi_q(m, bi, pb, n):
                    k = cnt[0]
                    cnt[0] += 1
                    ec, qf, st, st2 = ecs[k % 3], qfb[k % 2], stg[(2 * k) % 4], stg[(2 * k + 1) % 4]
                    c.dma("sp", ec[:], EcpD[m * 128:(m + 1) * 128, bi * TB:(bi + 1) * TB], reads=EcpD.all(), writes=[ec.b()])
                    c.op("dve", lambda e: e.scalar_tensor_tensor(out=qf[:], in0=pb[:, 0:n], scalar=0.0625, in1=ec[:], op0=ALU.mult, op1=ALU.mult),
                         reads=[pb.b(), ec.b()], writes=[qf.b()])
                    c.op("act", lambda e: e.activation(out=st[:], in_=qf[:], func=AF.Copy), reads=[qf.b()], writes=[st.b()])
                    c.dma("sp", qdT[m * 128:(m + 1) * 128, bi * TB:(bi + 1) * TB], st[:], reads=[st.b()], writes=[qdT.b()])
                    for cc in range(8):
                        nn = bi * 8 + cc
                        c.op("dve", lambda e, cc=cc, nn=nn: e.tensor_scalar(st2[:, cc * 64:(cc + 1) * 64], qf[:, cc * 64:(cc + 1) * 64],
                                                                            EP[:, nn * 8 + m:nn * 8 + m + 1], None, ALU.mult),
                             reads=[qf.b(), EP.b()], writes=[st2.b()])
                    c.dma("sp", qtil[m * 128:(m + 1) * 128, bi * TB:(bi + 1) * TB], st2[:], reads=[st2.b()], writes=[qtil.b()])
                proj_fm(P, hT, hb, lambda c0, n: Wq[j, :, c0:c0 + n], Wq, 1024, std_blocks(0), epi_q, wname="gl_wq")

                def epi_k(m, bi, pb, n):
                    k = cnt[0]
                    cnt[0] += 1
                    ec, st = ecs[k % 3], stg[k % 4]
                    c.dma("sp", ec[:], EcmD[m * 128:(m + 1) * 128, bi * TB:(bi + 1) * TB], reads=EcmD.all(), writes=[ec.b()])
                    c.op("dve", lambda e: e.tensor_tensor(out=st[:], in0=pb[:, 0:n], in1=ec[:], op=ALU.mult), reads=[pb.b(), ec.b()], writes=[st.b()])
                    c.dma("sp", kinvT[m * 128:(m + 1) * 128, bi * TB:(bi + 1) * TB], st[:], reads=[st.b()], writes=[kinvT.b()])
                proj_fm(P, hT, hb, lambda c0, n: Wq[j, :, 1024 + c0:1024 + c0 + n], Wq, 1024, std_blocks(0), epi_k, wname="gl_wk")

                def epi_kt(nb, ti, pb, n):
                    k = cnt[0]
                    cnt[0] += 1
                    ec, st = ecs[k % 3], stg[k % 4]
                    c.dma("sp", ec[:], EendD[ti * 128:(ti + 1) * 128, nb * 512:(nb + 1) * 512], reads=EendD.all(), writes=[ec.b()])
                    c.op("dve", lambda e: e.tensor_tensor(out=st[:], in0=pb[:, 0:n], in1=ec[:], op=ALU.mult), reads=[pb.b(), ec.b()], writes=[st.b()])
                    c.dma("sp", kendD[ti * 128:(ti + 1) * 128, nb * 512:(nb + 1) * 512], st[:], reads=[st.b()], writes=[kendD.b()])
                tiles = [ti * 128 for ti in range(16)]
                proj_tm(P, hT, hb, lambda c0, n: Wq[j, :, 1024 + c0:1024 + c0 + n], Wq, 1024, tiles, epi_kt, wname="gl_wkt")

                def epi_v(nb, ti, pb, n):
                    k = cnt[0]
                    cnt[0] += 1
                    st = stg[k % 4]
                    c.op("act", lambda e: e.activation(out=st[:], in_=pb[:, 0:n], func=AF.Copy), reads=[pb.b()], writes=[st.b()])
                    c.dma("sp", vD[ti * 128:(ti + 1) * 128, nb * 512:(nb + 1) * 512], st[:], reads=[st.b()], writes=[vD.b()])
                proj_tm(P, hT, hb, lambda c0, n: Wq[j, :, 2048 + c0:2048 + c0 + n], Wq, 2048, tiles, epi_v, wname="gl_wv")

                def epi_r(nb, ti, pb, n):
                    k = cnt[0]
                    cnt[0] += 1
                    st = stg[k % 4]
                    c.op("act", lambda e: e.activation(out=st[:], in_=pb[:, 0:n], func=AF.Silu), reads=[pb.b()], writes=[st.b()])
                    c.dma("sp", sr[ti * 128:(ti + 1) * 128, nb * 512:(nb + 1) * 512], st[:], reads=[st.b()], writes=[sr.b()])
                proj_tm(P, hT, hb, lambda c0, n: Wq[j, :, 4096 + c0:4096 + c0 + n], Wq, 2048, tiles, epi_r, wname="gl_wr")
        c.barrier()
        if GLA_STOP[0] == 3:
            return
        with ExitStack() as es:
            qs = c.sbuf(es, "gl_qs", [128, 8, T], BF16)
            ks = c.sbuf(es, "gl_ks", [128, 8, T], BF16)
            c.dma("sp", qs[:], qdT[:, :].rearrange("(m p) t -> p m t", p=128), reads=qdT.all(), writes=[qs.b()])
            c.dma("sp", ks[:], kinvT[:, :].rearrange("(m p) t -> p m t", p=128), reads=kinvT.all(), writes=[ks.b()])
            S = c.sbuf(es, "gl_S", [128, 8, 512], F32)
            Sb = c.sbuf(es, "gl_Sb", [128, 8, 512], BF16)
            for hd in range(8):
                c.op("dve", lambda e, hd=hd: e.memset(S[:, hd, :], 0.0), writes=[S.b(hd)])
                c.op("pool", lambda e, hd=hd: e.memset(Sb[:, hd, :], 0.0), writes=[Sb.b(hd)])
            cm = P.load_small(es, "gl_cm", prm["cmaskT"][:, :], [64, 64], prm["cmaskT"])
            vcb = [c.sbuf(es, f"gl_vc{i}", [64, 2048], BF16) for i in range(2)]
            kcb = [c.sbuf(es, f"gl_kc{i}", [64, 1024], BF16) for i in range(2)]
            osbb = [c.sbuf(es, f"gl_os{i}", [64, 2048], F32) for i in range(2)]
            attb = [c.sbuf(es, f"gl_att{i}", [64, 64], BF16) for i in range(2)]
            it = 0
            for n in range(32):
                vc, kc_, osb = vcb[n % 2], kcb[n % 2], osbb[n % 2]
                cs = slice(n * 64, (n + 1) * 64)
                c.dma("sp", vc[:], vD[n * 64:(n + 1) * 64, :], reads=vD.all(), writes=[vc.b()])
                c.dma("sp", kc_[:], kendD[n * 64:(n + 1) * 64, :], reads=kendD.all(), writes=[kc_.b()])
                for h in range(4):
                    pa, po, at = P.ps[it % 2], P.ps[2 + it % 2], attb[it % 2]
                    it += 1
                    c.mm(pa[0:64, 0:64], [(ks[:, h * 2 + dc, cs], qs[:, h * 2 + dc, cs]) for dc in range(2)], pa.b(), [ks.b(), qs.b()])
                    c.op("dve", lambda e, at=at, pa=pa: e.tensor_tensor(out=at[:], in0=pa[0:64, 0:64], in1=cm[:], op=ALU.mult),
                         reads=[pa.b(), cm.b()], writes=[at.b()])
                    c.mm(po[0:64, 0:512], [(at[:], vc[:, h * 512:(h + 1) * 512]),
                                           (qs[:, h * 2, cs], Sb[:, h * 2, :]), (qs[:, h * 2 + 1, cs], Sb[:, h * 2 + 1, :])],
                         po.b(), [at.b(), vc.b(), qs.b(), Sb.b(h * 2), Sb.b(h * 2 + 1)])
                    c.op("act", lambda e, osb=osb, po=po, h=h: e.activation(out=osb[:, h * 512:(h + 1) * 512], in_=po[0:64, 0:512], func=AF.Copy),
                         reads=[po.b()], writes=[osb.b()])
                    for dc in range(2):
                        hd = h * 2 + dc
                        pk = P.ps[4 + dc]
                        c.mm(pk[:], [(kc_[:, hd * 128:(hd + 1) * 128], vc[:, h * 512:(h + 1) * 512])], pk.b(), [kc_.b(), vc.b()])
                        c.op("dve", lambda e, hd=hd, pk=pk, n=n: e.scalar_tensor_tensor(
                            out=S[:, hd, :], in0=S[:, hd, :], scalar=EL[:, n * 8 + hd:n * 8 + hd + 1], in1=pk[:], op0=ALU.mult, op1=ALU.add),
                            reads=[S.b(hd), pk.b(), EL.b()], writes=[S.b(hd)])
                        c.op("act", lambda e, hd=hd: e.activation(out=Sb[:, hd, :], in_=S[:, hd, :], func=AF.Copy),
                             reads=[S.b(hd)], writes=[Sb.b(hd)])
                c.dma("sp", o_loc[n * 64:(n + 1) * 64, :], osb[:], reads=[osb.b()], writes=[o_loc.b()])
            c.dma("sp", Send[:, :].rearrange("(m p) e -> p m e", p=128), S[:], reads=[S.b(hd) for hd in range(8)], writes=[Send.b()])
    c.barrier()


def gla_b(P, L, j, xT_src, xT_dst, prm, Send_all, Lam_all, cmask, o_loc, qtil, sr):
    c = P.c
    with ExitStack() as es1:
        zT = c.sbuf(es1, "gb_zT", [128, KC, T], BF16)
        gpost = P.load_small(es1, "gb_gpost", prm["g_mix_post"][L], [128, KC], prm["g_mix_post"])
        with ExitStack() as es:
            S = c.sbuf(es, "gb_S", [128, 8, 512], F32)
            Sb = c.sbuf(es, "gb_Sb", [128, 8, 512], BF16)
            for hd in range(8):
                c.op("dve", lambda e, hd=hd: e.memset(S[:, hd, :], 0.0), writes=[S.b(hd)])
            cmk = P.load_small(es, "gb_cm", cmask[:, :], [128, 8], cmask)
            onb = P.load_small(es, "gb_on", prm["b_on_bc"][j], [128, 512], prm["b_on_bc"])
            esA = ExitStack()
            Eb = [c.sbuf(esA, f"gb_E{i}", [128, 8, 512], F32) for i in range(2)]
            lam = c.sbuf(esA, "gb_lam", [128, 8], F32)
            Ap = c.sbuf(esA, "gb_Ap", [128, 8], F32)
            tmp = c.sbuf(esA, "gb_tmp", [128, 512], F32)
            for cp in range(7):
                E_ = Eb[cp % 2]
                c.dma("sp", E_[:], Send_all[cp].rearrange("(m p) e -> p m e", p=128), reads=[Send_all.b()], writes=[E_.b()])
                c.dma("sp", lam[:], Lam_all[cp], reads=[Lam_all.b()], writes=[lam.b()])
                c.op("act", lambda e: e.activation(out=Ap[:], in_=lam[:], func=AF.Exp), reads=[lam.b()], writes=[Ap.b()])
                c.op("dve", lambda e, cp=cp: e.tensor_scalar(Ap[:], Ap[:], -1.0, cmk[:, cp:cp + 1], ALU.add, ALU.mult), reads=[Ap.b(), cmk.b()], writes=[Ap.b()])
                c.op("dve", lambda e: e.tensor_scalar(Ap[:], Ap[:], 1.0, None, ALU.add), reads=[Ap.b()], writes=[Ap.b()])
                for hd in range(8):
                    c.op("dve", lambda e, hd=hd, cp=cp, E_=E_: e.tensor_scalar(tmp[:], E_[:, hd, :], cmk[:, cp:cp + 1], None, ALU.mult),
                         reads=[E_.b(), cmk.b()], writes=[tmp.b()])
                    c.op("dve", lambda e, hd=hd: e.scalar_tensor_tensor(out=S[:, hd, :], in0=S[:, hd, :], scalar=Ap[:, hd:hd + 1], in1=tmp[:],
                                                                        op0=ALU.mult, op1=ALU.add),
                         reads=[S.b(hd), Ap.b(), tmp.b()], writes=[S.b(hd)])
            for hd in range(8):
                c.op("act", lambda e, hd=hd: e.activation(out=Sb[:, hd, :], in_=S[:, hd, :], func=AF.Copy), reads=[S.b(hd)], writes=[Sb.b(hd)])
            c.barrier()
            esA.close()
            qt = c.sbuf(es, "gb_qt", [128, 8, T], BF16)
            c.dma("sp", qt[:], qtil[:, :].rearrange("(m p) t -> p m t", p=128), reads=[qtil.b()], writes=[qt.b()])
            olb = [c.sbuf(es, f"gb_ol{i}", [128, 2048], F32) for i in range(2)]
            srb = [c.sbuf(es, f"gb_sr{i}", [128, 2048], BF16) for i in range(2)]
            zbb = [c.sbuf(es, f"gb_zb{i}", [128, 2048], BF16) for i in range(2)]
            junk = c.sbuf(es, "gb_junk", [128, 512], F32)
            stt = [c.sbuf(es, f"gb_st{i}", [128, 8], F32) for i in range(2)]
            Sbb = [Sb.b(hd) for hd in range(8)]
            for ti in range(16):
                ol, srt, zb, st = olb[ti % 2], srb[ti % 2], zbb[ti % 2], stt[ti % 2]
                ts_ = slice(ti * 128, (ti + 1) * 128)
                c.dma("sp", ol[:], o_loc[ti * 128:(ti + 1) * 128, :], reads=[o_loc.b()], writes=[ol.b()])
                c.dma("sp", srt[:], sr[ti * 128:(ti + 1) * 128, :], reads=[sr.b()], writes=[srt.b()])
                for h in range(4):
                    pc = P.ps[h]
                    hs = slice(h * 512, (h + 1) * 512)
                    c.mm(pc[:], [(qt[:, h * 2 + dc, ts_], Sb[:, h * 2 + dc, :]) for dc in range(2)], pc.b(), [qt.b()] + Sbb)
                    c.op("dve", lambda e, ol=ol, pc=pc, hs=hs: e.tensor_tensor(out=ol[:, hs], in0=ol[:, hs], in1=pc[:], op=ALU.add),
                         reads=[ol.b(), pc.b()], writes=[ol.b()])
                    c.op("act", lambda e, ol=ol, hs=hs, st=st, h=h: e.activation(out=junk[:], in_=ol[:, hs], func=AF.Square, accum_out=st[:, h:h + 1]),
                         reads=[ol.b()], writes=[junk.b(), st.b()])
                c.op("dve", lambda e, st=st: e.tensor_scalar(st[:, 4:8], st[:, 0:4], 1.0 / 512, EPS, ALU.mult, ALU.add), reads=[st.b()], writes=[st.b()])
                c.op("act", lambda e, st=st: e.activation(out=st[:, 4:8], in_=st[:, 4:8], func=AF.Sqrt), reads=[st.b()], writes=[st.b()])
                c.op("dve", lambda e, st=st: e.reciprocal(st[:, 4:8], st[:, 4:8]), reads=[st.b()], writes=[st.b()])
                for h in range(4):
                    hs = slice(h * 512, (h + 1) * 512)
                    c.op("dve", lambda e, ol=ol, hs=hs, st=st, h=h: e.scalar_tensor_tensor(
                        out=ol[:, hs], in0=ol[:, hs], scalar=st[:, 4 + h:5 + h], in1=onb[:], op0=ALU.mult, op1=ALU.mult),
                        reads=[ol.b(), st.b(), onb.b()], writes=[ol.b()])
                c.op("pool", lambda e, ol=ol, srt=srt, zb=zb: e.tensor_tensor(out=zb[:], in0=ol[:], in1=srt[:], op=ALU.mult),
                     reads=[ol.b(), srt.b()], writes=[zb.b()])
                for q in range(2):
                    pb = P.psb[q]
                    for r in range(8):
                        kc = q * 8 + r
                        c.transpose(pb[:, r * 128:(r + 1) * 128], zb[:, kc * 128:(kc + 1) * 128], P.ident[:], pb.b(), [zb.b(), P.ident.b()])
                    for r in range(8):
                        kc = q * 8 + r
                        c.op("act", lambda e, pb=pb, r=r, kc=kc, ts_=ts_: e.activation(out=zT[:, kc, ts_], in_=pb[:, r * 128:(r + 1) * 128], func=AF.Copy),
                             reads=[pb.b()], writes=[zT.b()])
        c.barrier()

        def zt_get(tb):
            return (lambda kc: zT[:, kc, tb * TB:(tb + 1) * TB]), zT.all()
        final_proj(P, zt_get, KC, prm["b_w_o"][j], prm["b_w_o"], None, gpost, xT_src, xT_dst)


import ml_dtypes
from concourse.bass_utils import run_bass_kernel_spmd

NCORES = 8


def mix_x(P, mixer_fn, xT, xo, prm):
    if DBG["skip_mixer"]:
        xattn(P, 0, xT, xo, prm)
    elif DBG["skip_xattn"]:
        mixer_fn(xo)
    else:
        x1 = P.scratch("x1", [2048, T], F32)
        mixer_fn(x1)
        xattn(P, 0, x1, xo, prm)


def t5_bucket_np(dist):
    import math
    n = np.maximum(dist, 0)
    large = 16 + (np.log(np.maximum(n, 1) / 16) / math.log(128 / 16) * (32 - 16)).astype(np.int32)
    large = np.minimum(large, 31)
    return np.where(n < 16, n, large).astype(np.int32)


def bc128(v):
    v = np.asarray(v, np.float32).reshape(1, -1)
    return np.ascontiguousarray(np.broadcast_to(v, (128, v.shape[1])))


def declare(P, specs):
    return {k: P.inp(k, list(shape), dt) for k, (shape, dt) in specs.items()}


def run(P, maps):
    res = run_bass_kernel_spmd(P.nc, maps, core_ids=list(range(NCORES)))
    return res.results


def xattn_specs():
    return {"g_xattn_pre": ((1, 128, 16), F32), "g_xattn_post": ((1, 128, 16), F32), "g_mem": ((1, 128, 16), F32),
            "memT": ((2048, 256), F32), "x_w_q": ((1, 2048, 512), F32), "x_w_kv": ((1, 2048, 1024), F32),
            "x_w_o": ((1, 512, 2048), F32)}


def xattn_vals(inp, L):
    return {"g_xattn_pre": colfmt(inp["norm_xattn_pre"][L])[None], "g_xattn_post": colfmt(inp["norm_xattn_post"][L])[None],
            "g_mem": colfmt(inp["norm_mem"][L])[None], "memT": np.ascontiguousarray(inp["mem"][0].T),
            "x_w_q": inp["x_w_q"][L:L + 1], "x_w_kv": inp["x_w_kv"][L:L + 1], "x_w_o": inp["x_w_o"][L:L + 1]}


def mix_norm_specs():
    return {"g_mix_pre": ((1, 128, 16), F32), "g_mix_post": ((1, 128, 16), F32)}


def mix_norm_vals(inp, L):
    return {"g_mix_pre": colfmt(inp["norm_mix_pre"][L])[None], "g_mix_post": colfmt(inp["norm_mix_post"][L])[None]}


def halos(xTs, H):
    out = []
    for cix in range(NCORES):
        if cix == 0:
            out.append(np.zeros((2048, H), np.float32))
        else:
            out.append(np.ascontiguousarray(xTs[cix - 1][:, T - H:]))
    return out


def launch_ffn(inp, L, xTs):
    P = Prog()
    specs = {"f_w_gate_up": ((1, 2048, 16384), F32), "f_w_down": ((1, 8192, 2048), F32), "fcw": ((1, 128, 192), F32),
             "fcb": ((1, 128, 64), F32), "g_ffn_pre": ((1, 128, 16), F32), "g_ffn_post": ((1, 128, 16), F32)}
    prm = declare(P, specs)
    xT = P.inp("xT", [2048, T])
    halo = P.inp("halo", [2048, 2])
    xo = P.out("xo", [2048, T])
    ffn(P, 0, xT, xo, halo, prm)
    P.c.finish()
    fcw = np.stack([colfmt(inp["f_w_conv"][L, k]) for k in range(3)], axis=2).reshape(128, 192)[None]
    common = {"ident_in": np.eye(128, dtype=np.float32), "f_w_gate_up": inp["f_w_gate_up"][L:L + 1], "f_w_down": inp["f_w_down"][L:L + 1],
              "fcw": np.ascontiguousarray(fcw), "fcb": colfmt(inp["f_b_conv"][L])[None],
              "g_ffn_pre": colfmt(inp["norm_ffn_pre"][L])[None], "g_ffn_post": colfmt(inp["norm_ffn_post"][L])[None]}
    hl = halos(xTs, 2)
    maps = [dict(common, xT=xTs[cix], halo=hl[cix]) for cix in range(NCORES)]
    return [r["xo"] for r in run(P, maps)]


def launch_swa(inp, L, j, xTs):
    P = Prog()
    specs = dict(mix_norm_specs(), **xattn_specs())
    specs.update({"a_w_qkv": ((1, 2048, 2560), F32), "a_w_o": ((1, 2048, 2048), F32), "rel_bc": ((128, 1024), F32),
                  "sink_bc": ((1, 128, 32), F32), "maskadd": ((128, 256), F32), "halo_mask": ((128, 256), F32),
                  "Eoh": ((32, 128, 256), F32)})
    prm = declare(P, specs)
    xT = P.inp("xT", [2048, T])
    halo = P.inp("halo", [2048, 128])
    xo = P.out("xo", [2048, T])
    mix_x(P, lambda dst: swa(P, 0, 0, xT, dst, halo, prm), xT, xo, prm)
    P.c.finish()
    qi = np.arange(128)[:, None]
    kj = np.arange(256)[None, :]
    dist = qi + 128 - kj
    inw = (dist >= 0) & (dist < 128)
    bk = t5_bucket_np(dist)
    maskadd = np.where(inw, 0.0, -30000.0).astype(np.float32)
    Eoh = np.stack([((bk == b) & inw).astype(np.float32) for b in range(32)])
    common = dict(mix_norm_vals(inp, L), **xattn_vals(inp, L))
    common.update({"ident_in": np.eye(128, dtype=np.float32), "a_w_qkv": inp["a_w_qkv"][j:j + 1], "a_w_o": inp["a_w_o"][j:j + 1],
                   "rel_bc": bc128(inp["rel_bias_table"].reshape(-1)), "sink_bc": bc128(inp["a_sinks"][j])[None],
                   "maskadd": maskadd, "Eoh": Eoh})
    hl = halos(xTs, 128)
    maps = []
    for cix in range(NCORES):
        hm = np.zeros((128, 256), np.float32)
        if cix == 0:
            hm[:, 0:128] = -30000.0
        maps.append(dict(common, xT=xTs[cix], halo=hl[cix], halo_mask=hm))
    return [r["xo"] for r in run(P, maps)]


def launch_conf(inp, L, j, xTs):
    P = Prog()
    specs = dict(mix_norm_specs(), **xattn_specs())
    specs.update({"c_w_pw1": ((1, 2048, 4096), F32), "c_b_pw1c": ((1, 128, 32), F32), "c_w_dwc": ((1, 128, 496), F32),
                  "c_b_dwc": ((1, 128, 16), F32), "c_ln_gc": ((1, 128, 16), F32), "c_ln_bc": ((1, 128, 16), F32),
                  "c_w_pw2": ((1, 2048, 2048), F32), "c_b_pw2c": ((1, 128, 16), F32), "hv": ((128, 1), F32)})
    prm = declare(P, specs)
    xT = P.inp("xT", [2048, T])
    halo = P.inp("halo", [2048, 32])
    xo = P.out("xo", [2048, T])
    mix_x(P, lambda dst: conformer(P, 0, 0, xT, dst, halo, prm), xT, xo, prm)
    P.c.finish()
    wdw = np.ascontiguousarray(inp["c_w_dw"][j].reshape(31, 16, 128).transpose(2, 1, 0)).reshape(128, 496)
    common = dict(mix_norm_vals(inp, L), **xattn_vals(inp, L))
    common.update({"ident_in": np.eye(128, dtype=np.float32), "c_w_pw1": inp["c_w_pw1"][j:j + 1], "c_b_pw1c": colfmt(inp["c_b_pw1"][j])[None],
                   "c_w_dwc": wdw[None], "c_b_dwc": colfmt(inp["c_b_dw"][j])[None], "c_ln_gc": colfmt(inp["c_ln_g"][j])[None],
                   "c_ln_bc": colfmt(inp["c_ln_b"][j])[None], "c_w_pw2": inp["c_w_pw2"][j:j + 1], "c_b_pw2c": colfmt(inp["c_b_pw2"][j])[None]})
    hl = halos(xTs, 32)
    maps = [dict(common, xT=xTs[cix], halo=hl[cix], hv=np.full((128, 1), 0.0 if cix == 0 else 1.0, np.float32)) for cix in range(NCORES)]
    return [r["xo"] for r in run(P, maps)]


def launch_gmlp(inp, L, j, xTs):
    P = Prog()
    specs = dict(mix_norm_specs(), **xattn_specs())
    specs.update({"d_w_in": ((1, 2048, 8192), F32), "d_b_inc": ((1, 128, 64), F32), "d_ln_gc": ((1, 128, 32), F32),
                  "d_ln_bc": ((1, 128, 32), F32), "d_w_sT": ((1, 128, 1024), F32), "triT": ((128, 128), F32),
                  "d_b_sbc": ((1, 128, 1024), F32), "d_w_out": ((1, 4096, 2048), F32)})
    prm = declare(P, specs)
    xT = P.inp("xT", [2048, T])
    xo = P.out("xo", [2048, T])
    mix_x(P, lambda dst: gmlp(P, 0, 0, xT, dst, prm), xT, xo, prm)
    P.c.finish()
    wsT = np.ascontiguousarray(inp["d_w_s"][j].transpose(2, 0, 1)).reshape(128, 1024)
    triT = (np.arange(128)[:, None] <= np.arange(128)[None, :]).astype(np.float32)
    common = dict(mix_norm_vals(inp, L), **xattn_vals(inp, L))
    common.update({"ident_in": np.eye(128, dtype=np.float32), "d_w_in": inp["d_w_in"][j:j + 1], "d_b_inc": colfmt(inp["d_b_in"][j])[None],
                   "d_ln_gc": colfmt(inp["d_ln_g"][j])[None], "d_ln_bc": colfmt(inp["d_ln_b"][j])[None], "d_w_sT": wsT[None],
                   "triT": triT, "d_b_sbc": bc128(inp["d_b_s"][j].reshape(-1))[None], "d_w_out": inp["d_w_out"][j:j + 1]})
    maps = [dict(common, xT=xTs[cix]) for cix in range(NCORES)]
    return [r["xo"] for r in run(P, maps)]


def launch_gla(inp, L, j, xTs):
    P = Prog()
    specs = dict(mix_norm_specs())
    specs.update({"b_w_qkvr": ((1, 2048, 6144), F32), "b_w_gate1": ((1, 2048, 16), F32), "b_w_gate2": ((1, 16, 1024), F32),
                  "b_gb_bc": ((1, 128, 1024), F32), "tri2": ((128, 128), F32), "u2": ((128, 128), F32), "cmaskT": ((64, 64), F32)})
    prm = declare(P, specs)
    xT = P.inp("xT", [2048, T])
    o_loc = P.out("o_loc", [T, 2048], F32)
    qtil = P.out("qtil", [1024, T], BF16)
    sr = P.out("sr", [T, 2048], BF16)
    Send = P.out("Send", [1024, 512], F32)
    Lam = P.out("Lam", [128, 8], F32)
    gla_a(P, 0, 0, xT, prm, o_loc, qtil, sr, Send, Lam)
    P.c.finish()
    s_ = np.arange(128)[:, None]
    t_ = np.arange(128)[None, :]
    same = (s_ // 64) == (t_ // 64)
    tri2 = (same & (s_ <= t_)).astype(np.float32)
    u2 = (same & (s_ > t_)).astype(np.float32)
    cmT = (np.arange(64)[:, None] <= np.arange(64)[None, :]).astype(np.float32)
    common = dict(mix_norm_vals(inp, L))
    common.update({"ident_in": np.eye(128, dtype=np.float32), "b_w_qkvr": inp["b_w_qkvr"][j:j + 1], "b_w_gate1": inp["b_w_gate1"][j:j + 1],
                   "b_w_gate2": inp["b_w_gate2"][j:j + 1], "b_gb_bc": bc128(inp["b_gate_bias"][j])[None], "tri2": tri2, "u2": u2, "cmaskT": cmT})
    ra = run(P, [dict(common, xT=xTs[cix]) for cix in range(NCORES)])
    Send_all = np.stack([np.asarray(r["Send"], np.float32) for r in ra])
    Lam_all = np.stack([np.asarray(r["Lam"], np.float32) for r in ra])
    P = Prog()
    specs = dict(mix_norm_specs(), **xattn_specs())
    specs.update({"b_on_bc": ((1, 128, 512), F32), "b_w_o": ((1, 2048, 2048), F32)})
    prm = declare(P, specs)
    xT = P.inp("xT", [2048, T])
    Sa = P.inp("Send_all", [8, 1024, 512])
    La = P.inp("Lam_all", [8, 128, 8])
    cmk = P.inp("cmask", [128, 8])
    o_loc = P.inp("o_loc", [T, 2048])
    qtil = P.inp("qtil", [1024, T], BF16)
    sr = P.inp("sr", [T, 2048], BF16)
    xo = P.out("xo", [2048, T])
    mix_x(P, lambda dst: gla_b(P, 0, 0, xT, dst, prm, Sa, La, cmk, o_loc, qtil, sr), xT, xo, prm)
    P.c.finish()
    common = dict(mix_norm_vals(inp, L), **xattn_vals(inp, L))
    common.update({"ident_in": np.eye(128, dtype=np.float32), "b_on_bc": bc128(inp["b_o_norm"][j])[None], "b_w_o": inp["b_w_o"][j:j + 1],
                   "Send_all": Send_all, "Lam_all": Lam_all})
    maps = []
    for cix in range(NCORES):
        cm = np.zeros((128, 8), np.float32)
        cm[:, :cix] = 1.0
        maps.append(dict(common, xT=xTs[cix], cmask=cm, o_loc=ra[cix]["o_loc"], qtil=ra[cix]["qtil"], sr=ra[cix]["sr"]))
    return [r["xo"] for r in run(P, maps)]


def kernel(**inputs):
    inp = {k: np.asarray(v) for k, v in inputs.items()}
    x = np.asarray(inp["x"], np.float32)[0]
    xTs = [np.ascontiguousarray(x[cix * T:(cix + 1) * T].T) for cix in range(NCORES)]
    for L in range(4):
        kind, j = L % 4, L // 4
        if kind == 0:
            xTs = launch_swa(inp, L, j, xTs)
        elif kind == 1:
            xTs = launch_gla(inp, L, j, xTs)
        elif kind == 2:
            xTs = launch_conf(inp, L, j, xTs)
        else:
            xTs = launch_gmlp(inp, L, j, xTs)
        xTs = [np.asarray(a, np.float32) for a in xTs]
        xTs = launch_ffn(inp, L, xTs)
        xTs = [np.asarray(a, np.float32) for a in xTs]
    out = np.concatenate([a.T for a in xTs], axis=0)[None]
    return np.ascontiguousarray(out.astype(np.float32))
```

```python
from contextlib import ExitStack
import numpy as np
import concourse.bass as bass
import concourse.mybir as mybir

F32 = mybir.dt.float32
BF16 = mybir.dt.bfloat16
AF = mybir.ActivationFunctionType
ALU = mybir.AluOpType
AX = mybir.AxisListType
NDS = 8
DBG = {"skip_xattn": False, "skip_mixer": False, "dump": False}


class Buf:
    __slots__ = ("w", "r")

    def __init__(self):
        self.w = None
        self.r = {}


class TT:
    def __init__(self, t):
        self.t = t
        self.bufs = {}

    def b(self, key=None):
        if key not in self.bufs:
            self.bufs[key] = Buf()
        return self.bufs[key]

    def all(self):
        return list(self.bufs.values())

    def __getitem__(self, idx):
        return self.t[idx]


class Eng:
    def __init__(self, name, h, sem):
        self.name, self.h, self.sem = name, h, sem
        self.count = 0
        self.seen = {}
        self.dsems = []
        self.duses = [0] * NDS
        self.dn = 0


class Ctx:
    def __init__(self, nc):
        self.nc = nc
        self.es = ExitStack()
        self.sems = {}
        self.eng = {}
        for name, h in (("pe", nc.tensor), ("act", nc.scalar), ("dve", nc.vector),
                        ("pool", nc.gpsimd), ("sp", nc.sync)):
            s = self.es.enter_context(nc.semaphore("s_" + name))
            self.eng[name] = Eng(name, h, s)
            self.sems[name] = s
        for q in ("sp", "pool", "act"):
            E = self.eng[q]
            for i in range(NDS):
                s = self.es.enter_context(nc.semaphore(f"d_{q}{i}"))
                E.dsems.append(s)
                self.sems[("d", q, i)] = s
        self.ninstr = 0
        self.cc_sem = self.es.enter_context(nc.semaphore("s_cc"))
        self.sems["cc"] = self.cc_sem
        self.cc_count = 0

    def sbuf(self, es, name, shape, dt):
        self.uid = getattr(self, "uid", 0) + 1
        return TT(es.enter_context(self.nc.sbuf_tensor(f"{name}_u{self.uid}", list(shape), dt)))

    def psum(self, es, name, shape, dt):
        return TT(es.enter_context(self.nc.psum_tensor(name, list(shape), dt)))

    def dram(self, name, shape, dt, kind="Internal"):
        return TT(self.nc.dram_tensor(name, list(shape), dt, kind=kind).ap())

    def _wait(self, E, key, val):
        if val <= 0:
            return
        if E.seen.get(key, 0) >= val:
            return
        E.h.wait_ge(self.sems[key], val)
        E.seen[key] = val
        self.ninstr += 1

    def _sync(self, E, reads, writes):
        deps = {}
        for b in reads:
            if b.w is not None:
                k, v = b.w
                if deps.get(k, 0) < v:
                    deps[k] = v
        for b in writes:
            if b.w is not None:
                k, v = b.w
                if deps.get(k, 0) < v:
                    deps[k] = v
            for k, v in b.r.items():
                if deps.get(k, 0) < v:
                    deps[k] = v
        for k, v in deps.items():
            if k == "pe" and E.name == "pe":
                continue
            self._wait(E, k, v)

    def _mark(self, tok, reads, writes):
        k, v = tok
        for b in reads:
            if b.r.get(k, 0) < v:
                b.r[k] = v
        for b in writes:
            b.w = tok
            b.r = {}

    def op(self, eng, fn, reads=(), writes=()):
        E = self.eng[eng]
        self._sync(E, reads, writes)
        ins = fn(E.h)
        E.count += 1
        ins.then_inc(E.sem, 1)
        self.ninstr += 1
        self._mark((eng, E.count), reads, writes)

    def mm(self, out_ap, pairs, out_buf, read_bufs, start=True, stop=True):
        E = self.eng["pe"]
        self._sync(E, read_bufs, [out_buf])
        n = len(pairs)
        ins = None
        for i, (l, r) in enumerate(pairs):
            ins = self.nc.tensor.matmul(out_ap, l, r, start=(start and i == 0), stop=(stop and i == n - 1))
        E.count += 1
        ins.then_inc(E.sem, 1)
        self.ninstr += n
        self._mark(("pe", E.count), read_bufs, [out_buf])

    def transpose(self, out_ap, in_ap, ident_ap, out_buf, read_bufs):
        E = self.eng["pe"]
        self._sync(E, read_bufs, [out_buf])
        ins = self.nc.tensor.transpose(out_ap, in_ap, ident_ap)
        E.count += 1
        ins.then_inc(E.sem, 1)
        self.ninstr += 1
        self._mark(("pe", E.count), read_bufs, [out_buf])

    def dma(self, q, out, in_, reads=(), writes=()):
        E = self.eng[q]
        idx = E.dn % NDS
        E.dn += 1
        key = ("d", q, idx)
        self._wait(E, key, E.duses[idx] * 16)
        self._sync(E, reads, writes)
        E.h.dma_start(out=out, in_=in_).then_inc(E.dsems[idx], 16)
        E.duses[idx] += 1
        self.ninstr += 1
        self._mark((key, E.duses[idx] * 16), reads, writes)

    def allgather(self, in_tt, out_tt):
        E = self.eng["pool"]
        self._sync(E, in_tt.all() or [in_tt.b()], out_tt.all() or [out_tt.b()])
        ins = self.nc.gpsimd.collective_compute("AllGather", ALU.bypass, replica_groups=[list(range(8))],
                                                ins=[in_tt.t.opt()], outs=[out_tt.t.opt()])
        ins.then_inc(self.cc_sem)
        self.cc_count += 1
        self.ninstr += 1
        self._mark(("cc", self.cc_count), in_tt.all() or [in_tt.b()], out_tt.all() or [out_tt.b()])

    def barrier(self):
        for E in self.eng.values():
            for A in self.eng.values():
                if A is not E:
                    self._wait(E, A.name, A.count)
            for q in ("sp", "pool", "act"):
                Q = self.eng[q]
                for i in range(NDS):
                    self._wait(E, ("d", q, i), Q.duses[i] * 16)
            self._wait(E, "cc", self.cc_count)

    def finish(self):
        self.barrier()


def colfmt(v):
    v = np.asarray(v, np.float32)
    return np.ascontiguousarray(v.reshape(-1, 128).T)

T = 2048
TB = 512
NTB = 4
D = 2048
KC = 16
EPS = 1e-6


class Prog:
    def __init__(self):
        self.nc = bass.Bass("TRN2", target_bir_lowering=False)
        self.c = Ctx(self.nc)
        c = self.c
        self.in_names = []
        self.out_names = []
        self.ps = [c.psum(c.es, f"ps{i}", [128, 512], F32) for i in range(6)]
        self.psb = [c.psum(c.es, f"psb{i}", [128, 1024], BF16) for i in range(2)]
        self.ones = c.sbuf(c.es, "ones", [128, 128], BF16)
        c.op("dve", lambda e: e.memset(self.ones[:], 1.0), writes=[self.ones.b()])
        self.identf = c.sbuf(c.es, "identf", [128, 128], F32)
        self.ident = c.sbuf(c.es, "ident", [128, 128], BF16)
        idd = self.inp("ident_in", [128, 128])
        c.dma("sp", self.identf[:], idd[:], reads=[idd.b()], writes=[self.identf.b()])
        c.op("dve", lambda e: e.tensor_copy(self.ident[:], self.identf[:]), reads=[self.identf.b()], writes=[self.ident.b()])

    def inp(self, name, shape, dt=F32):
        self.in_names.append(name)
        return self.c.dram(name, shape, dt, kind="ExternalInput")

    def out(self, name, shape, dt=F32):
        self.out_names.append(name)
        return self.c.dram(name, shape, dt, kind="ExternalOutput")

    def dump(self, name, tt, shape, dt):
        if not DBG.get("dump"):
            return
        o = self.out("dbg_" + name, shape, dt)
        self.c.dma("sp", o[:], tt[:], reads=tt.all() or [tt.b()], writes=[o.b()])

    def scratch(self, name, shape, dt):
        self.suid = getattr(self, "suid", 0) + 1
        return self.c.dram(f"{name}_s{self.suid}", shape, dt)

    def load_small(self, es, name, dram_ap, shape, src_tt):
        c = self.c
        t = c.sbuf(es, name, shape, F32)
        c.dma("sp", t[:], dram_ap, reads=[src_tt.b()], writes=[t.b()])
        return t


def rstd_from_stat(c, rstd, ps_ap, n, dim, rbuf, psbuf):
    c.op("dve", lambda e: e.tensor_scalar(rstd[:, 0:n], ps_ap, 1.0 / dim, EPS, ALU.mult, ALU.add),
         reads=[psbuf], writes=[rbuf])
    c.op("act", lambda e: e.activation(out=rstd[:, 0:n], in_=rstd[:, 0:n], func=AF.Sqrt),
         reads=[rbuf], writes=[rbuf])
    c.op("dve", lambda e: e.reciprocal(rstd[:, 0:n], rstd[:, 0:n]),
         reads=[rbuf], writes=[rbuf])


def prenorm(P, xT, halo, H, gcol, hT):
    c = P.c
    with ExitStack() as es:
        xsb2 = [c.sbuf(es, f"pn_xs{i}", [128, KC, TB], F32) for i in range(2)]
        sq = c.sbuf(es, "pn_sq", [128, KC, TB], BF16)
        rstd = c.sbuf(es, "pn_rstd", [128, TB], F32)
        blocks = []
        if H:
            blocks.append((halo, None, 0, H, 0))
        for tb in range(NTB):
            blocks.append((xT, tb, tb * TB, TB, H + tb * TB))
        for bi, (src, key, s0, n, c0) in enumerate(blocks):
            xs = xsb2[bi % 2]
            c.dma("sp", xs[:, :, 0:n], src[:, s0:s0 + n].rearrange("(m p) t -> p m t", p=128),
                  reads=[src.b(key)], writes=[xs.b()])
            c.op("act", lambda e: e.activation(out=sq[:, :, 0:n], in_=xs[:, :, 0:n], func=AF.Square),
                 reads=[xs.b()], writes=[sq.b()])
            pst = P.ps[4 + bi % 2]
            c.mm(pst[:, 0:n], [(P.ones[:], sq[:, kc, 0:n]) for kc in range(KC)], pst.b(), [sq.b(), P.ones.b()])
            rstd_from_stat(c, rstd, pst[:, 0:n], n, D, rstd.b(), pst.b())
            for kc in range(KC):
                eng = "dve"
                c.op(eng, lambda e, kc=kc: e.scalar_tensor_tensor(
                    out=hT[:, kc, c0:c0 + n], in0=xs[:, kc, 0:n], scalar=gcol[:, kc:kc + 1], in1=rstd[:, 0:n],
                    op0=ALU.mult, op1=ALU.mult),
                    reads=[xs.b(), rstd.b(), gcol.b()], writes=[hT.b(("tb", bi))])
    c.barrier()


def hT_bufs(hT):
    return hT.all()


def final_proj(P, zt_get, KCz, W, Wtt, bias_col, gcol, xT_src, xT_dst, Wb16=None):
    c = P.c
    ncols = min(512, 8192 // KCz)
    with ExitStack() as es:
        NW = 3 if KCz >= 64 else 2
        wbufs = [c.sbuf(es, f"fp_w{i}", [128, KCz, ncols], BF16) for i in range(NW)]
        ysb = c.sbuf(es, "fp_y", [128, KC, TB], F32)
        xsb = c.sbuf(es, "fp_x", [128, KC, TB], F32)
        ysq = [c.sbuf(es, f"fp_ysq{i}", [128, TB], BF16) for i in range(2)]
        rstd = c.sbuf(es, "fp_rstd", [128, TB], F32)
        wi = 0
        for tb in range(NTB):
            zfn, zbufs = zt_get(tb)
            pst = P.ps[4 + tb % 2]
            for cg in range(D // ncols):
                wb = wbufs[wi % NW]
                wi += 1
                if Wb16 is not None:
                    assert ncols == 128
                    c.dma("sp", wb[:], Wb16[cg].rearrange("p (kc n) -> p kc n", n=128), reads=[Wb16.b(cg)], writes=[wb.b()])
                else:
                    c.dma("pool", wb[:], W[:, cg * ncols:(cg + 1) * ncols].rearrange("(kc p) n -> p kc n", p=128),
                          reads=[Wtt.b()], writes=[wb.b()])
                for mm_ in range(ncols // 128):
                    m = cg * (ncols // 128) + mm_
                    pb = P.ps[m % 4]
                    c.mm(pb[:], [(wb[:, kc, mm_ * 128:(mm_ + 1) * 128], zfn(kc)) for kc in range(KCz)],
                         pb.b(), [wb.b()] + zbufs)
                    if bias_col is not None:
                        c.op("act", lambda e, m=m, pb=pb: e.activation(out=ysb[:, m, :], in_=pb[:], func=AF.Identity,
                                                                       bias=bias_col[:, m:m + 1], scale=1.0),
                             reads=[pb.b(), bias_col.b()], writes=[ysb.b(m)])
                        sqin, sqb = ysb[:, m, :], [ysb.b(m)]
                    else:
                        c.op("act", lambda e, m=m, pb=pb: e.activation(out=ysb[:, m, :], in_=pb[:], func=AF.Copy),
                             reads=[pb.b()], writes=[ysb.b(m)])
                        sqin, sqb = pb[:], [pb.b()]
                    yq = ysq[m % 2]
                    c.op("act", lambda e, yq=yq, sqin=sqin: e.activation(out=yq[:], in_=sqin, func=AF.Square),
                         reads=sqb, writes=[yq.b()])
                    c.mm(pst[:], [(P.ones[:], yq[:])], pst.b(), [yq.b(), P.ones.b()], start=(m == 0), stop=(m == KC - 1))
            rstd_from_stat(c, rstd, pst[:], TB, D, rstd.b(), pst.b())
            c.dma("sp", xsb[:], xT_src[:, tb * TB:(tb + 1) * TB].rearrange("(m p) t -> p m t", p=128),
                  reads=[xT_src.b(tb)], writes=[xsb.b()])
            for m in range(KC):
                c.op("dve", lambda e, m=m: e.scalar_tensor_tensor(
                    out=ysb[:, m, :], in0=ysb[:, m, :], scalar=gcol[:, m:m + 1], in1=rstd[:], op0=ALU.mult, op1=ALU.mult),
                    reads=[ysb.b(m), rstd.b(), gcol.b()], writes=[ysb.b(m)])
                c.op("pool", lambda e, m=m: e.tensor_tensor(out=ysb[:, m, :], in0=ysb[:, m, :], in1=xsb[:, m, :], op=ALU.add),
                     reads=[ysb.b(m), xsb.b()], writes=[ysb.b(m)])
            c.dma("sp", xT_dst[:, tb * TB:(tb + 1) * TB].rearrange("(m p) t -> p m t", p=128), ysb[:],
                  reads=[ysb.b(m) for m in range(KC)], writes=[xT_dst.b(tb)])
    c.barrier()


def ffn(P, L, xT_src, xT_dst, halo, prm):
    c = P.c
    H = 2
    hid = P.scratch(f"hid{L}", [8192, T], BF16)
    wdb = P.scratch("wdb", [16, 128, 64 * 128], BF16)
    with ExitStack() as es0:
        hT = c.sbuf(es0, "ffn_hT", [128, KC, H + T], BF16)
        gpre = P.load_small(es0, "ffn_gpre", prm["g_ffn_pre"][L], [128, KC], prm["g_ffn_pre"])
        gpost = P.load_small(es0, "ffn_gpost", prm["g_ffn_post"][L], [128, KC], prm["g_ffn_post"])
        fcw = P.load_small(es0, "ffn_cw", prm["fcw"][L], [128, 64 * 3], prm["fcw"])
        fcb = P.load_small(es0, "ffn_cb", prm["fcb"][L], [128, 64], prm["fcb"])
        prenorm(P, xT_src, halo, H, gpre, hT)
        Wgu = prm["f_w_gate_up"]
        with ExitStack() as es:
            wg = [c.sbuf(es, f"ffn_wg{i}", [128, KC, 512], BF16) for i in range(2)]
            wu = [c.sbuf(es, f"ffn_wu{i}", [128, KC, 512], BF16) for i in range(2)]
            gsb = [c.sbuf(es, f"ffn_gsb{i}", [128, H + T], F32) for i in range(2)]
            cv = [c.sbuf(es, f"ffn_cv{i}", [128, TB], F32) for i in range(2)]
            ge = [c.sbuf(es, f"ffn_ge{i}", [128, TB], F32) for i in range(2)]
            hsb = [c.sbuf(es, f"ffn_h{i}", [128, T], BF16) for i in range(2)]
            stgw = c.sbuf(es, "ffn_stgw", [128, 32, 128], BF16)
            hb = hT.all()
            it = 0
            for jg in range(16):
                g_, u_ = wg[jg % 2], wu[jg % 2]
                c.dma("pool", g_[:], Wgu[L, :, jg * 512:(jg + 1) * 512].rearrange("(kc p) n -> p kc n", p=128),
                      reads=[Wgu.b()], writes=[g_.b()])
                c.dma("pool", u_[:], Wgu[L, :, 8192 + jg * 512:8192 + (jg + 1) * 512].rearrange("(kc p) n -> p kc n", p=128),
                      reads=[Wgu.b()], writes=[u_.b()])
                for hh in range(2):
                    c.dma("pool", stgw[:], prm["f_w_down"][L, hh * 4096:(hh + 1) * 4096, jg * 128:(jg + 1) * 128].rearrange("(kc p) n -> p kc n", p=128),
                          reads=[prm["f_w_down"].b()], writes=[stgw.b()])
                    c.dma("sp", wdb[jg][:, hh * 4096:(hh + 1) * 4096].rearrange("p (kc n) -> p kc n", n=128), stgw[:],
                          reads=[stgw.b()], writes=[wdb.b(jg)])
                for jj in range(4):
                    j = jg * 4 + jj
                    gs, hs = gsb[j % 2], hsb[j % 2]
                    ph = P.ps[4]
                    c.mm(ph[:, 0:H], [(g_[:, kc, jj * 128:(jj + 1) * 128], hT[:, kc, 0:H]) for kc in range(KC)],
                         ph.b(), [g_.b()] + hb)
                    c.op("act", lambda e, gs=gs, ph=ph: e.activation(out=gs[:, 0:H], in_=ph[:, 0:H], func=AF.Copy),
                         reads=[ph.b()], writes=[gs.b("h")])
                    for tb in range(NTB):
                        pg, pu = P.ps[it % 2], P.ps[2 + it % 2]
                        cvt, get = cv[it % 2], ge[it % 2]
                        it += 1
                        c0 = H + tb * TB
                        c.mm(pg[:], [(g_[:, kc, jj * 128:(jj + 1) * 128], hT[:, kc, c0:c0 + TB]) for kc in range(KC)],
                             pg.b(), [g_.b()] + hb)
                        c.mm(pu[:], [(u_[:, kc, jj * 128:(jj + 1) * 128], hT[:, kc, c0:c0 + TB]) for kc in range(KC)],
                             pu.b(), [u_.b()] + hb)
                        c.op("act", lambda e, gs=gs, pg=pg, c0=c0: e.activation(out=gs[:, c0:c0 + TB], in_=pg[:], func=AF.Copy),
                             reads=[pg.b()], writes=[gs.b(tb)])
                        prev = gs.b("h") if tb == 0 else gs.b(tb - 1)
                        s = tb * TB
                        c.op("act", lambda e, gs=gs, cvt=cvt, s=s, j=j: e.activation(
                            out=cvt[:], in_=gs[:, s:s + TB], func=AF.Identity, bias=fcb[:, j:j + 1], scale=fcw[:, j * 3:j * 3 + 1]),
                            reads=[gs.b(tb), prev, fcw.b(), fcb.b()], writes=[cvt.b()])
                        for k in (1, 2):
                            c.op("dve", lambda e, gs=gs, cvt=cvt, s=s, j=j, k=k: e.scalar_tensor_tensor(
                                out=cvt[:], in0=gs[:, s + k:s + k + TB], scalar=fcw[:, j * 3 + k:j * 3 + k + 1], in1=cvt[:],
                                op0=ALU.mult, op1=ALU.add),
                                reads=[gs.b(tb), prev, cvt.b(), fcw.b()], writes=[cvt.b()])
                        c.op("act", lambda e, cvt=cvt, get=get: e.activation(out=get[:], in_=cvt[:], func=AF.Gelu_apprx_tanh),
                             reads=[cvt.b()], writes=[get.b()])
                        c.op("dve", lambda e, get=get, pu=pu, hs=hs, s=s: e.tensor_tensor(
                            out=hs[:, s:s + TB], in0=get[:], in1=pu[:], op=ALU.mult),
                            reads=[get.b(), pu.b()], writes=[hs.b()])
                    c.dma("sp", hid[j * 128:(j + 1) * 128, :], hs[:], reads=[hs.b()], writes=[hid.b(j)])
        c.barrier()
    with ExitStack() as es:
        zt = c.sbuf(es, "ffn_zt", [128, 64, TB], BF16)
        gpost = P.load_small(es, "ffn_gpost2", prm["g_ffn_post"][L], [128, KC], prm["g_ffn_post"])

        def zt_get(tb):
            c.dma("sp", zt[:], hid[:, tb * TB:(tb + 1) * TB].rearrange("(j p) t -> p j t", p=128),
                  reads=hid.all(), writes=[zt.b()])
            return (lambda kc: zt[:, kc, :]), [zt.b()]

        final_proj(P, zt_get, 64, prm["f_w_down"][L], prm["f_w_down"], None, gpost, xT_src, xT_dst, Wb16=wdb)


def std_blocks(H):
    bl = []
    if H:
        bl.append((0, H))
    for tb in range(NTB):
        bl.append((H + tb * TB, TB))
    return bl


def proj_fm(P, hT, hbufs, W_ap_fn, Wtt, ncols, blocks, epi, wname="pfw", kc_n=KC, psl=(0, 1, 2, 3)):
    c = P.c
    with ExitStack() as es:
        wb = [c.sbuf(es, f"{wname}{i}", [128, kc_n, 512], BF16) for i in range(2)]
        it = 0
        for cg in range((ncols + 511) // 512):
            w = wb[cg % 2]
            nc_ = min(512, ncols - cg * 512)
            c.dma("pool", w[:, :, 0:nc_], W_ap_fn(cg * 512, nc_).rearrange("(kc p) n -> p kc n", p=128),
                  reads=[Wtt.b()], writes=[w.b()])
            for mm_ in range((nc_ + 127) // 128):
                m = cg * 4 + mm_
                mw = min(128, nc_ - mm_ * 128)
                for bi, (c0, n) in enumerate(blocks):
                    pb = P.ps[psl[it % len(psl)]]
                    it += 1
                    c.mm(pb[0:mw, 0:n], [(w[:, kc, mm_ * 128:mm_ * 128 + mw], hT[:, kc, c0:c0 + n]) for kc in range(kc_n)],
                         pb.b(), [w.b()] + hbufs)
                    epi(m, bi, pb, n)
        c.barrier()


def proj_tm(P, hT, hbufs, W_ap_fn, Wtt, ncols, tiles, epi, wname="ptw", kc_n=KC, psl=(0, 1, 2, 3)):
    c = P.c
    with ExitStack() as es:
        wb = [c.sbuf(es, f"{wname}{i}", [128, kc_n, 512], BF16) for i in range(2)]
        it = 0
        for nb in range((ncols + 511) // 512):
            w = wb[nb % 2]
            nc_ = min(512, ncols - nb * 512)
            c.dma("pool", w[:, :, 0:nc_], W_ap_fn(nb * 512, nc_).rearrange("(kc p) n -> p kc n", p=128),
                  reads=[Wtt.b()], writes=[w.b()])
            for ti, c0 in enumerate(tiles):
                pb = P.ps[psl[it % len(psl)]]
                it += 1
                c.mm(pb[:, 0:nc_], [(hT[:, kc, c0:c0 + 128], w[:, kc, 0:nc_]) for kc in range(kc_n)],
                     pb.b(), [w.b()] + hbufs)
                epi(nb, ti, pb, nc_)
        c.barrier()


def prenorm_blocks(P, blocks, gcol, hT, nkc=KC):
    c = P.c
    with ExitStack() as es:
        xsb2 = [c.sbuf(es, f"pn_xs{i}", [128, KC, TB], F32) for i in range(2)]
        sq = c.sbuf(es, "pn_sq", [128, KC, TB], BF16)
        rstd = c.sbuf(es, "pn_rstd", [128, TB], F32)
        for bi, (src, key, s0, n, c0) in enumerate(blocks):
            xs = xsb2[bi % 2]
            c.dma("sp", xs[:, :, 0:n], src[:, s0:s0 + n].rearrange("(m p) t -> p m t", p=128),
                  reads=[src.b(key)], writes=[xs.b()])
            c.op("act", lambda e: e.activation(out=sq[:, :, 0:n], in_=xs[:, :, 0:n], func=AF.Square),
                 reads=[xs.b()], writes=[sq.b()])
            pst = P.ps[4 + bi % 2]
            c.mm(pst[:, 0:n], [(P.ones[:], sq[:, kc, 0:n]) for kc in range(KC)], pst.b(), [sq.b(), P.ones.b()])
            rstd_from_stat(c, rstd, pst[:, 0:n], n, D, rstd.b(), pst.b())
            for kc in range(KC):
                c.op("dve", lambda e, kc=kc: e.scalar_tensor_tensor(
                    out=hT[:, kc, c0:c0 + n], in0=xs[:, kc, 0:n], scalar=gcol[:, kc:kc + 1], in1=rstd[:, 0:n],
                    op0=ALU.mult, op1=ALU.mult),
                    reads=[xs.b(), rstd.b(), gcol.b()], writes=[hT.b(("tb", bi))])
    c.barrier()


class AttnScratch:
    def __init__(self, P, es):
        c = P.c
        self.s = [c.sbuf(es, f"at_s{i}", [128, 256], F32) for i in range(3)]
        self.p = [c.sbuf(es, f"at_p{i}", [128, 256], F32) for i in range(3)]
        self.pn = [c.sbuf(es, f"at_pn{i}", [128, 256], BF16) for i in range(3)]
        self.pT = [c.sbuf(es, f"at_pT{i}", [128, 2, 128], BF16) for i in range(3)]
        self.st = [c.sbuf(es, f"at_st{i}", [128, 8], F32) for i in range(3)]
        self.it = 0
        self.pending = None


def attn_p1(P, S, q_ap, k_ap, qk_bufs, v_aps, v_bufs, scale, bias_ap, extra_ap, sink_ap, cbufs, out_fn):
    c = P.c
    k = S.it
    S.it += 1
    s, p, st = S.s[k % 3], S.p[k % 3], S.st[k % 3]
    pl = P.ps[k % 2]
    c.mm(pl[:, 0:256], [(q_ap, k_ap)], pl.b(), qk_bufs)
    if bias_ap is not None:
        c.op("dve", lambda e: e.scalar_tensor_tensor(out=s[:], in0=pl[:, 0:256], scalar=scale, in1=bias_ap,
                                                     op0=ALU.mult, op1=ALU.add),
             reads=[pl.b()] + cbufs, writes=[s.b()])
        if extra_ap is not None:
            c.op("pool", lambda e: e.tensor_tensor(out=s[:], in0=s[:], in1=extra_ap, op=ALU.add),
                 reads=[s.b()] + cbufs, writes=[s.b()])
        src, srcb, esc = s[:], s.b(), 1.0
    else:
        src, srcb, esc = pl[:, 0:256], pl.b(), scale
    c.op("dve", lambda e: e.reduce_max(out=st[:, 0:1], in_=src, axis=AX.X), reads=[srcb], writes=[st.b()])
    if sink_ap is not None:
        c.op("dve", lambda e: e.tensor_tensor(out=st[:, 0:1], in0=st[:, 0:1], in1=sink_ap, op=ALU.max),
             reads=[st.b()] + cbufs, writes=[st.b()])
    c.op("dve", lambda e: e.tensor_scalar(st[:, 1:2], st[:, 0:1], -esc, None, ALU.mult), reads=[st.b()], writes=[st.b()])
    c.op("act", lambda e: e.activation(out=p[:], in_=src, func=AF.Exp, bias=st[:, 1:2], scale=esc, accum_out=st[:, 2:3]),
         reads=[srcb, st.b()], writes=[p.b(), st.b()])
    if sink_ap is not None:
        c.op("act", lambda e: e.activation(out=st[:, 3:4], in_=st[:, 1:2], func=AF.Exp, bias=sink_ap, scale=1.0),
             reads=[st.b()] + cbufs, writes=[st.b()])
    return (k, sink_ap is not None, v_aps, v_bufs, out_fn)


def attn_p2(P, S, ctx):
    c = P.c
    k, has_sink, v_aps, v_bufs, out_fn = ctx
    p, pn, pT, st = S.p[k % 3], S.pn[k % 3], S.pT[k % 3], S.st[k % 3]
    if has_sink:
        c.op("dve", lambda e: e.tensor_tensor(out=st[:, 2:3], in0=st[:, 2:3], in1=st[:, 3:4], op=ALU.add),
             reads=[st.b()], writes=[st.b()])
    c.op("dve", lambda e: e.reciprocal(st[:, 4:5], st[:, 2:3]), reads=[st.b()], writes=[st.b()])
    c.op("dve", lambda e: e.tensor_scalar(pn[:], p[:], st[:, 4:5], None, ALU.mult), reads=[p.b(), st.b()], writes=[pn.b()])
    pb = P.psb[k % 2]
    for kt in range(2):
        c.transpose(pb[:, kt * 128:(kt + 1) * 128], pn[:, kt * 128:(kt + 1) * 128], P.ident[:], pb.b(), [pn.b(), P.ident.b()])
    c.op("act", lambda e: e.activation(out=pT[:, 0, :], in_=pb[:, 0:128], func=AF.Copy), reads=[pb.b()], writes=[pT.b()])
    c.op("act", lambda e: e.activation(out=pT[:, 1, :], in_=pb[:, 128:256], func=AF.Copy), reads=[pb.b()], writes=[pT.b()])
    po = P.ps[2 + k % 2]
    c.mm(po[:, 0:128], [(v_aps[0], pT[:, 0, :]), (v_aps[1], pT[:, 1, :])], po.b(), [pT.b()] + v_bufs)
    out_fn(po)


def attn_core(P, S, *args):
    ctx = attn_p1(P, S, *args)
    if S.pending is not None:
        attn_p2(P, S, S.pending)
    S.pending = ctx


def attn_flush(P, S):
    if S.pending is not None:
        attn_p2(P, S, S.pending)
        S.pending = None


def xattn(P, L, xT_src, xT_dst, prm):
    c = P.c
    with ExitStack() as es1:
        qT = c.sbuf(es1, "xa_qT", [128, 4, T], BF16)
        kT = c.sbuf(es1, "xa_kT", [128, 4, 256], BF16)
        vm = c.sbuf(es1, "xa_v", [128, 2, 512], BF16)
        oT = c.sbuf(es1, "xa_oT", [128, 4, T], BF16)
        gpost = P.load_small(es1, "xa_gpost", prm["g_xattn_post"][L], [128, KC], prm["g_xattn_post"])
        with ExitStack() as es0:
            hT = c.sbuf(es0, "xa_hT", [128, KC, T], BF16)
            mT = c.sbuf(es0, "xa_mT", [128, KC, 256], BF16)
            gpre = P.load_small(es0, "xa_gpre", prm["g_xattn_pre"][L], [128, KC], prm["g_xattn_pre"])
            gmem = P.load_small(es0, "xa_gmem", prm["g_mem"][L], [128, KC], prm["g_mem"])
            prenorm_blocks(P, [(xT_src, tb, tb * TB, TB, tb * TB) for tb in range(NTB)], gpre, hT)
            prenorm_blocks(P, [(prm["memT"], None, 0, 256, 0)], gmem, mT)
            Wq, Wkv = prm["x_w_q"], prm["x_w_kv"]

            def epi_q(m, bi, pb, n):
                c.op("act", lambda e: e.activation(out=qT[:, m, bi * TB:(bi + 1) * TB], in_=pb[:, 0:n], func=AF.Copy),
                     reads=[pb.b()], writes=[qT.b()])
            proj_fm(P, hT, hT.all(), lambda c0, n: Wq[L, :, c0:c0 + n], Wq, 512, std_blocks(0), epi_q)

            def epi_k(m, bi, pb, n):
                c.op("act", lambda e: e.activation(out=kT[:, m, :], in_=pb[:, 0:n], func=AF.Copy),
                     reads=[pb.b()], writes=[kT.b()])
            proj_fm(P, mT, mT.all(), lambda c0, n: Wkv[L, :, c0:c0 + n], Wkv, 512, [(0, 256)], epi_k)

            def epi_v(nb, ti, pb, n):
                c.op("act", lambda e: e.activation(out=vm[:, ti, :], in_=pb[:, 0:n], func=AF.Copy),
                     reads=[pb.b()], writes=[vm.b()])
            proj_tm(P, mT, mT.all(), lambda c0, n: Wkv[L, :, 512 + c0:512 + c0 + n], Wkv, 512, [0, 128], epi_v)
            c.barrier()
            P.dump("hT", hT, [128, KC, T], BF16)
            P.dump("mT", mT, [128, KC, 256], BF16)
            c.barrier()
        c.barrier()
        with ExitStack() as es2:
            S = AttnScratch(P, es2)
            for i in range(T // 128):
                for h in range(4):
                    def out_fn(po, i=i, h=h):
                        c.op("act", lambda e: e.activation(out=oT[:, h, i * 128:(i + 1) * 128], in_=po[:, 0:128], func=AF.Copy),
                             reads=[po.b()], writes=[oT.b((h, i))])
                    attn_core(P, S, qT[:, h, i * 128:(i + 1) * 128], kT[:, h, :], [qT.b(), kT.b()],
                              [vm[:, 0, h * 128:(h + 1) * 128], vm[:, 1, h * 128:(h + 1) * 128]], [vm.b()],
                              128 ** -0.5, None, None, None, [], out_fn)
            attn_flush(P, S)
        c.barrier()
        P.dump("qT", qT, [128, 4, T], BF16)
        P.dump("kT", kT, [128, 4, 256], BF16)
        P.dump("vm", vm, [128, 2, 512], BF16)
        P.dump("oT", oT, [128, 4, T], BF16)
        c.barrier()

        def zt_get(tb):
            return (lambda kc: oT[:, kc, tb * TB:(tb + 1) * TB]), oT.all()
        final_proj(P, zt_get, 4, prm["x_w_o"][L], prm["x_w_o"], None, gpost, xT_src, xT_dst)


def swa(P, L, j, xT_src, xT_dst, halo, prm):
    c = P.c
    H = 128
    Wqkv = prm["a_w_qkv"]
    qd = P.scratch(f"swa_qd{L}", [2048, T], BF16)
    with ExitStack() as es1:
        kT2 = c.sbuf(es1, "sw_kT2", [128, 4, H + T], BF16)
        vdup = c.sbuf(es1, "sw_vdup", [128, 17, 512], BF16)
        gpost = P.load_small(es1, "sw_gpost", prm["g_mix_post"][L], [128, KC], prm["g_mix_post"])
        with ExitStack() as es0:
            hT = c.sbuf(es0, "sw_hT", [128, KC, H + T], BF16)
            gpre = P.load_small(es0, "sw_gpre", prm["g_mix_pre"][L], [128, KC], prm["g_mix_pre"])
            prenorm_blocks(P, [(halo, None, 0, H, 0)] + [(xT_src, tb, tb * TB, TB, H + tb * TB) for tb in range(NTB)], gpre, hT)
            hb = hT.all()
            with ExitStack() as es:
                stg = [c.sbuf(es, f"sw_stg{i}", [128, TB], BF16) for i in range(4)]
                cnt = [0]

                def epi_q(m, bi, pb, n):
                    st = stg[cnt[0] % 4]
                    cnt[0] += 1
                    c.op("act", lambda e: e.activation(out=st[:], in_=pb[:, 0:n], func=AF.Copy), reads=[pb.b()], writes=[st.b()])
                    c.dma("sp", qd[m * 128:(m + 1) * 128, bi * TB:(bi + 1) * TB], st[:], reads=[st.b()], writes=[qd.b()])
                proj_fm(P, hT, hb, lambda c0, n: Wqkv[j, :, c0:c0 + n], Wqkv, 2048, std_blocks(0)[0:0] + [(H + tb * TB, TB) for tb in range(NTB)], epi_q)
                wkd = c.sbuf(es, "sw_wkd", [128, KC, 512], BF16)
                wvd = c.sbuf(es, "sw_wvd", [128, KC, 512], BF16)
                for g in range(4):
                    for dup in range(2):
                        o = g * 128 + dup * 64
                        c.dma("pool", wkd[:, :, o:o + 64], Wqkv[j, :, 2048 + g * 64:2048 + (g + 1) * 64].rearrange("(kc p) n -> p kc n", p=128),
                              reads=[Wqkv.b()], writes=[wkd.b()])
                        c.dma("pool", wvd[:, :, o:o + 64], Wqkv[j, :, 2304 + g * 64:2304 + (g + 1) * 64].rearrange("(kc p) n -> p kc n", p=128),
                              reads=[Wqkv.b()], writes=[wvd.b()])
                it = 0
                for g in range(4):
                    for (c0, n) in std_blocks(H):
                        pb = P.ps[it % 4]
                        it += 1
                        c.mm(pb[:, 0:n], [(wkd[:, kc, g * 128:(g + 1) * 128], hT[:, kc, c0:c0 + n]) for kc in range(KC)], pb.b(), [wkd.b()] + hb)
                        c.op("act", lambda e, pb=pb, g=g, c0=c0, n=n: e.activation(out=kT2[:, g, c0:c0 + n], in_=pb[:, 0:n], func=AF.Copy),
                             reads=[pb.b()], writes=[kT2.b()])
                for ti in range(17):
                    pb = P.ps[it % 4]
                    it += 1
                    c.mm(pb[:], [(hT[:, kc, ti * 128:(ti + 1) * 128], wvd[:, kc, :]) for kc in range(KC)], pb.b(), [wvd.b()] + hb)
                    c.op("act", lambda e, pb=pb, ti=ti: e.activation(out=vdup[:, ti, :], in_=pb[:], func=AF.Copy),
                         reads=[pb.b()], writes=[vdup.b()])
            c.barrier()
        qT = c.sbuf(es1, "sw_qT", [128, KC, T], BF16)
        c.dma("sp", qT[:], qd[:, :].rearrange("(m p) t -> p m t", p=128), reads=qd.all(), writes=[qT.b()])
        with ExitStack() as es2:
            biasT = c.sbuf(es2, "sw_bias", [128, 32, 256], F32)
            tab = P.load_small(es2, "sw_tab", prm["rel_bc"][:, :], [128, 1024], prm["rel_bc"])
            sink = P.load_small(es2, "sw_sink", prm["sink_bc"][j], [128, 32], prm["sink_bc"])
            madd = P.load_small(es2, "sw_madd", prm["maskadd"][:, :], [128, 256], prm["maskadd"])
            hm = P.load_small(es2, "sw_hm", prm["halo_mask"][:, :], [128, 256], prm["halo_mask"])
            eb = [c.sbuf(es2, f"sw_eb{i}", [128, 256], F32) for i in range(2)]
            for h in range(32):
                c.op("pool", lambda e, h=h: e.tensor_copy(biasT[:, h, :], madd[:]), reads=[madd.b()], writes=[biasT.b(h)])
            Eoh = prm["Eoh"]
            for b in range(32):
                e_ = eb[b % 2]
                c.dma("sp", e_[:], Eoh[b], reads=[Eoh.b()], writes=[e_.b()])
                for h in range(32):
                    c.op("dve", lambda e, h=h, b=b, e_=e_: e.scalar_tensor_tensor(
                        out=biasT[:, h, :], in0=e_[:], scalar=tab[:, b * 32 + h:b * 32 + h + 1], in1=biasT[:, h, :],
                        op0=ALU.mult, op1=ALU.add), reads=[e_.b(), tab.b(), biasT.b(h)], writes=[biasT.b(h)])
            S = AttnScratch(P, es2)
            for i in range(T // 128):
                for h in range(32):
                    g, hp, tl = h // 8, (h % 2) * 64, h // 2

                    def out_fn(po, i=i, hp=hp, tl=tl, h=h):
                        c.op("act", lambda e: e.activation(out=qT[hp:hp + 64, tl, i * 128:(i + 1) * 128], in_=po[hp:hp + 64, 0:128], func=AF.Copy),
                             reads=[po.b()], writes=[qT.b((h, i))])
                    attn_core(P, S, qT[hp:hp + 64, tl, i * 128:(i + 1) * 128], kT2[hp:hp + 64, g, i * 128:(i + 2) * 128],
                              [qT.b(), qT.b((h, i)), kT2.b()],
                              [vdup[:, i, g * 128:(g + 1) * 128], vdup[:, i + 1, g * 128:(g + 1) * 128]], [vdup.b()],
                              0.125, biasT[:, h, :], (hm[:] if i == 0 else None), sink[:, h:h + 1],
                              [biasT.b(h), hm.b(), sink.b()], out_fn)
            attn_flush(P, S)
        c.barrier()

        def zt_get(tb):
            return (lambda kc: qT[:, kc, tb * TB:(tb + 1) * TB]), qT.all()
        final_proj(P, zt_get, KC, prm["a_w_o"][j], prm["a_w_o"], None, gpost, xT_src, xT_dst)


def stat_accum(P, src_ap, src_buf, tb, s1, s2, first, tmp):
    c = P.c
    a, b = tmp
    c.op("act", lambda e: e.activation(out=a[:], in_=src_ap, func=AF.Copy), reads=[src_buf], writes=[a.b()])
    c.op("act", lambda e: e.activation(out=b[:], in_=src_ap, func=AF.Square), reads=[src_buf], writes=[b.b()])
    for (st, t_, pi) in ((s1, a, 4), (s2, b, 5)):
        pb = P.ps[pi]
        c.mm(pb[:], [(P.ones[:], t_[:])], pb.b(), [t_.b(), P.ones.b()])
        sl = st[:, tb * TB:(tb + 1) * TB]
        if first:
            c.op("dve", lambda e, sl=sl, pb=pb: e.tensor_copy(sl, pb[:]), reads=[pb.b()], writes=[st.b(tb)])
        else:
            c.op("dve", lambda e, sl=sl, pb=pb: e.tensor_tensor(out=sl, in0=sl, in1=pb[:], op=ALU.add),
                 reads=[pb.b(), st.b(tb)], writes=[st.b(tb)])


def ln_stats(P, s1, s2, tb, dim, mean, rstd, tmp):
    c = P.c
    sl1, sl2 = s1[:, tb * TB:(tb + 1) * TB], s2[:, tb * TB:(tb + 1) * TB]
    c.op("dve", lambda e: e.tensor_scalar(mean[:], sl1, 1.0 / dim, None, ALU.mult), reads=[s1.b(tb)], writes=[mean.b()])
    c.op("dve", lambda e: e.tensor_tensor(out=tmp[:], in0=mean[:], in1=mean[:], op=ALU.mult), reads=[mean.b()], writes=[tmp.b()])
    c.op("dve", lambda e: e.scalar_tensor_tensor(out=rstd[:], in0=sl2, scalar=1.0 / dim, in1=tmp[:], op0=ALU.mult, op1=ALU.subtract),
         reads=[s2.b(tb), tmp.b()], writes=[rstd.b()])
    c.op("dve", lambda e: e.tensor_scalar(rstd[:], rstd[:], 0.0, EPS, ALU.max, ALU.add), reads=[rstd.b()], writes=[rstd.b()])
    c.op("act", lambda e: e.activation(out=rstd[:], in_=rstd[:], func=AF.Sqrt), reads=[rstd.b()], writes=[rstd.b()])
    c.op("dve", lambda e: e.reciprocal(rstd[:], rstd[:]), reads=[rstd.b()], writes=[rstd.b()])


def conformer(P, L, j, xT_src, xT_dst, halo, prm):
    c = P.c
    H = 32
    W1 = prm["c_w_pw1"]
    cvd = P.scratch(f"cf_cvd{L}", [2048, T], F32)
    with ExitStack() as es1:
        s1 = c.sbuf(es1, "cf_s1", [128, T], F32)
        s2 = c.sbuf(es1, "cf_s2", [128, T], F32)
        with ExitStack() as es0:
            hT = c.sbuf(es0, "cf_hT", [128, KC, H + T], BF16)
            gpre = P.load_small(es0, "cf_gpre", prm["g_mix_pre"][L], [128, KC], prm["g_mix_pre"])
            b1 = P.load_small(es0, "cf_b1", prm["c_b_pw1c"][j], [128, 32], prm["c_b_pw1c"])
            wdw = P.load_small(es0, "cf_wdw", prm["c_w_dwc"][j], [128, 16 * 31], prm["c_w_dwc"])
            bdw = P.load_small(es0, "cf_bdw", prm["c_b_dwc"][j], [128, 16], prm["c_b_dwc"])
            hv = P.load_small(es0, "cf_hv", prm["hv"][:, :], [128, 1], prm["hv"])
            prenorm_blocks(P, [(halo, None, 0, H, 0)] + [(xT_src, tb, tb * TB, TB, H + tb * TB) for tb in range(NTB)], gpre, hT)
            hb = hT.all()
            wa = [c.sbuf(es0, f"cf_wa{i}", [128, KC, 128], BF16) for i in range(2)]
            wg = [c.sbuf(es0, f"cf_wg{i}", [128, KC, 128], BF16) for i in range(2)]
            glb = [c.sbuf(es0, f"cf_gl{i}", [128, H + T], BF16) for i in range(2)]
            dkb = [c.sbuf(es0, f"cf_dk{i}", [128, 31, 128], BF16) for i in range(2)]
            cvb = [c.sbuf(es0, f"cf_cv{i}", [128, T], F32) for i in range(2)]
            sgt = [c.sbuf(es0, f"cf_sg{i}", [128, TB], F32) for i in range(2)]
            tmpa = c.sbuf(es0, "cf_ta", [128, TB], BF16)
            tmpb = c.sbuf(es0, "cf_tb", [128, TB], BF16)
            it = 0
            for m in range(16):
                wa_, wg_, gl, cvo = wa[m % 2], wg[m % 2], glb[m % 2], cvb[m % 2]
                c.dma("pool", wa_[:], W1[j, :, m * 128:(m + 1) * 128].rearrange("(kc p) n -> p kc n", p=128), reads=[W1.b()], writes=[wa_.b()])
                c.dma("pool", wg_[:], W1[j, :, 2048 + m * 128:2048 + (m + 1) * 128].rearrange("(kc p) n -> p kc n", p=128), reads=[W1.b()], writes=[wg_.b()])
                for (c0, n) in std_blocks(H):
                    pa, pg, sg = P.ps[0], P.ps[1], sgt[it % 2]
                    it += 1
                    c.mm(pa[:, 0:n], [(wa_[:, kc, :], hT[:, kc, c0:c0 + n]) for kc in range(KC)], pa.b(), [wa_.b()] + hb)
                    c.mm(pg[:, 0:n], [(wg_[:, kc, :], hT[:, kc, c0:c0 + n]) for kc in range(KC)], pg.b(), [wg_.b()] + hb)
                    c.op("act", lambda e, sg=sg, pg=pg, n=n, m=m: e.activation(out=sg[:, 0:n], in_=pg[:, 0:n], func=AF.Sigmoid, bias=b1[:, 16 + m:17 + m], scale=1.0),
                         reads=[pg.b(), b1.b()], writes=[sg.b()])
                    c.op("dve", lambda e, sg=sg, pa=pa, n=n, m=m, gl=gl, c0=c0: e.scalar_tensor_tensor(
                        out=gl[:, c0:c0 + n], in0=pa[:, 0:n], scalar=b1[:, m:m + 1], in1=sg[:, 0:n], op0=ALU.add, op1=ALU.mult),
                        reads=[pa.b(), sg.b(), b1.b()], writes=[gl.b()])
                c.op("dve", lambda e, gl=gl: e.tensor_scalar(gl[:, 0:H], gl[:, 0:H], hv[:, 0:1], None, ALU.mult), reads=[gl.b(), hv.b()], writes=[gl.b()])
                dk = dkb[m % 2]
                for k in range(31):
                    c.op("dve", lambda e, dk=dk, k=k, m=m: e.tensor_scalar(dk[:, k, :], P.identf[:], wdw[:, m * 31 + k:m * 31 + k + 1], None, ALU.mult),
                         reads=[P.identf.b(), wdw.b()], writes=[dk.b()])
                for tb in range(NTB):
                    pcv = P.ps[2 + tb % 2]
                    c.mm(pcv[:], [(dk[:, k, :], gl[:, 2 + k + tb * TB:2 + k + (tb + 1) * TB]) for k in range(31)], pcv.b(), [dk.b(), gl.b()])
                    c.op("act", lambda e, pcv=pcv, cvo=cvo, tb=tb, m=m: e.activation(out=cvo[:, tb * TB:(tb + 1) * TB], in_=pcv[:], func=AF.Identity,
                                                                                 bias=bdw[:, m:m + 1], scale=1.0),
                         reads=[pcv.b(), bdw.b()], writes=[cvo.b()])
                c.dma("sp", cvd[m * 128:(m + 1) * 128, :], cvo[:], reads=[cvo.b()], writes=[cvd.b(m)])
                for tb in range(NTB):
                    stat_accum(P, cvo[:, tb * TB:(tb + 1) * TB], cvo.b(), tb, s1, s2, m == 0, (tmpa, tmpb))
        c.barrier()
        zT = c.sbuf(es1, "cf_zT", [128, KC, T], BF16)
        gpost = P.load_small(es1, "cf_gpost", prm["g_mix_post"][L], [128, KC], prm["g_mix_post"])
        b2 = P.load_small(es1, "cf_b2", prm["c_b_pw2c"][j], [128, 16], prm["c_b_pw2c"])
        with ExitStack() as es2:
            lg = P.load_small(es2, "cf_lg", prm["c_ln_gc"][j], [128, 16], prm["c_ln_gc"])
            lb = P.load_small(es2, "cf_lb", prm["c_ln_bc"][j], [128, 16], prm["c_ln_bc"])
            cx = c.sbuf(es2, "cf_cx", [128, KC, TB], F32)
            mean = c.sbuf(es2, "cf_mean", [128, TB], F32)
            rstd = c.sbuf(es2, "cf_rstd", [128, TB], F32)
            tmp = c.sbuf(es2, "cf_tmp", [128, TB], F32)
            for tb in range(NTB):
                ln_stats(P, s1, s2, tb, D, mean, rstd, tmp)
                c.dma("sp", cx[:], cvd[:, tb * TB:(tb + 1) * TB].rearrange("(m p) t -> p m t", p=128), reads=cvd.all(), writes=[cx.b()])
                for m in range(KC):
                    c.op("pool", lambda e, m=m: e.tensor_tensor(out=cx[:, m, :], in0=cx[:, m, :], in1=mean[:], op=ALU.subtract),
                         reads=[cx.b(), mean.b()], writes=[cx.b()])
                    c.op("dve", lambda e, m=m: e.tensor_tensor(out=cx[:, m, :], in0=cx[:, m, :], in1=rstd[:], op=ALU.mult),
                         reads=[cx.b(), rstd.b()], writes=[cx.b()])
                    c.op("act", lambda e, m=m, tb=tb: e.activation(out=zT[:, m, tb * TB:(tb + 1) * TB], in_=cx[:, m, :], func=AF.Silu,
                                                                   bias=lb[:, m:m + 1], scale=lg[:, m:m + 1]),
                         reads=[cx.b(), lg.b(), lb.b()], writes=[zT.b()])
        c.barrier()

        def zt_get(tb):
            return (lambda kc: zT[:, kc, tb * TB:(tb + 1) * TB]), zT.all()
        final_proj(P, zt_get, KC, prm["c_w_pw2"][j], prm["c_w_pw2"], b2, gpost, xT_src, xT_dst)


def gmlp(P, L, j, xT_src, xT_dst, prm):
    c = P.c
    Win = prm["d_w_in"]
    ud = P.scratch(f"gm_ud{L}", [4096, T], BF16)
    vd = P.scratch(f"gm_vd{L}", [4096, T], F32)
    zd = P.scratch(f"gm_zd{L}", [4096, T], BF16)
    with ExitStack() as es1:
        s1 = c.sbuf(es1, "gm_s1", [128, T], F32)
        s2 = c.sbuf(es1, "gm_s2", [128, T], F32)
        with ExitStack() as es0:
            hT = c.sbuf(es0, "gm_hT", [128, KC, T], BF16)
            gpre = P.load_small(es0, "gm_gpre", prm["g_mix_pre"][L], [128, KC], prm["g_mix_pre"])
            bin_ = P.load_small(es0, "gm_bin", prm["d_b_inc"][j], [128, 64], prm["d_b_inc"])
            prenorm_blocks(P, [(xT_src, tb, tb * TB, TB, tb * TB) for tb in range(NTB)], gpre, hT)
            stb = [c.sbuf(es0, f"gm_stb{i}", [128, TB], BF16) for i in range(3)]
            stf = [c.sbuf(es0, f"gm_stf{i}", [128, TB], F32) for i in range(3)]
            tmpa = c.sbuf(es0, "gm_ta", [128, TB], BF16)
            tmpb = c.sbuf(es0, "gm_tb", [128, TB], BF16)
            cnt = [0]

            def epi_u(m, bi, pb, n):
                st = stb[cnt[0] % 3]
                cnt[0] += 1
                c.op("act", lambda e: e.activation(out=st[:], in_=pb[:, 0:n], func=AF.Gelu, bias=bin_[:, m:m + 1], scale=1.0),
                     reads=[pb.b(), bin_.b()], writes=[st.b()])
                c.dma("sp", ud[m * 128:(m + 1) * 128, bi * TB:(bi + 1) * TB], st[:], reads=[st.b()], writes=[ud.b()])
            proj_fm(P, hT, hT.all(), lambda c0, n: Win[j, :, c0:c0 + n], Win, 4096, std_blocks(0), epi_u)

            def epi_v(m, bi, pb, n):
                st = stf[cnt[0] % 3]
                cnt[0] += 1
                c.op("act", lambda e: e.activation(out=st[:], in_=pb[:, 0:n], func=AF.Gelu, bias=bin_[:, 32 + m:33 + m], scale=1.0),
                     reads=[pb.b(), bin_.b()], writes=[st.b()])
                c.dma("sp", vd[m * 128:(m + 1) * 128, bi * TB:(bi + 1) * TB], st[:], reads=[st.b()], writes=[vd.b()])
                stat_accum(P, st[:], st.b(), bi, s1, s2, m == 0, (tmpa, tmpb))
            proj_fm(P, hT, hT.all(), lambda c0, n: Win[j, :, 4096 + c0:4096 + c0 + n], Win, 4096, std_blocks(0), epi_v)
        c.barrier()
        with ExitStack() as es2:
            lg = P.load_small(es2, "gm_lg", prm["d_ln_gc"][j], [128, 32], prm["d_ln_gc"])
            lb = P.load_small(es2, "gm_lb", prm["d_ln_bc"][j], [128, 32], prm["d_ln_bc"])
            wsf = P.load_small(es2, "gm_wsf", prm["d_w_sT"][j], [128, 8 * 128], prm["d_w_sT"])
            tri = P.load_small(es2, "gm_tri", prm["triT"][:, :], [128, 128], prm["triT"])
            bsb = P.load_small(es2, "gm_bsb", prm["d_b_sbc"][j], [128, 8 * 128], prm["d_b_sbc"])
            wm = c.sbuf(es2, "gm_wm", [128, 8, 128], BF16)
            for g in range(8):
                c.op("dve", lambda e, g=g: e.tensor_tensor(out=wm[:, g, :], in0=wsf[:, g * 128:(g + 1) * 128], in1=tri[:], op=ALU.mult),
                     reads=[wsf.b(), tri.b()], writes=[wm.b()])
            vx = c.sbuf(es2, "gm_vx", [128, 8, TB], F32)
            vln = c.sbuf(es2, "gm_vln", [128, 32, TB], BF16)
            vtokb = [c.sbuf(es2, f"gm_vtok{i}", [128, 4096], BF16) for i in range(2)]
            uxb = [c.sbuf(es2, f"gm_ux{i}", [128, 32, 128], BF16) for i in range(2)]
            ztb = [c.sbuf(es2, f"gm_zt{i}", [128, 32, 128], BF16) for i in range(2)]
            mean = c.sbuf(es2, "gm_mean", [128, TB], F32)
            rstd = c.sbuf(es2, "gm_rstd", [128, TB], F32)
            tmp = c.sbuf(es2, "gm_tmp", [128, TB], F32)
            svt = [c.sbuf(es2, f"gm_sv{i}", [128, 512], F32) for i in range(2)]
            it = 0
            for tb in range(NTB):
                ln_stats(P, s1, s2, tb, 4096, mean, rstd, tmp)
                for qtr in range(4):
                    c.dma("sp", vx[:], vd[qtr * 1024:(qtr + 1) * 1024, tb * TB:(tb + 1) * TB].rearrange("(m p) t -> p m t", p=128),
                          reads=vd.all(), writes=[vx.b()])
                    for mm_ in range(8):
                        m = qtr * 8 + mm_
                        c.op("pool", lambda e, mm_=mm_: e.tensor_tensor(out=vx[:, mm_, :], in0=vx[:, mm_, :], in1=mean[:], op=ALU.subtract),
                             reads=[vx.b(), mean.b()], writes=[vx.b()])
                        c.op("dve", lambda e, mm_=mm_: e.tensor_tensor(out=vx[:, mm_, :], in0=vx[:, mm_, :], in1=rstd[:], op=ALU.mult),
                             reads=[vx.b(), rstd.b()], writes=[vx.b()])
                        c.op("act", lambda e, mm_=mm_, m=m: e.activation(out=vln[:, m, :], in_=vx[:, mm_, :], func=AF.Identity,
                                                                         bias=lb[:, m:m + 1], scale=lg[:, m:m + 1]),
                             reads=[vx.b(), lg.b(), lb.b()], writes=[vln.b()])
                for tt in range(4):
                    k2 = (tb * 4 + tt) % 2
                    vtok, ux, zt = vtokb[k2], uxb[k2], ztb[k2]
                    t0 = tb * TB + tt * 128
                    c.dma("sp", ux[:], ud[:, t0:t0 + 128].rearrange("(m p) t -> p m t", p=128), reads=ud.all(), writes=[ux.b()])
                    for q in range(4):
                        pb = P.psb[q % 2]
                        for r in range(8):
                            m = q * 8 + r
                            c.transpose(pb[:, r * 128:(r + 1) * 128], vln[:, m, tt * 128:(tt + 1) * 128], P.ident[:], pb.b(), [vln.b(), P.ident.b()])
                        c.op("act", lambda e, pb=pb, vtok=vtok, q=q: e.activation(out=vtok[:, q * 1024:(q + 1) * 1024], in_=pb[:], func=AF.Copy),
                             reads=[pb.b()], writes=[vtok.b()])
                    for g in range(8):
                        pb = P.ps[it % 4]
                        sv = svt[it % 2]
                        it += 1
                        for dc in range(4):
                            ch = g * 4 + dc
                            c.mm(pb[:, dc * 128:(dc + 1) * 128], [(vtok[:, ch * 128:(ch + 1) * 128], wm[:, g, :])], pb.b(), [vtok.b(), wm.b()])
                        for dc in range(4):
                            ch = g * 4 + dc
                            c.op("dve", lambda e, sv=sv, pb=pb, g=g, dc=dc: e.tensor_tensor(
                                out=sv[:, dc * 128:(dc + 1) * 128], in0=pb[:, dc * 128:(dc + 1) * 128], in1=bsb[:, g * 128:(g + 1) * 128], op=ALU.add),
                                reads=[pb.b(), bsb.b()], writes=[sv.b()])
                            c.op("pool", lambda e, sv=sv, ch=ch, dc=dc, zt=zt, ux=ux: e.tensor_tensor(
                                out=zt[:, ch, :], in0=sv[:, dc * 128:(dc + 1) * 128], in1=ux[:, ch, :], op=ALU.mult),
                                reads=[sv.b(), ux.b()], writes=[zt.b()])
                    c.dma("sp", zd[:, t0:t0 + 128].rearrange("(m p) t -> p m t", p=128), zt[:], reads=[zt.b()], writes=[zd.b((tb, tt))])
        c.barrier()
    with ExitStack() as es3:
        zt2 = c.sbuf(es3, "gm_zt2", [128, 32, TB], BF16)
        gpost = P.load_small(es3, "gm_gpost", prm["g_mix_post"][L], [128, KC], prm["g_mix_post"])

        def zt_get(tb):
            c.dma("sp", zt2[:], zd[:, tb * TB:(tb + 1) * TB].rearrange("(m p) t -> p m t", p=128), reads=zd.all(), writes=[zt2.b()])
            return (lambda kc: zt2[:, kc, :]), [zt2.b()]
        final_proj(P, zt_get, 32, prm["d_w_out"][j], prm["d_w_out"], None, gpost, xT_src, xT_dst)


GLA_STOP = [99]


class _Stop(Exception):
    pass


def _chk(c, n):
    if GLA_STOP[0] == n:
        c.barrier()
        raise _Stop()


def gla_a(P, L, j, xT_src, prm, o_loc, qtil, sr, Send, Lam):
    c = P.c
    Wq = prm["b_w_qkvr"]
    EcpD = P.scratch("gl_ecp", [1024, T], F32)
    EcmD = P.scratch("gl_ecm", [1024, T], F32)
    EendD = P.scratch("gl_eend", [T, 1024], F32)
    qdT = P.scratch("gl_qdT", [1024, T], BF16)
    kinvT = P.scratch("gl_kinvT", [1024, T], BF16)
    kendD = P.scratch("gl_kend", [T, 1024], BF16)
    vD = P.scratch("gl_v", [T, 2048], BF16)
    with ExitStack() as es1:
        lastT = c.sbuf(es1, "gl_last", [128, 256], F32)
        EL = c.sbuf(es1, "gl_EL", [128, 256], F32)
        PF = c.sbuf(es1, "gl_PF", [128, 256], F32)
        EP = c.sbuf(es1, "gl_EP", [128, 256], F32)
        with ExitStack() as es0:
            hT = c.sbuf(es0, "gl_hT", [128, KC, T], BF16)
            gpre = P.load_small(es0, "gl_gpre", prm["g_mix_pre"][L], [128, KC], prm["g_mix_pre"])
            prenorm_blocks(P, [(xT_src, tb, tb * TB, TB, tb * TB) for tb in range(NTB)], gpre, hT)
            hb = hT.all()
            g1T = c.sbuf(es0, "gl_g1T", [16, T], BF16)
            Wg1 = prm["b_w_gate1"]

            def epi_g1(m, bi, pb, n):
                c.op("act", lambda e: e.activation(out=g1T[0:16, bi * TB:(bi + 1) * TB], in_=pb[0:16, 0:n], func=AF.Copy),
                     reads=[pb.b()], writes=[g1T.b()])
            proj_fm(P, hT, hb, lambda c0, n: Wg1[j, :, c0:c0 + n], Wg1, 16, std_blocks(0), epi_g1, wname="gl_w1")
            if GLA_STOP[0] == 1:
                c.barrier()
                return
            with ExitStack() as es:
                wg2 = c.sbuf(es, "gl_wg2", [16, 1024], BF16)
                c.dma("pool", wg2[:], prm["b_w_gate2"][j], reads=[prm["b_w_gate2"].b()], writes=[wg2.b()])
                gb = P.load_small(es, "gl_gb", prm["b_gb_bc"][j], [128, 1024], prm["b_gb_bc"])
                tri2 = P.load_small(es, "gl_tri2", prm["tri2"][:, :], [128, 128], prm["tri2"])
                u2 = P.load_small(es, "gl_u2", prm["u2"][:, :], [128, 128], prm["u2"])
                lab = [c.sbuf(es, f"gl_la{i}", [128, 1024], F32) for i in range(2)]
                ecpb = [c.sbuf(es, f"gl_ecp{i}", [128, 1024], F32) for i in range(2)]
                ecmb = [c.sbuf(es, f"gl_ecm{i}", [128, 1024], F32) for i in range(2)]
                eeb = [c.sbuf(es, f"gl_ee{i}", [128, 1024], F32) for i in range(2)]
                cumsb = [c.sbuf(es, f"gl_cums{i}", [128, 512], F32) for i in range(2)]
                for ti in range(16):
                    la, ecp, ecm, ee = lab[ti % 2], ecpb[ti % 2], ecmb[ti % 2], eeb[ti % 2]
                    for b in range(2):
                        pk = P.ps[b]
                        c.mm(pk[:], [(g1T[0:16, ti * 128:(ti + 1) * 128], wg2[0:16, b * 512:(b + 1) * 512])], pk.b(), [g1T.b(), wg2.b()])
                        c.op("dve", lambda e, la=la, pk=pk, b=b: e.tensor_tensor(out=la[:, b * 512:(b + 1) * 512], in0=pk[:], in1=gb[:, b * 512:(b + 1) * 512], op=ALU.add),
                             reads=[pk.b(), gb.b()], writes=[la.b()])
                    c.op("act", lambda e, la=la: e.activation(out=la[:], in_=la[:], func=AF.Exp, scale=-1.0), reads=[la.b()], writes=[la.b()])
                    c.op("act", lambda e, la=la: e.activation(out=la[:], in_=la[:], func=AF.Ln, bias=1.0, scale=1.0), reads=[la.b()], writes=[la.b()])
                    _chk(c, 20)
                    c.op("dve", lambda e, la=la: e.tensor_scalar(la[:], la[:], -1.0 / 16.0, None, ALU.mult), reads=[la.b()], writes=[la.b()])
                    _chk(c, 21)
                    for half in range(2):
                        pc = P.ps[2 + half]
                        for q in range(4):
                            dc = half * 4 + q
                            c.mm(pc[:, q * 128:(q + 1) * 128], [(la[:, dc * 128:(dc + 1) * 128], tri2[:])], pc.b(), [la.b(), tri2.b()])
                        c.op("act", lambda e, ecp=ecp, pc=pc, half=half: e.activation(out=ecp[:, half * 512:(half + 1) * 512], in_=pc[:], func=AF.Exp),
                             reads=[pc.b()], writes=[ecp.b()])
                        c.op("act", lambda e, ecm=ecm, pc=pc, half=half: e.activation(out=ecm[:, half * 512:(half + 1) * 512], in_=pc[:], func=AF.Exp, scale=-1.0),
                             reads=[pc.b()], writes=[ecm.b()])
                        _chk(c, 22)
                        cums = cumsb[half]
                        c.op("act", lambda e, cums=cums, pc=pc: e.activation(out=cums[:], in_=pc[:], func=AF.Copy), reads=[pc.b()], writes=[cums.b()])
                        for q in range(4):
                            dc = half * 4 + q
                            for cc in range(2):
                                n = ti * 2 + cc
                                col = q * 128 + cc * 64 + 63
                                c.op("dve", lambda e, col=col, n=n, dc=dc, cums=cums: e.tensor_copy(lastT[:, n * 8 + dc:n * 8 + dc + 1], cums[:, col:col + 1]),
                                     reads=[cums.b()], writes=[lastT.b()])
                    _chk(c, 23)
                    for dc in range(8):
                        c.dma("sp", EcpD[dc * 128:(dc + 1) * 128, ti * 128:(ti + 1) * 128], ecp[:, dc * 128:(dc + 1) * 128], reads=[ecp.b()], writes=[EcpD.b()])
                        c.dma("sp", EcmD[dc * 128:(dc + 1) * 128, ti * 128:(ti + 1) * 128], ecm[:, dc * 128:(dc + 1) * 128], reads=[ecm.b()], writes=[EcmD.b()])
                    _chk(c, 24)
                    for b in range(2):
                        pe = P.ps[4 + b]
                        c.mm(pe[:], [(u2[:], la[:, b * 512:(b + 1) * 512])], pe.b(), [la.b(), u2.b()])
                        c.op("act", lambda e, ee=ee, pe=pe, b=b: e.activation(out=ee[:, b * 512:(b + 1) * 512], in_=pe[:], func=AF.Exp),
                             reads=[pe.b()], writes=[ee.b()])
                    c.dma("sp", EendD[ti * 128:(ti + 1) * 128, :], ee[:], reads=[ee.b()], writes=[EendD.b()])
            c.barrier()
            if GLA_STOP[0] == 2:
                return
            c.op("act", lambda e: e.activation(out=EL[:], in_=lastT[:], func=AF.Exp), reads=[lastT.b()], writes=[EL.b()])
            c.op("dve", lambda e: e.memset(PF[:, 0:8], 0.0), writes=[PF.b()])
            for n in range(1, 32):
                c.op("dve", lambda e, n=n: e.tensor_tensor(out=PF[:, n * 8:(n + 1) * 8], in0=PF[:, (n - 1) * 8:n * 8], in1=lastT[:, (n - 1) * 8:n * 8], op=ALU.add),
                     reads=[PF.b(), lastT.b()], writes=[PF.b()])
            c.op("act", lambda e: e.activation(out=EP[:], in_=PF[:], func=AF.Exp), reads=[PF.b()], writes=[EP.b()])
            lam = c.sbuf(es0, "gl_lam", [128, 8], F32)
            c.op("dve", lambda e: e.tensor_tensor(out=lam[:], in0=PF[:, 248:256], in1=lastT[:, 248:256], op=ALU.add),
                 reads=[PF.b(), lastT.b()], writes=[lam.b()])
            c.dma("sp", Lam[:, :], lam[:], reads=[lam.b()], writes=[Lam.b()])
            with ExitStack() as es:
                ecs = [c.sbuf(es, f"gl_ecs{i}", [128, TB], F32) for i in range(3)]
                qfb = [c.sbuf(es, f"gl_qf{i}", [128, TB], F32) for i in range(2)]
                stg = [c.sbuf(es, f"gl_stg{i}", [128, TB], BF16) for i in range(4)]
                cnt = [0]

                def epi_q(m, bi, pb, n):
                    k = cnt[0]
                    cnt[0] += 1
                    ec, qf, st, st2 = ecs[k % 3], qfb[k % 2], stg[(2 * k) % 4], stg[(2 * k + 1) % 4]
                    c.dma("sp", ec[:], EcpD[m * 128:(m + 1) * 128, bi * TB:(bi + 1) * TB], reads=EcpD.all(), writes=[ec.b()])
                    c.op("dve", lambda e: e.scalar_tensor_tensor(out=qf[:], in0=pb[:, 0:n], scalar=0.0625, in1=ec[:], op0=ALU.mult, op1=ALU.mult),
                         reads=[pb.b(), ec.b()], writes=[qf.b()])
                    c.op("act", lambda e: e.activation(out=st[:], in_=qf[:], func=AF.Copy), reads=[qf.b()], writes=[st.b()])
                    c.dma("sp", qdT[m * 128:(m + 1) * 128, bi * TB:(bi + 1) * TB], st[:], reads=[st.b()], writes=[qdT.b()])
                    for cc in range(8):
                        nn = bi * 8 + cc
                        c.op("dve", lambda e, cc=cc, nn=nn: e.tensor_scalar(st2[:, cc * 64:(cc + 1) * 64], qf[:, cc * 64:(cc + 1) * 64],
                                                                            EP[:, nn * 8 + m:nn * 8 + m + 1], None, ALU.mult),
                             reads=[qf.b(), EP.b()], writes=[st2.b()])
                    c.dma("sp", qtil[m * 128:(m + 1) * 128, bi * TB:(bi + 1) * TB], st2[:], reads=[st2.b()], writes=[qtil.b()])
                proj_fm(P, hT, hb, lambda c0, n: Wq[j, :, c0:c0 + n], Wq, 1024, std_blocks(0), epi_q, wname="gl_wq")

                def epi_k(m, bi, pb, n):
                    k = cnt[0]
                    cnt[0] += 1
                    ec, st = ecs[k % 3], stg[k % 4]
                    c.dma("sp", ec[:], EcmD[m * 128:(m + 1) * 128, bi * TB:(bi + 1) * TB], reads=EcmD.all(), writes=[ec.b()])
                    c.op("dve", lambda e: e.tensor_tensor(out=st[:], in0=pb[:, 0:n], in1=ec[:], op=ALU.mult), reads=[pb.b(), ec.b()], writes=[st.b()])
                    c.dma("sp", kinvT[m * 128:(m + 1) * 128, bi * TB:(bi + 1) * TB], st[:], reads=[st.b()], writes=[kinvT.b()])
                proj_fm(P, hT, hb, lambda c0, n: Wq[j, :, 1024 + c0:1024 + c0 + n], Wq, 1024, std_blocks(0), epi_k, wname="gl_wk")

                def epi_kt(nb, ti, pb, n):
                    k = cnt[0]
                    cnt[0] += 1
                    ec, st = ecs[k % 3], stg[k % 4]
                    c.dma("sp", ec[:], EendD[ti * 128:(ti + 1) * 128, nb * 512:(nb + 1) * 512], reads=EendD.all(), writes=[ec.b()])
                    c.op("dve", lambda e: e.tensor_tensor(out=st[:], in0=pb[:, 0:n], in1=ec[:], op=ALU.mult), reads=[pb.b(), ec.b()], writes=[st.b()])
                    c.dma("sp", kendD[ti * 128:(ti + 1) * 128, nb * 512:(nb + 1) * 512], st[:], reads=[st.b()], writes=[kendD.b()])
                tiles = [ti * 128 for ti in range(16)]
                proj_tm(P, hT, hb, lambda c0, n: Wq[j, :, 1024 + c0:1024 + c0 + n], Wq, 1024, tiles, epi_kt, wname="gl_wkt")

                def epi_v(nb, ti, pb, n):
                    k = cnt[0]
                    cnt[0] += 1
                    st = stg[k % 4]
                    c.op("act", lambda e: e.activation(out=st[:], in_=pb[:, 0:n], func=AF.Copy), reads=[pb.b()], writes=[st.b()])
                    c.dma("sp", vD[ti * 128:(ti + 1) * 128, nb * 512:(nb + 1) * 512], st[:], reads=[st.b()], writes=[vD.b()])
                proj_tm(P, hT, hb, lambda c0, n: Wq[j, :, 2048 + c0:2048 + c0 + n], Wq, 2048, tiles, epi_v, wname="gl_wv")

                def epi_r(nb, ti, pb, n):
                    k = cnt[0]
                    cnt[0] += 1
                    st = stg[k % 4]
                    c.op("act", lambda e: e.activation(out=st[:], in_=pb[:, 0:n], func=AF.Silu), reads=[pb.b()], writes=[st.b()])
                    c.dma("sp", sr[ti * 128:(ti + 1) * 128, nb * 512:(nb + 1) * 512], st[:], reads=[st.b()], writes=[sr.b()])
                proj_tm(P, hT, hb, lambda c0, n: Wq[j, :, 4096 + c0:4096 + c0 + n], Wq, 2048, tiles, epi_r, wname="gl_wr")
        c.barrier()
        if GLA_STOP[0] == 3:
            return
        with ExitStack() as es:
            qs = c.sbuf(es, "gl_qs", [128, 8, T], BF16)
            ks = c.sbuf(es, "gl_ks", [128, 8, T], BF16)
            c.dma("sp", qs[:], qdT[:, :].rearrange("(m p) t -> p m t", p=128), reads=qdT.all(), writes=[qs.b()])
            c.dma("sp", ks[:], kinvT[:, :].rearrange("(m p) t -> p m t", p=128), reads=kinvT.all(), writes=[ks.b()])
            S = c.sbuf(es, "gl_S", [128, 8, 512], F32)
            Sb = c.sbuf(es, "gl_Sb", [128, 8, 512], BF16)
            for hd in range(8):
                c.op("dve", lambda e, hd=hd: e.memset(S[:, hd, :], 0.0), writes=[S.b(hd)])
                c.op("pool", lambda e, hd=hd: e.memset(Sb[:, hd, :], 0.0), writes=[Sb.b(hd)])
            cm = P.load_small(es, "gl_cm", prm["cmaskT"][:, :], [64, 64], prm["cmaskT"])
            vcb = [c.sbuf(es, f"gl_vc{i}", [64, 2048], BF16) for i in range(2)]
            kcb = [c.sbuf(es, f"gl_kc{i}", [64, 1024], BF16) for i in range(2)]
            osbb = [c.sbuf(es, f"gl_os{i}", [64, 2048], F32) for i in range(2)]
            attb = [c.sbuf(es, f"gl_att{i}", [64, 64], BF16) for i in range(2)]
            it = 0
            for n in range(32):
                vc, kc_, osb = vcb[n % 2], kcb[n % 2], osbb[n % 2]
                cs = slice(n * 64, (n + 1) * 64)
                c.dma("sp", vc[:], vD[n * 64:(n + 1) * 64, :], reads=vD.all(), writes=[vc.b()])
                c.dma("sp", kc_[:], kendD[n * 64:(n + 1) * 64, :], reads=kendD.all(), writes=[kc_.b()])
                for h in range(4):
                    pa, po, at = P.ps[it % 2], P.ps[2 + it % 2], attb[it % 2]
                    it += 1
                    c.mm(pa[0:64, 0:64], [(ks[:, h * 2 + dc, cs], qs[:, h * 2 + dc, cs]) for dc in range(2)], pa.b(), [ks.b(), qs.b()])
                    c.op("dve", lambda e, at=at, pa=pa: e.tensor_tensor(out=at[:], in0=pa[0:64, 0:64], in1=cm[:], op=ALU.mult),
                         reads=[pa.b(), cm.b()], writes=[at.b()])
                    c.mm(po[0:64, 0:512], [(at[:], vc[:, h * 512:(h + 1) * 512]),
                                           (qs[:, h * 2, cs], Sb[:, h * 2, :]), (qs[:, h * 2 + 1, cs], Sb[:, h * 2 + 1, :])],
                         po.b(), [at.b(), vc.b(), qs.b(), Sb.b(h * 2), Sb.b(h * 2 + 1)])
                    c.op("act", lambda e, osb=osb, po=po, h=h: e.activation(out=osb[:, h * 512:(h + 1) * 512], in_=po[0:64, 0:512], func=AF.Copy),
                         reads=[po.b()], writes=[osb.b()])
                    for dc in range(2):
                        hd = h * 2 + dc
                        pk = P.ps[4 + dc]
                        c.mm(pk[:], [(kc_[:, hd * 128:(hd + 1) * 128], vc[:, h * 512:(h + 1) * 512])], pk.b(), [kc_.b(), vc.b()])
                        c.op("dve", lambda e, hd=hd, pk=pk, n=n: e.scalar_tensor_tensor(
                            out=S[:, hd, :], in0=S[:, hd, :], scalar=EL[:, n * 8 + hd:n * 8 + hd + 1], in1=pk[:], op0=ALU.mult, op1=ALU.add),
                            reads=[S.b(hd), pk.b(), EL.b()], writes=[S.b(hd)])
                        c.op("act", lambda e, hd=hd: e.activation(out=Sb[:, hd, :], in_=S[:, hd, :], func=AF.Copy),
                             reads=[S.b(hd)], writes=[Sb.b(hd)])
                c.dma("sp", o_loc[n * 64:(n + 1) * 64, :], osb[:], reads=[osb.b()], writes=[o_loc.b()])
            c.dma("sp", Send[:, :].rearrange("(m p) e -> p m e", p=128), S[:], reads=[S.b(hd) for hd in range(8)], writes=[Send.b()])
    c.barrier()


def gla_b(P, L, j, xT_src, xT_dst, prm, Send_all, Lam_all, cmask, o_loc, qtil, sr):
    c = P.c
    with ExitStack() as es1:
        zT = c.sbuf(es1, "gb_zT", [128, KC, T], BF16)
        gpost = P.load_small(es1, "gb_gpost", prm["g_mix_post"][L], [128, KC], prm["g_mix_post"])
        with ExitStack() as es:
            S = c.sbuf(es, "gb_S", [128, 8, 512], F32)
            Sb = c.sbuf(es, "gb_Sb", [128, 8, 512], BF16)
            for hd in range(8):
                c.op("dve", lambda e, hd=hd: e.memset(S[:, hd, :], 0.0), writes=[S.b(hd)])
            cmk = P.load_small(es, "gb_cm", cmask[:, :], [128, 8], cmask)
            onb = P.load_small(es, "gb_on", prm["b_on_bc"][j], [128, 512], prm["b_on_bc"])
            esA = ExitStack()
            Eb = [c.sbuf(esA, f"gb_E{i}", [128, 8, 512], F32) for i in range(2)]
            lam = c.sbuf(esA, "gb_lam", [128, 8], F32)
            Ap = c.sbuf(esA, "gb_Ap", [128, 8], F32)
            tmp = c.sbuf(esA, "gb_tmp", [128, 512], F32)
            for cp in range(7):
                E_ = Eb[cp % 2]
                c.dma("sp", E_[:], Send_all[cp * 1024:(cp + 1) * 1024, :].rearrange("(m p) e -> p m e", p=128), reads=Send_all.all() or [Send_all.b()], writes=[E_.b()])
                c.dma("sp", lam[:], Lam_all[cp * 128:(cp + 1) * 128, :], reads=Lam_all.all() or [Lam_all.b()], writes=[lam.b()])
                c.op("act", lambda e: e.activation(out=Ap[:], in_=lam[:], func=AF.Exp), reads=[lam.b()], writes=[Ap.b()])
                c.op("dve", lambda e, cp=cp: e.tensor_scalar(Ap[:], Ap[:], -1.0, cmk[:, cp:cp + 1], ALU.add, ALU.mult), reads=[Ap.b(), cmk.b()], writes=[Ap.b()])
                c.op("dve", lambda e: e.tensor_scalar(Ap[:], Ap[:], 1.0, None, ALU.add), reads=[Ap.b()], writes=[Ap.b()])
                for hd in range(8):
                    c.op("dve", lambda e, hd=hd, cp=cp, E_=E_: e.tensor_scalar(tmp[:], E_[:, hd, :], cmk[:, cp:cp + 1], None, ALU.mult),
                         reads=[E_.b(), cmk.b()], writes=[tmp.b()])
                    c.op("dve", lambda e, hd=hd: e.scalar_tensor_tensor(out=S[:, hd, :], in0=S[:, hd, :], scalar=Ap[:, hd:hd + 1], in1=tmp[:],
                                                                        op0=ALU.mult, op1=ALU.add),
                         reads=[S.b(hd), Ap.b(), tmp.b()], writes=[S.b(hd)])
            for hd in range(8):
                c.op("act", lambda e, hd=hd: e.activation(out=Sb[:, hd, :], in_=S[:, hd, :], func=AF.Copy), reads=[S.b(hd)], writes=[Sb.b(hd)])
            c.barrier()
            esA.close()
            qt = c.sbuf(es, "gb_qt", [128, 8, T], BF16)
            c.dma("sp", qt[:], qtil[:, :].rearrange("(m p) t -> p m t", p=128), reads=qtil.all() or [qtil.b()], writes=[qt.b()])
            olb = [c.sbuf(es, f"gb_ol{i}", [128, 2048], F32) for i in range(2)]
            srb = [c.sbuf(es, f"gb_sr{i}", [128, 2048], BF16) for i in range(2)]
            zbb = [c.sbuf(es, f"gb_zb{i}", [128, 2048], BF16) for i in range(2)]
            junk = c.sbuf(es, "gb_junk", [128, 512], F32)
            stt = [c.sbuf(es, f"gb_st{i}", [128, 8], F32) for i in range(2)]
            Sbb = [Sb.b(hd) for hd in range(8)]
            for ti in range(16):
                ol, srt, zb, st = olb[ti % 2], srb[ti % 2], zbb[ti % 2], stt[ti % 2]
                ts_ = slice(ti * 128, (ti + 1) * 128)
                c.dma("sp", ol[:], o_loc[ti * 128:(ti + 1) * 128, :], reads=o_loc.all() or [o_loc.b()], writes=[ol.b()])
                c.dma("sp", srt[:], sr[ti * 128:(ti + 1) * 128, :], reads=sr.all() or [sr.b()], writes=[srt.b()])
                for h in range(4):
                    pc = P.ps[h]
                    hs = slice(h * 512, (h + 1) * 512)
                    c.mm(pc[:], [(qt[:, h * 2 + dc, ts_], Sb[:, h * 2 + dc, :]) for dc in range(2)], pc.b(), [qt.b()] + Sbb)
                    c.op("dve", lambda e, ol=ol, pc=pc, hs=hs: e.tensor_tensor(out=ol[:, hs], in0=ol[:, hs], in1=pc[:], op=ALU.add),
                         reads=[ol.b(), pc.b()], writes=[ol.b()])
                    c.op("act", lambda e, ol=ol, hs=hs, st=st, h=h: e.activation(out=junk[:], in_=ol[:, hs], func=AF.Square, accum_out=st[:, h:h + 1]),
                         reads=[ol.b()], writes=[junk.b(), st.b()])
                c.op("dve", lambda e, st=st: e.tensor_scalar(st[:, 4:8], st[:, 0:4], 1.0 / 512, EPS, ALU.mult, ALU.add), reads=[st.b()], writes=[st.b()])
                c.op("act", lambda e, st=st: e.activation(out=st[:, 4:8], in_=st[:, 4:8], func=AF.Sqrt), reads=[st.b()], writes=[st.b()])
                c.op("dve", lambda e, st=st: e.reciprocal(st[:, 4:8], st[:, 4:8]), reads=[st.b()], writes=[st.b()])
                for h in range(4):
                    hs = slice(h * 512, (h + 1) * 512)
                    c.op("dve", lambda e, ol=ol, hs=hs, st=st, h=h: e.scalar_tensor_tensor(
                        out=ol[:, hs], in0=ol[:, hs], scalar=st[:, 4 + h:5 + h], in1=onb[:], op0=ALU.mult, op1=ALU.mult),
                        reads=[ol.b(), st.b(), onb.b()], writes=[ol.b()])
                c.op("pool", lambda e, ol=ol, srt=srt, zb=zb: e.tensor_tensor(out=zb[:], in0=ol[:], in1=srt[:], op=ALU.mult),
                     reads=[ol.b(), srt.b()], writes=[zb.b()])
                for q in range(2):
                    pb = P.psb[q]
                    for r in range(8):
                        kc = q * 8 + r
                        c.transpose(pb[:, r * 128:(r + 1) * 128], zb[:, kc * 128:(kc + 1) * 128], P.ident[:], pb.b(), [zb.b(), P.ident.b()])
                    for r in range(8):
                        kc = q * 8 + r
                        c.op("act", lambda e, pb=pb, r=r, kc=kc, ts_=ts_: e.activation(out=zT[:, kc, ts_], in_=pb[:, r * 128:(r + 1) * 128], func=AF.Copy),
                             reads=[pb.b()], writes=[zT.b()])
        c.barrier()

        def zt_get(tb):
            return (lambda kc: zT[:, kc, tb * TB:(tb + 1) * TB]), zT.all()
        final_proj(P, zt_get, KC, prm["b_w_o"][j], prm["b_w_o"], None, gpost, xT_src, xT_dst)


import ml_dtypes
from concourse.bass_utils import run_bass_kernel_spmd

NCORES = 8


def mix_x(P, mixer_fn, xT, xo, prm):
    if DBG["skip_mixer"]:
        xattn(P, 0, xT, xo, prm)
    elif DBG["skip_xattn"]:
        mixer_fn(xo)
    else:
        x1 = P.scratch("x1", [2048, T], F32)
        mixer_fn(x1)
        xattn(P, 0, x1, xo, prm)


def t5_bucket_np(dist):
    import math
    n = np.maximum(dist, 0)
    large = 16 + (np.log(np.maximum(n, 1) / 16) / math.log(128 / 16) * (32 - 16)).astype(np.int32)
    large = np.minimum(large, 31)
    return np.where(n < 16, n, large).astype(np.int32)


def bc128(v):
    v = np.asarray(v, np.float32).reshape(1, -1)
    return np.ascontiguousarray(np.broadcast_to(v, (128, v.shape[1])))


def declare(P, specs):
    return {k: P.inp(k, list(shape), dt) for k, (shape, dt) in specs.items()}


def run(P, maps):
    res = run_bass_kernel_spmd(P.nc, maps, core_ids=list(range(NCORES)))
    return res.results


def xattn_specs():
    return {"g_xattn_pre": ((1, 128, 16), F32), "g_xattn_post": ((1, 128, 16), F32), "g_mem": ((1, 128, 16), F32),
            "memT": ((2048, 256), F32), "x_w_q": ((1, 2048, 512), F32), "x_w_kv": ((1, 2048, 1024), F32),
            "x_w_o": ((1, 512, 2048), F32)}


def xattn_vals(inp, L):
    return {"g_xattn_pre": colfmt(inp["norm_xattn_pre"][L])[None], "g_xattn_post": colfmt(inp["norm_xattn_post"][L])[None],
            "g_mem": colfmt(inp["norm_mem"][L])[None], "memT": np.ascontiguousarray(inp["mem"][0].T),
            "x_w_q": inp["x_w_q"][L:L + 1], "x_w_kv": inp["x_w_kv"][L:L + 1], "x_w_o": inp["x_w_o"][L:L + 1]}


def mix_norm_specs():
    return {"g_mix_pre": ((1, 128, 16), F32), "g_mix_post": ((1, 128, 16), F32)}


def mix_norm_vals(inp, L):
    return {"g_mix_pre": colfmt(inp["norm_mix_pre"][L])[None], "g_mix_post": colfmt(inp["norm_mix_post"][L])[None]}


def exchange_halo(P, xsrc, H, selc):
    c = P.c
    bounce = P.scratch("hx_b", [2048, H], F32)
    gath = P.scratch("hx_g", [8 * 2048, H], F32)
    halo = P.scratch("hx_h", [2048, H], F32)
    c.dma("sp", bounce[:, :], xsrc[:, T - H:T], reads=xsrc.all(), writes=[bounce.b()])
    c.allgather(bounce, gath)
    with ExitStack() as es:
        G = c.sbuf(es, "hx_G", [128, 8, KC, H], F32)
        hs = c.sbuf(es, "hx_hs", [128, KC, H], F32)
        for r in range(8):
            c.dma("sp", G[:, r], gath[r * 2048:(r + 1) * 2048, :].rearrange("(m p) h -> p m h", p=128), reads=[gath.b()], writes=[G.b(r)])
        c.op("dve", lambda e: e.tensor_scalar(hs[:], G[:, 0], selc[:, 0:1], None, ALU.mult), reads=[G.b(0), selc.b()], writes=[hs.b()])
        for r in range(1, 8):
            c.op("dve", lambda e, r=r: e.scalar_tensor_tensor(out=hs[:], in0=G[:, r], scalar=selc[:, r:r + 1], in1=hs[:], op0=ALU.mult, op1=ALU.add),
                 reads=[G.b(r), selc.b(), hs.b()], writes=[hs.b()])
        c.dma("sp", halo[:, :].rearrange("(m p) h -> p m h", p=128), hs[:], reads=[hs.b()], writes=[halo.b()])
        c.barrier()
    return halo


def ffn_decl(inp, L):
    specs = {"f_w_gate_up": ((1, 2048, 16384), F32), "f_w_down": ((1, 8192, 2048), F32), "fcw": ((1, 128, 192), F32),
             "fcb": ((1, 128, 64), F32), "g_ffn_pre": ((1, 128, 16), F32), "g_ffn_post": ((1, 128, 16), F32)}
    fcw = np.stack([colfmt(inp["f_w_conv"][L, k]) for k in range(3)], axis=2).reshape(128, 192)[None]
    vals = {"f_w_gate_up": inp["f_w_gate_up"][L:L + 1], "f_w_down": inp["f_w_down"][L:L + 1],
            "fcw": np.ascontiguousarray(fcw), "fcb": colfmt(inp["f_b_conv"][L])[None],
            "g_ffn_pre": colfmt(inp["norm_ffn_pre"][L])[None], "g_ffn_post": colfmt(inp["norm_ffn_post"][L])[None]}
    return specs, vals


def swa_decl(inp, L, j):
    specs = dict(mix_norm_specs(), **xattn_specs())
    specs.update({"a_w_qkv": ((1, 2048, 2560), F32), "a_w_o": ((1, 2048, 2048), F32), "rel_bc": ((128, 1024), F32),
                  "sink_bc": ((1, 128, 32), F32), "maskadd": ((128, 256), F32), "Eoh": ((32, 128, 256), F32)})
    qi = np.arange(128)[:, None]
    kj = np.arange(256)[None, :]
    dist = qi + 128 - kj
    inw = (dist >= 0) & (dist < 128)
    bk = t5_bucket_np(dist)
    maskadd = np.where(inw, 0.0, -30000.0).astype(np.float32)
    Eoh = np.stack([((bk == b) & inw).astype(np.float32) for b in range(32)])
    vals = dict(mix_norm_vals(inp, L), **xattn_vals(inp, L))
    vals.update({"a_w_qkv": inp["a_w_qkv"][j:j + 1], "a_w_o": inp["a_w_o"][j:j + 1],
                 "rel_bc": bc128(inp["rel_bias_table"].reshape(-1)), "sink_bc": bc128(inp["a_sinks"][j])[None],
                 "maskadd": maskadd, "Eoh": Eoh})
    return specs, vals


def conf_decl(inp, L, j):
    specs = dict(mix_norm_specs(), **xattn_specs())
    specs.update({"c_w_pw1": ((1, 2048, 4096), F32), "c_b_pw1c": ((1, 128, 32), F32), "c_w_dwc": ((1, 128, 496), F32),
                  "c_b_dwc": ((1, 128, 16), F32), "c_ln_gc": ((1, 128, 16), F32), "c_ln_bc": ((1, 128, 16), F32),
                  "c_w_pw2": ((1, 2048, 2048), F32), "c_b_pw2c": ((1, 128, 16), F32)})
    wdw = np.ascontiguousarray(inp["c_w_dw"][j].reshape(31, 16, 128).transpose(2, 1, 0)).reshape(128, 496)
    vals = dict(mix_norm_vals(inp, L), **xattn_vals(inp, L))
    vals.update({"c_w_pw1": inp["c_w_pw1"][j:j + 1], "c_b_pw1c": colfmt(inp["c_b_pw1"][j])[None],
                 "c_w_dwc": wdw[None], "c_b_dwc": colfmt(inp["c_b_dw"][j])[None], "c_ln_gc": colfmt(inp["c_ln_g"][j])[None],
                 "c_ln_bc": colfmt(inp["c_ln_b"][j])[None], "c_w_pw2": inp["c_w_pw2"][j:j + 1], "c_b_pw2c": colfmt(inp["c_b_pw2"][j])[None]})
    return specs, vals


def gmlp_decl(inp, L, j):
    specs = dict(mix_norm_specs(), **xattn_specs())
    specs.update({"d_w_in": ((1, 2048, 8192), F32), "d_b_inc": ((1, 128, 64), F32), "d_ln_gc": ((1, 128, 32), F32),
                  "d_ln_bc": ((1, 128, 32), F32), "d_w_sT": ((1, 128, 1024), F32), "triT": ((128, 128), F32),
                  "d_b_sbc": ((1, 128, 1024), F32), "d_w_out": ((1, 4096, 2048), F32)})
    wsT = np.ascontiguousarray(inp["d_w_s"][j].transpose(2, 0, 1)).reshape(128, 1024)
    triT = (np.arange(128)[:, None] <= np.arange(128)[None, :]).astype(np.float32)
    vals = dict(mix_norm_vals(inp, L), **xattn_vals(inp, L))
    vals.update({"d_w_in": inp["d_w_in"][j:j + 1], "d_b_inc": colfmt(inp["d_b_in"][j])[None],
                 "d_ln_gc": colfmt(inp["d_ln_g"][j])[None], "d_ln_bc": colfmt(inp["d_ln_b"][j])[None], "d_w_sT": wsT[None],
                 "triT": triT, "d_b_sbc": bc128(inp["d_b_s"][j].reshape(-1))[None], "d_w_out": inp["d_w_out"][j:j + 1]})
    return specs, vals


def gla_decl(inp, L, j):
    specs = dict(mix_norm_specs(), **xattn_specs())
    specs.update({"b_w_qkvr": ((1, 2048, 6144), F32), "b_w_gate1": ((1, 2048, 16), F32), "b_w_gate2": ((1, 16, 1024), F32),
                  "b_gb_bc": ((1, 128, 1024), F32), "tri2": ((128, 128), F32), "u2": ((128, 128), F32), "cmaskT": ((64, 64), F32),
                  "b_on_bc": ((1, 128, 512), F32), "b_w_o": ((1, 2048, 2048), F32)})
    s_ = np.arange(128)[:, None]
    t_ = np.arange(128)[None, :]
    same = (s_ // 64) == (t_ // 64)
    tri2 = (same & (s_ <= t_)).astype(np.float32)
    u2 = (same & (s_ > t_)).astype(np.float32)
    cmT = (np.arange(64)[:, None] <= np.arange(64)[None, :]).astype(np.float32)
    vals = dict(mix_norm_vals(inp, L), **xattn_vals(inp, L))
    vals.update({"b_w_qkvr": inp["b_w_qkvr"][j:j + 1], "b_w_gate1": inp["b_w_gate1"][j:j + 1],
                 "b_w_gate2": inp["b_w_gate2"][j:j + 1], "b_gb_bc": bc128(inp["b_gate_bias"][j])[None], "tri2": tri2, "u2": u2, "cmaskT": cmT,
                 "b_on_bc": bc128(inp["b_o_norm"][j])[None], "b_w_o": inp["b_w_o"][j:j + 1]})
    return specs, vals


def build_fused(inp):
    P = Prog()
    c = P.c
    vals = {"ident_in": np.eye(128, dtype=np.float32)}

    def decl(sv, pre):
        specs, v = sv
        prm = {k: P.inp(pre + k, list(shape), dt) for k, (shape, dt) in specs.items()}
        for k in specs:
            vals[pre + k] = np.ascontiguousarray(v[k])
        return prm

    xT = P.inp("xT", [2048, T])
    halo_swa = P.inp("halo_swa", [2048, 128])
    hmask = P.inp("halo_mask", [128, 256])
    hv = P.inp("hv", [128, 1])
    cmask = P.inp("cmask", [128, 8])
    selc_d = P.inp("selc", [128, 8])
    xo = P.out("xo", [2048, T])
    selc = P.load_small(c.es, "selc_sb", selc_d[:, :], [128, 8], selc_d)
    cur = xT
    for L in range(4):
        kind, j = L % 4, L // 4
        nxt = P.scratch("xa", [2048, T], F32)
        if kind == 0:
            prm = decl(swa_decl(inp, L, j), f"L{L}_")
            prm["halo_mask"] = hmask
            mix_x(P, lambda dst: swa(P, 0, 0, cur, dst, halo_swa, prm), cur, nxt, prm)
        elif kind == 1:
            prm = decl(gla_decl(inp, L, j), f"L{L}_")
            o_loc = P.scratch("gl_oloc", [T, 2048], F32)
            qtil = P.scratch("gl_qtil", [1024, T], BF16)
            sr = P.scratch("gl_sr", [T, 2048], BF16)
            Send = P.scratch("gl_send", [1024, 512], F32)
            Lam = P.scratch("gl_lam", [128, 8], F32)
            Sall = P.scratch("gl_sall", [8 * 1024, 512], F32)
            Lall = P.scratch("gl_lall", [8 * 128, 8], F32)
            gla_a(P, 0, 0, cur, prm, o_loc, qtil, sr, Send, Lam)
            c.allgather(Send, Sall)
            c.allgather(Lam, Lall)
            c.barrier()
            mix_x(P, lambda dst: gla_b(P, 0, 0, cur, dst, prm, Sall, Lall, cmask, o_loc, qtil, sr), cur, nxt, prm)
        elif kind == 2:
            prm = decl(conf_decl(inp, L, j), f"L{L}_")
            prm["hv"] = hv
            h32 = exchange_halo(P, cur, 32, selc)
            mix_x(P, lambda dst: conformer(P, 0, 0, cur, dst, h32, prm), cur, nxt, prm)
        else:
            prm = decl(gmlp_decl(inp, L, j), f"L{L}_")
            mix_x(P, lambda dst: gmlp(P, 0, 0, cur, dst, prm), cur, nxt, prm)
        cur = nxt
        fprm = decl(ffn_decl(inp, L), f"F{L}_")
        h2 = exchange_halo(P, cur, 2, selc)
        nxt = xo if L == 3 else P.scratch("xb", [2048, T], F32)
        ffn(P, 0, cur, nxt, h2, fprm)
        cur = nxt
    c.finish()
    return P, vals


def kernel(**inputs):
    inp = {k: np.asarray(v) for k, v in inputs.items()}
    x = np.asarray(inp["x"], np.float32)[0]
    P, vals = build_fused(inp)
    maps = []
    for cix in range(NCORES):
        m = dict(vals)
        m["xT"] = np.ascontiguousarray(x[cix * T:(cix + 1) * T].T)
        m["halo_swa"] = np.ascontiguousarray(x[cix * T - 128:cix * T].T) if cix > 0 else np.zeros((2048, 128), np.float32)
        hm = np.zeros((128, 256), np.float32)
        if cix == 0:
            hm[:, 0:128] = -30000.0
        m["halo_mask"] = hm
        m["hv"] = np.full((128, 1), 0.0 if cix == 0 else 1.0, np.float32)
        cm = np.zeros((128, 8), np.float32)
        cm[:, :cix] = 1.0
        m["cmask"] = cm
        sc = np.zeros((128, 8), np.float32)
        if cix > 0:
            sc[:, cix - 1] = 1.0
        m["selc"] = sc
        maps.append(m)
    res = run(P, maps)
    out = np.concatenate([np.asarray(r["xo"], np.float32).T for r in res], axis=0)[None]
    return np.ascontiguousarray(out.astype(np.float32))
```
